# Optimizing a Trainium2 kernel written in Bass

```python
import math
import jax, jax.numpy as jnp
from jax import lax
import numpy as np

D_MODEL = 1024
BATCH = 8
SEQ = 2048
DEPTH = 4
DEC_BATCH = 128
DEC_SEQ = 4
PAST_LEN = 8192
PAGE_SIZE = 128

MIX_WIDTH = D_MODEL
A_WIDTH = MIX_WIDTH // 2
B_WIDTH = MIX_WIDTH - A_WIDTH
H_A = 4
DK_A = A_WIDTH // H_A
DV_A = A_WIDTH // H_A
CONV_WIDTH = 4
CONV_DIM = 2 * H_A * DK_A + H_A * DV_A
CHUNK = 64
HD_B = 64
H_QB = B_WIDTH // HD_B
H_KVB = 2
GQA_GROUP = H_QB // H_KVB
WINDOW = 128
D_FF = -(-8 * D_MODEL // (3 * 256)) * 256
ALPHA = (2 * DEPTH) ** 0.25
BETA_INIT = (8 * DEPTH) ** -0.25
EPS = 1e-6
PROJ_SIZES = (CONV_DIM, H_A * DV_A, H_A, H_A, H_QB * HD_B, H_KVB * HD_B, H_KVB * HD_B)
PROJ_SPLITS = tuple(int(s) for s in np.cumsum(PROJ_SIZES)[:-1])
PROJ_WIDTH = int(sum(PROJ_SIZES))

kernel_name = "hymba_gdn_swa_sink_deepnorm_step"


def layernorm(x, g, b):
    xf = x.astype(jnp.float32)
    mu = xf.mean(-1, keepdims=True)
    var = jnp.square(xf - mu).mean(-1, keepdims=True)
    return ((xf - mu) * lax.rsqrt(var + EPS) * g.astype(jnp.float32) + b.astype(jnp.float32)).astype(x.dtype)


def l2norm(t):
    return t * lax.rsqrt(jnp.sum(t * t, axis=-1, keepdims=True) + EPS)


def rmsnorm_gated(o, w, z):
    of = o.astype(jnp.float32)
    of = of * lax.rsqrt(jnp.mean(of * of, axis=-1, keepdims=True) + EPS) * w.astype(jnp.float32)
    return (of * jax.nn.silu(z.astype(jnp.float32))).astype(z.dtype)


def causal_conv(x, buf, w):
    L = x.shape[1]
    xc = jnp.concatenate([buf, x], axis=1)
    y = xc[:, 0:L] * w[0]
    for j in range(1, CONV_WIDTH):
        y = y + xc[:, j:j + L] * w[j]
    return jax.nn.silu(y), xc[:, -(CONV_WIDTH - 1):]


def gated_delta_rule(q, k, v, g, beta, S0):
    Bn, L, H, DK = q.shape
    DV = v.shape[-1]
    C = math.gcd(L, CHUNK)
    N = L // C

    def chunks(t):
        t = t.reshape((Bn, N, C, H) + t.shape[3:])
        return jnp.swapaxes(t, 2, 3)

    qc, kc, vc, gc_, bc = chunks(q), chunks(k), chunks(v), chunks(g), chunks(beta)
    gc = jnp.cumsum(gc_, axis=-1)
    ii = jnp.arange(C)[:, None]
    jj = jnp.arange(C)[None, :]
    causal = ii >= jj
    strict = ii > jj
    decay = jnp.exp(jnp.where(causal, gc[..., :, None] - gc[..., None, :], -jnp.inf))
    kb = kc * bc[..., None]
    A = jnp.where(strict, jnp.einsum('bnhid,bnhjd->bnhij', kb, kc) * decay, 0.0)
    M = A + jnp.eye(C, dtype=A.dtype)
    rhs = jnp.concatenate([vc * bc[..., None], kb * jnp.exp(gc)[..., None]], axis=-1)
    sol = lax.linalg.triangular_solve(M, rhs, left_side=True, lower=True, unit_diagonal=True)
    u, w = sol[..., :DV], sol[..., DV:]
    qk = jnp.einsum('bnhid,bnhjd->bnhij', qc, kc) * decay
    q_dec = qc * jnp.exp(gc)[..., None]
    k_dec = kc * jnp.exp(gc[..., -1:] - gc)[..., None]
    g_last = jnp.exp(gc[..., -1])

    def step(S, inp):
        u_n, w_n, qk_n, qd_n, kd_n, gl_n = inp
        v_new = u_n - jnp.einsum('bhcd,bhde->bhce', w_n, S)
        o_n = jnp.einsum('bhcd,bhde->bhce', qd_n, S) + jnp.einsum('bhij,bhje->bhie', qk_n, v_new)
        S = S * gl_n[..., None, None] + jnp.einsum('bhcd,bhce->bhde', kd_n, v_new)
        return S, o_n

    xs = tuple(jnp.moveaxis(t, 1, 0) for t in (u, w, qk, q_dec, k_dec, g_last))
    S_final, o = lax.scan(step, S0, xs)
    o = jnp.transpose(o, (1, 0, 3, 2, 4)).reshape(Bn, L, H, DV)
    return o, S_final


def swa_sinks(q, k, v, k_pre, v_pre, prefix_len, sinks):
    Bn, L = q.shape[:2]
    QB = math.gcd(L, WINDOW)
    N = L // QB
    KB = WINDOW + QB
    k_ext = jnp.concatenate([k_pre, k], axis=1)
    v_ext = jnp.concatenate([v_pre, v], axis=1)
    idx = jnp.arange(N)[:, None] * QB + jnp.arange(KB)[None, :]
    kbk = k_ext[:, idx]
    vbk = v_ext[:, idx]
    qb = q.reshape(Bn, N, QB, H_KVB, GQA_GROUP, HD_B)
    s = jnp.einsum('bnqhgd,bnkhd->bnhgqk', qb, kbk).astype(jnp.float32) * (HD_B ** -0.5)
    r = jnp.arange(QB)[:, None]
    c = jnp.arange(KB)[None, :]
    band = (c > r) & (c <= r + WINDOW)
    mask = band[None] & (idx >= WINDOW - prefix_len)[:, None, :]
    s = jnp.where(mask[None, :, None, None], s, -jnp.inf)
    sink = jnp.broadcast_to(sinks.astype(jnp.float32).reshape(1, 1, H_KVB, GQA_GROUP, 1, 1),
                            s.shape[:-1] + (1,))
    p = jax.nn.softmax(jnp.concatenate([s, sink], axis=-1), axis=-1)[..., :KB]
    o = jnp.einsum('bnhgqk,bnkhd->bnqhgd', p.astype(v.dtype), vbk).reshape(Bn, L, H_QB * HD_B)
    return o, k_ext[:, -WINDOW:], v_ext[:, -WINDOW:]


def token_mix(h, S0, conv_buf, k_pre, v_pre, prefix_len,
              w_in, conv_w, a_log, dt_bias, norm_a_w, sinks, w_out):
    Bn, L, _ = h.shape
    proj = jnp.einsum('bld,dp->blp', h, w_in)
    qkv_a, z_a, b_a, a_a, q_b, k_b, v_b = jnp.split(proj, PROJ_SPLITS, axis=-1)
    qkv_c, new_conv = causal_conv(qkv_a, conv_buf, conv_w)
    qc, kc, vc = jnp.split(qkv_c.astype(jnp.float32), (H_A * DK_A, 2 * H_A * DK_A), axis=-1)
    q = l2norm(qc.reshape(Bn, L, H_A, DK_A)) * (DK_A ** -0.5)
    k = l2norm(kc.reshape(Bn, L, H_A, DK_A))
    v = vc.reshape(Bn, L, H_A, DV_A)
    beta = jax.nn.sigmoid(b_a.astype(jnp.float32))
    g = -jnp.exp(a_log.astype(jnp.float32)) * jax.nn.softplus(a_a.astype(jnp.float32) + dt_bias.astype(jnp.float32))
    o_a, S_new = gated_delta_rule(q, k, v, g, beta, S0.astype(jnp.float32))
    o_a = rmsnorm_gated(o_a, norm_a_w, z_a.reshape(Bn, L, H_A, DV_A))
    o_b, k_new, v_new = swa_sinks(q_b.reshape(Bn, L, H_QB, HD_B), k_b.reshape(Bn, L, H_KVB, HD_B),
                                  v_b.reshape(Bn, L, H_KVB, HD_B), k_pre, v_pre, prefix_len, sinks)
    mixed = jnp.concatenate([o_a.reshape(Bn, L, A_WIDTH).astype(h.dtype), o_b.astype(h.dtype)], axis=-1)
    out = jnp.einsum('blm,md->bld', mixed, w_out)
    return out, S_new.astype(S0.dtype), new_conv, k_new, v_new


def swiglu(h, w_ffn_in, w_ffn_out):
    gate, up = jnp.split(jnp.einsum('bld,df->blf', h, w_ffn_in), 2, axis=-1)
    return jnp.einsum('blf,fd->bld', jax.nn.silu(gate) * up, w_ffn_out)


def setup_inputs(seed: int = 0) -> dict:
    key = jax.random.key(seed)
    ks = jax.random.split(key, 20)
    f32 = jnp.float32

    def nrm(k, shape, s):
        return jax.random.normal(k, shape, f32) * s

    swa_len = min(WINDOW, PAST_LEN)
    dt = jnp.exp(jax.random.uniform(ks[8], (DEPTH, H_A), f32, math.log(1e-3), math.log(1e-1)))
    return {
        "x_prompt": nrm(ks[0], (BATCH, SEQ, D_MODEL), 1.0),
        "x_sample": nrm(ks[1], (DEC_BATCH, DEC_SEQ, D_MODEL), 1.0),
        "state_delta": nrm(ks[2], (DEPTH, DEC_BATCH, H_A, DK_A, DV_A), 0.1),
        "state_conv": nrm(ks[3], (DEPTH, DEC_BATCH, CONV_WIDTH - 1, CONV_DIM), 1.0),
        "cache_swa_k": nrm(ks[4], (DEPTH, DEC_BATCH, swa_len, H_KVB, HD_B), 1.0),
        "cache_swa_v": nrm(ks[5], (DEPTH, DEC_BATCH, swa_len, H_KVB, HD_B), 1.0),
        "w_in": nrm(ks[6], (DEPTH, D_MODEL, PROJ_WIDTH), D_MODEL ** -0.5),
        "conv_w": nrm(ks[7], (DEPTH, CONV_WIDTH, CONV_DIM), CONV_WIDTH ** -0.5),
        "a_log": jnp.log(jax.random.uniform(ks[9], (DEPTH, H_A), f32, 1.0, 16.0)),
        "dt_bias": dt + jnp.log(-jnp.expm1(-dt)),
        "norm_a_w": 1.0 + nrm(ks[10], (DEPTH, DV_A), 0.02),
        "sinks": nrm(ks[11], (DEPTH, H_QB), 1.0),
        "w_out": nrm(ks[12], (DEPTH, MIX_WIDTH, D_MODEL), BETA_INIT * MIX_WIDTH ** -0.5),
        "ln1_g": 1.0 + nrm(ks[13], (DEPTH, D_MODEL), 0.02),
        "ln1_b": nrm(ks[14], (DEPTH, D_MODEL), 0.02),
        "w_ffn_in": nrm(ks[15], (DEPTH, D_MODEL, 2 * D_FF), D_MODEL ** -0.5),
        "w_ffn_out": nrm(ks[16], (DEPTH, D_FF, D_MODEL), BETA_INIT * D_FF ** -0.5),
        "ln2_g": 1.0 + nrm(ks[17], (DEPTH, D_MODEL), 0.02),
        "ln2_b": nrm(ks[18], (DEPTH, D_MODEL), 0.02),
    }


def reference(x_prompt, x_sample, state_delta, state_conv, cache_swa_k, cache_swa_v,
              w_in, conv_w, a_log, dt_bias, norm_a_w, sinks, w_out,
              ln1_g, ln1_b, w_ffn_in, w_ffn_out, ln2_g, ln2_b):
    xp, xs = x_prompt, x_sample
    Bp = xp.shape[0]
    dt_ = xp.dtype
    S_zero = jnp.zeros((Bp, H_A, DK_A, DV_A), state_delta.dtype)
    conv_zero = jnp.zeros((Bp, CONV_WIDTH - 1, CONV_DIM), dt_)
    kv_zero = jnp.zeros((Bp, WINDOW, H_KVB, HD_B), dt_)
    sample_prefix = cache_swa_k.shape[2]
    Sp, Cp, Kp, Vp, Ss, Cs, Ks, Vs = [], [], [], [], [], [], [], []
    for l in range(DEPTH):
        lw = (w_in[l], conv_w[l], a_log[l], dt_bias[l], norm_a_w[l], sinks[l], w_out[l])
        mp, s1, c1, k1, v1 = token_mix(xp, S_zero, conv_zero, kv_zero, kv_zero, 0, *lw)
        xp = layernorm(ALPHA * xp + mp, ln1_g[l], ln1_b[l])
        xp = layernorm(ALPHA * xp + swiglu(xp, w_ffn_in[l], w_ffn_out[l]), ln2_g[l], ln2_b[l])
        ms, s2, c2, k2, v2 = token_mix(xs, state_delta[l], state_conv[l], cache_swa_k[l], cache_swa_v[l],
                                       sample_prefix, *lw)
        xs = layernorm(ALPHA * xs + ms, ln1_g[l], ln1_b[l])
        xs = layernorm(ALPHA * xs + swiglu(xs, w_ffn_in[l], w_ffn_out[l]), ln2_g[l], ln2_b[l])
        Sp.append(s1); Cp.append(c1); Kp.append(k1); Vp.append(v1)
        Ss.append(s2); Cs.append(c2); Ks.append(k2); Vs.append(v2)
    new_delta_p = jnp.stack(Sp)
    new_conv_p = jnp.stack(Cp)
    new_swa_k_p = jnp.stack(Kp)
    new_swa_v_p = jnp.stack(Vp)
    new_delta_s = jnp.stack(Ss)
    new_conv_s = jnp.stack(Cs)
    new_swa_k_s = jnp.stack(Ks)
    new_swa_v_s = jnp.stack(Vs)
    return (xp, xs, new_delta_p, new_conv_p, new_swa_k_p, new_swa_v_p,
            new_delta_s, new_conv_s, new_swa_k_s, new_swa_v_s)
```

```python
import numpy as np
import concourse.bass as bass
import concourse.mybir as mybir
from concourse.bass_utils import run_bass_kernel_spmd

F32 = mybir.dt.float32
BF16 = mybir.dt.bfloat16
AF = mybir.ActivationFunctionType
ALU = mybir.AluOpType
AX = mybir.AxisListType

NCORES = 8
D = 1024
NP = 2048
NSQ = 16
NS = 64
T = NP + NS
L = 4
PW = 2824
FF = 2816
NFC = 22
ALPHA = float(8 ** 0.25)
EPS = 1e-6
C_Q, C_K, C_V, C_Z, C_B, C_A, C_QB, C_KB, C_VB = 0, 512, 1024, 1536, 2048, 2052, 2056, 2568, 2696
NEG = -30000.0
FLAGS = {}


class View:
    __slots__ = ("b", "ap")

    def __init__(self, b, ap):
        self.b = b
        self.ap = ap

    def __getitem__(self, idx):
        return View(self.b, self.ap[idx])

    def bc(self, shape):
        return View(self.b, self.ap.broadcast_to(list(shape)))

    def re(self, pat, **kw):
        return View(self.b, self.ap.rearrange(pat, **kw))

    def bitcast(self, dt):
        return View(self.b, self.ap.bitcast(dt))


class Buf:
    __slots__ = ("t", "w", "r", "name", "psum")

    def __init__(self, t, name="", psum=False):
        self.t = t
        self.w = None
        self.r = {}
        self.name = name
        self.psum = psum

    def __getitem__(self, idx):
        return View(self, self.t[idx])


class MultiBuf:
    def __init__(self, t, bounds, name=""):
        self.t = t
        self.bounds = list(bounds)
        self.doms = [Buf(t, "%s%d" % (name, i)) for i in range(len(bounds) - 1)]

    def __getitem__(self, idx):
        ap = self.t[idx]
        cs = idx[2] if isinstance(idx, tuple) and len(idx) > 2 else slice(None)
        start = cs.start or 0
        stop = cs.stop if cs.stop is not None else self.bounds[-1]
        doms = [d for d, a, b in zip(self.doms, self.bounds[:-1], self.bounds[1:]) if start < b and stop > a]
        return View(doms[0] if len(doms) == 1 else tuple(doms), ap)


class Eng:
    def __init__(self, name, obj, sem, self_ordered=False):
        self.name, self.obj, self.sem = name, obj, sem
        self.count = 0
        self.seen = {}
        self.self_ordered = self_ordered


class Sched:
    def __init__(self, nc, n_dma_sems=32):
        self.nc = nc
        self._scopes = [[]]
        self.engs = {}
        for name, obj, so in (("pe", nc.tensor, True), ("dve", nc.vector, False), ("act", nc.scalar, False),
                              ("pool", nc.gpsimd, False), ("sp", nc.sync, False)):
            self.engs[name] = Eng(name, obj, self._enter(nc.semaphore("s_" + name)), so)
        self.dma_slots = [[self._enter(nc.semaphore("d_%d" % i)), 0] for i in range(n_dma_sems)]
        self.q_slots = {"sp": self.dma_slots[:n_dma_sems - 12], "pool": self.dma_slots[n_dma_sems - 12:]}
        self.q_rr = {"sp": 0, "pool": 0}
        self.hist = {}
        self.nbuf = 0

    def _enter(self, cm):
        v = cm.__enter__()
        self._scopes[-1].append(cm)
        return v

    def push(self):
        self._scopes.append([])

    def pop(self):
        self.barrier()
        sc = self._scopes.pop()
        while sc:
            sc.pop().__exit__(None, None, None)

    def close(self):
        while self._scopes:
            sc = self._scopes.pop()
            while sc:
                sc.pop().__exit__(None, None, None)

    def sbuf(self, name, shape, dtype=F32):
        self.nbuf += 1
        return Buf(self._enter(self.nc.sbuf_tensor("%s_%d" % (name, self.nbuf), list(shape), dtype)), name)

    def psum(self, name, shape, dtype=F32):
        self.nbuf += 1
        return Buf(self._enter(self.nc.psum_tensor("%s_%d" % (name, self.nbuf), list(shape), dtype)), name, psum=True)

    @staticmethod
    def _add(deps, tok):
        if tok is None:
            return
        k = id(tok[0])
        if k not in deps or deps[k][1] < tok[1]:
            deps[k] = tok

    def _collect(self, reads, writes):
        deps = {}
        for b in reads:
            self._add(deps, b.w)
            if b.psum:
                for tok in b.r.values():
                    self._add(deps, tok)
        for b in writes:
            self._add(deps, b.w)
            for tok in b.r.values():
                self._add(deps, tok)
        return deps

    def _waits(self, eng, deps):
        for k, (sem, val) in sorted(deps.items(), key=lambda kv: -kv[1][1]):
            if eng.self_ordered and sem is eng.sem:
                continue
            if eng.seen.get(k, 0) < val:
                eng.obj.wait_ge(sem, val)
                eng.seen[k] = val
                for k2, v2 in self.hist.get((k, val), {}).items():
                    if eng.seen.get(k2, 0) < v2:
                        eng.seen[k2] = v2

    def _mark(self, tok, reads, writes):
        k = id(tok[0])
        for b in reads:
            old = b.r.get(k)
            if old is None or old[1] < tok[1]:
                b.r[k] = tok
        for b in writes:
            b.w = tok
            b.r = {}

    @staticmethod
    def _bufs(vs):
        out = []
        for v in vs:
            if isinstance(v, View):
                for b in (v.b if isinstance(v.b, tuple) else (v.b,)):
                    if b not in out:
                        out.append(b)
        return out

    def op(self, engname, fn, reads, writes, signal=True):
        eng = self.engs[engname]
        rb, wb = self._bufs(reads), self._bufs(writes)
        self._waits(eng, self._collect(rb, wb))
        ins = fn(eng.obj)
        if signal:
            eng.count += 1
            ins.then_inc(eng.sem, 1)
            self.hist[(id(eng.sem), eng.count)] = dict(eng.seen)
            self._mark((eng.sem, eng.count), rb, wb)
        else:
            self._mark((eng.sem, eng.count + 1), rb, wb)

    def dma(self, q, out, in_, **kw):
        eng = self.engs[q]
        rb, wb = self._bufs([in_]), self._bufs([out])
        deps = self._collect(rb, wb)
        slots = self.q_slots[q]
        slot = slots[self.q_rr[q]]
        self.q_rr[q] = (self.q_rr[q] + 1) % len(slots)
        if slot[1] > 0:
            self._add(deps, (slot[0], slot[1]))
        self._waits(eng, deps)
        o = out.ap if isinstance(out, View) else out
        i = in_.ap if isinstance(in_, View) else in_
        ins = eng.obj.dma_start(out=o, in_=i, **kw)
        slot[1] += 16
        ins.then_inc(slot[0], 16)
        self.hist[(id(slot[0]), slot[1])] = dict(eng.seen)
        self._mark((slot[0], slot[1]), rb, wb)

    def barrier(self):
        toks = {}
        for e in self.engs.values():
            if e.count > 0:
                self._add(toks, (e.sem, e.count))
        for s in self.dma_slots:
            if s[1] > 0:
                self._add(toks, (s[0], s[1]))
        for e in self.engs.values():
            so, e.self_ordered = e.self_ordered, False
            self._waits(e, dict(toks))
            e.self_ordered = so

    def mm(self, o, l, r, start=True, stop=True):
        self.op("pe", lambda e: e.matmul(o.ap, lhsT=l.ap, rhs=r.ap, start=start, stop=stop), [l, r], [o],
                signal=bool(stop))

    def tr(self, o, i, ident):
        self.op("pe", lambda e: e.transpose(o.ap, i.ap, ident.ap), [i, ident], [o])

    def act(self, o, i, func, bias=None, scale=None, accum=None):
        kw = {}
        rd = [i]
        if bias is not None:
            kw["bias"] = bias.ap if isinstance(bias, View) else bias
            rd.append(bias)
        if scale is not None:
            kw["scale"] = scale.ap if isinstance(scale, View) else scale
            rd.append(scale)
        wr = [o]
        if accum is not None:
            kw["accum_out"] = accum.ap
            wr.append(accum)
        self.op("act", lambda e: e.activation(out=o.ap, in_=i.ap, func=func, **kw), rd, wr)

    def tt(self, eng, o, a, b, op):
        self.op(eng, lambda e: e.tensor_tensor(out=o.ap, in0=a.ap, in1=b.ap, op=op), [a, b], [o])

    def ts(self, eng, o, a, s1, op0, s2=None, op1=None):
        rd = [a, s1, s2]
        v1 = s1.ap if isinstance(s1, View) else s1
        v2 = s2.ap if isinstance(s2, View) else s2
        if op1 is None:
            self.op(eng, lambda e: e.tensor_scalar(out=o.ap, in0=a.ap, scalar1=v1, scalar2=None, op0=op0), rd, [o])
        else:
            self.op(eng, lambda e: e.tensor_scalar(out=o.ap, in0=a.ap, scalar1=v1, scalar2=v2, op0=op0, op1=op1),
                    rd, [o])

    def stt(self, eng, o, a, s, b, op0, op1):
        sv = s.ap if isinstance(s, View) else s
        self.op(eng, lambda e: e.scalar_tensor_tensor(out=o.ap, in0=a.ap, scalar=sv, in1=b.ap, op0=op0, op1=op1),
                [a, s, b], [o])

    def copy(self, eng, o, i):
        if eng == "act":
            self.op("act", lambda e: e.copy(out=o.ap, in_=i.ap), [i], [o])
        else:
            self.op(eng, lambda e: e.tensor_copy(out=o.ap, in_=i.ap), [i], [o])

    def memset(self, eng, o, val):
        self.op(eng, lambda e: e.memset(o.ap, val), [], [o])

    def red(self, eng, o, i, op):
        self.op(eng, lambda e: e.tensor_reduce(out=o.ap, in_=i.ap, axis=AX.X, op=op), [i], [o])


CONST_SLOTS = {}


def _build_consts():
    parts = []
    off = [0]

    def add(name, arr):
        a = np.zeros((128, arr.shape[1]), np.float32)
        a[:arr.shape[0]] = arr
        CONST_SLOTS[name] = (off[0], arr.shape[1])
        off[0] += arr.shape[1]
        parts.append(a)

    j = np.arange(128)[:, None]
    i = np.arange(128)[None, :]
    add("ident", (j == i).astype(np.float32))
    add("ones", np.ones((128, 128), np.float32))
    add("o1024", np.full((128, 128), 1.0 / 1024, np.float32))
    add("o128", np.full((128, 128), 1.0 / 128, np.float32))
    add("tri_p", (j <= i).astype(np.float32))
    add("neg_incl_p", np.where(i >= j, 0.0, NEG).astype(np.float32))
    add("neg_strict_p", np.where(i > j, 0.0, NEG).astype(np.float32))
    same = (j // 4 == i // 4) & (j < 64) & (i < 64)
    add("tri_s", (same & (j <= i)).astype(np.float32))
    add("neg_incl_s", np.where(same & (i >= j), 0.0, NEG).astype(np.float32))
    add("neg_strict_s", np.where(same & (i > j), 0.0, NEG).astype(np.float32))
    add("blk_s", same.astype(np.float32))
    s16 = np.arange(16)[None, :]
    add("sm", ((j // 4 == s16) & (j < 64)).astype(np.float32))
    smb = np.zeros((16, 64), np.float32)
    for s in range(16):
        smb[s, 4 * s:4 * s + 4] = 1.0
    global SMB_CONST
    SMB_CONST = np.broadcast_to(smb.reshape(1, 1024), (128, 1024)).copy()
    r = np.arange(128)[:, None]
    c = np.arange(256)[None, :]
    band = (c > r) & (c <= r + 128)
    add("swa_p", np.where(band, 0.0, NEG * 8).astype(np.float32))
    add("swa_p0", np.where(band & (c >= 128), 0.0, NEG * 8).astype(np.float32))
    ms = np.full((128, 192), NEG * 8, np.float32)
    for q in range(64):
        t = q % 4
        ms[q, t + 1:128] = 0.0
        for t2 in range(t + 1):
            ms[q, 128 + (q // 4) * 4 + t2] = 0.0
    add("swa_s", ms)
    return np.concatenate(parts, axis=1)


CONSTS = _build_consts()
NCON = CONSTS.shape[1]


def build(n_layers=L, do_a=True, do_b=True, dbg_n=0, dbg_hook=None, do_prompt=True, do_sample=True):
    nc = bass.Bass("TRN2", target_bir_lowering=False)

    def din(name, shape):
        return nc.dram_tensor(name, list(shape), F32, kind="ExternalInput").ap()

    def dout(name, shape):
        return nc.dram_tensor(name, list(shape), F32, kind="ExternalOutput").ap()

    xT = din("xT", [D, T])
    sd_in = din("sd", [L, NSQ, 4, 128, 128])
    scT_in = din("scT", [L, 128, 12, NSQ, 3])
    ckT_in = din("ckT", [L, 128, NSQ, 2, 128])
    cvD_in = din("cvD", [L, 128, NSQ, 128])
    ck_raw = din("ck_raw", [L, NSQ, 128, 128])
    cv_raw = din("cv_raw", [L, NSQ, 128, 128])
    w_in = din("w_in", [L, D, PW])
    w_out = din("w_out", [L, D, D])
    w_fi = din("w_ffn_in", [L, D, 2 * FF])
    w_fo = din("w_ffn_out", [L, FF, D])
    convw_in = din("conv_wT", [128, L, 12, 4])
    lnp_in = din("lnp", [128, L, 4, 8])
    gp_in = din("gp", [128, L, 3, 8])
    naw_in = din("naw", [128, L])
    consts_in = din("consts", [128, NCON])
    smb_in = din("smb", [128, 1024])

    yT = dout("yT", [D, T])
    nsd_p = dout("nsd_p", [L, 4, 128, 128])
    nsc_p = dout("nsc_p", [L, 3, 1536])
    nk_p = dout("nk_p", [L, 128, 128])
    nv_p = dout("nv_p", [L, 128, 128])
    nsd_s = dout("nsd_s", [L, NSQ, 4, 128, 128])
    nsc_s = dout("nsc_s", [L, NSQ, 3, 1536])
    nk_s = dout("nk_s", [L, NSQ, 128, 128])
    nv_s = dout("nv_s", [L, NSQ, 128, 128])
    dbg = dout("dbg", [128, dbg_n]) if dbg_n else None

    S = Sched(nc)
    X = MultiBuf(S.sbuf("X", [128, 8, T]).t, list(range(0, T + 1, 64)), "X")
    CON = S.sbuf("CON", [128, NCON])
    CONB = S.sbuf("CONB", [128, 256], BF16)
    CW = S.sbuf("CW", [128, L, 12, 4])
    LNP = S.sbuf("LNP", [128, L, 4, 8])
    GP = S.sbuf("GP", [128, L, 3, 8])
    NAW = S.sbuf("NAW", [128, L])
    PS = [S.psum("PS%d" % i, [128, 512]) for i in range(8)]
    ps_rr = [0]

    def nps():
        p = PS[ps_rr[0]]
        ps_rr[0] = (ps_rr[0] + 1) % 8
        return p

    def con(name):
        o, n = CONST_SLOTS[name]
        return CON[:, o:o + n]

    xT_r = xT.rearrange("(c p) t -> p c t", p=128)
    for c in range(8):
        S.dma("sp", X[:, c, :], xT_r[:, c, :])
    S.dma("sp", CON[:, :], consts_in)
    S.dma("sp", CW[:], convw_in)
    S.dma("sp", LNP[:], lnp_in)
    S.dma("sp", GP[:], gp_in)
    S.dma("sp", NAW[:], naw_in)
    S.copy("dve", CONB[:, 0:128], con("ident"))
    S.copy("dve", CONB[:, 128:256], con("ones"))
    IDB = CONB[:, 0:128]

    dbg_off = [0]

    def dump(view, n):
        if dbg is None:
            return
        p = view.ap.shape[0]
        S.dma("sp", dbg[0:p, dbg_off[0]:dbg_off[0] + n], view)
        dbg_off[0] += n

    def interleave(*gens):
        gens = list(gens)
        while gens:
            for g_ in list(gens):
                try:
                    next(g_)
                except StopIteration:
                    gens.remove(g_)

    def rsqrt(o, i, eps):
        S.act(o, i, AF.Ln, bias=eps)
        S.act(o, o, AF.Exp, scale=-0.5)

    def layernorm_g(cols, ncol, gi, l, DD, SQ, RS, sub_eng="dve", affine_eng="act"):
        c0 = cols
        xs = X[:, :, c0:c0 + ncol]
        S.red("dve", RS[:, 0:ncol], xs.re("p c t -> p t c"), ALU.add)
        mp = nps()
        S.mm(mp[:, 0:ncol], con("o1024"), RS[:, 0:ncol])
        S.copy("act", RS[:, 0:ncol], mp[:, 0:ncol])
        yield
        S.tt(sub_eng, DD[:, :, 0:ncol], xs, RS[:, None, 0:ncol].bc([128, 8, ncol]), ALU.subtract)
        S.act(SQ[:, :, 0:ncol], DD[:, :, 0:ncol], AF.Square)
        yield
        S.red("dve", RS[:, 0:ncol], SQ[:, :, 0:ncol].re("p c t -> p t c"), ALU.add)
        vp = nps()
        S.mm(vp[:, 0:ncol], con("o1024"), RS[:, 0:ncol])
        rsqrt(RS[:, 0:ncol], vp[:, 0:ncol], EPS)
        yield
        S.tt("dve", DD[:, :, 0:ncol], DD[:, :, 0:ncol], RS[:, None, 0:ncol].bc([128, 8, ncol]), ALU.mult)
        yield
        if affine_eng == "act":
            for c in range(8):
                S.act(X[:, c, c0:c0 + ncol], DD[:, c, 0:ncol], AF.Identity,
                      bias=LNP[:, l, gi + 1, c:c + 1], scale=LNP[:, l, gi, c:c + 1])
        else:
            S.tt(affine_eng, DD[:, :, 0:ncol], DD[:, :, 0:ncol], LNP[:, l, gi, :, None].bc([128, 8, ncol]), ALU.mult)
            S.tt(affine_eng, xs, DD[:, :, 0:ncol], LNP[:, l, gi + 1, :, None].bc([128, 8, ncol]), ALU.add)

    def layernorm(cols, ncol, gi, l, DD, SQ, RS, sub_eng="dve"):
        for _ in layernorm_g(cols, ncol, gi, l, DD, SQ, RS, sub_eng):
            pass


    H4 = "p (h t) -> p h t"

    def v4(ps, n=128):
        return ps[:, 0:4 * n].re(H4, h=4)

    def gates_and_decay(l, C, XBt, WIN, NEXPA, tri, blk_ones, G):
        gps = nps()
        for kc in range(8):
            S.mm(gps[0:C, 0:8], XBt[:, kc, 0:C], WIN[:, kc, C_B:C_B + 8], start=(kc == 0), stop=(kc == 7))
        S.act(G["LNB"][0:C, :], gps[0:C, 0:4], AF.Exp, scale=-1.0)
        S.tt("dve", G["G1"][0:C, :], gps[0:C, 4:8], GP[0:C, l, 0, 0:4], ALU.add)
        S.act(G["G1"][0:C, :], G["G1"][0:C, :], AF.Exp)
        S.act(G["LNB"][0:C, :], G["LNB"][0:C, :], AF.Ln, bias=1.0)
        S.act(G["G1"][0:C, :], G["G1"][0:C, :], AF.Ln, bias=1.0)
        S.act(G["BETA"][0:C, :], G["LNB"][0:C, :], AF.Exp, scale=-1.0)
        S.tt("dve", G["G"][0:C, :], G["G1"][0:C, :], NEXPA[0:C, :], ALU.mult)
        cps = nps()
        S.mm(cps[0:C, 0:4], tri[0:C, 0:C], G["G"][0:C, :])
        S.mm(cps[0:C, 4:8], blk_ones[0:C, 0:C], G["G"][0:C, :])
        S.copy("dve", G["GCC"][0:C, :], cps[0:C, 0:8])
        S.act(G["EGC"][0:C, :], G["GCC"][0:C, 0:4], AF.Exp)
        S.tt("dve", G["EDL"][0:C, :], G["GCC"][0:C, 4:8], G["GCC"][0:C, 0:4], ALU.subtract)
        S.act(G["EDL"][0:C, :], G["EDL"][0:C, :], AF.Exp)
        S.tt("dve", G["BEG"][0:C, :], G["BETA"][0:C, :], G["EGC"][0:C, :], ALU.mult)
        S.copy("dve", G["G2"][0:C, 0:4], G["GCC"][0:C, 0:4])
        S.tt("dve", G["G2"][0:C, 4:8], G["GCC"][0:C, 0:4], G["LNB"][0:C, :], ALU.subtract)
        tp = nps()
        S.tr(tp[0:4, 0:C], G["G2"][0:C, 0:4], con("ident")[0:C, 0:C])
        S.tr(tp[0:4, C:2 * C], G["G2"][0:C, 4:8], con("ident")[0:C, 0:C])
        S.copy("act", G["GT"][0:4, 0:2 * C], tp[0:4, 0:2 * C])
        S.tt("dve", G["BD"][0:4, 0:8 * C].re("p (a h i) -> p a h i", a=2, h=4),
             G["GT"][0:4, 0:2 * C].re("p (a i) -> p a i", a=2)[:, :, None, :].bc([4, 2, 4, C]),
             con("ident")[0:4, None, 0:4, None].bc([4, 2, 4, C]), ALU.mult)

    def decay_mats(C, G, neg_incl, neg_strict, DT, DTB, EB, TMP1, TMP2):
        W = 4 * C
        for a_, (neg, tmp, dst) in enumerate(((neg_incl, TMP1, DT), (neg_strict, TMP2, DTB))):
            bps = nps()
            S.mm(bps[:, 0:W], con("ones")[0:4, :], G["BD"][0:4, a_ * W:(a_ + 1) * W])
            if a_ == 0:
                S.act(EB[:, 0:W], bps[:, 0:W], AF.Exp)
            t4 = tmp[0:C, 0:W].re(H4, h=4)
            S.tt("dve", t4, bps[0:C, 0:W].re(H4, h=4), G["GCC"][0:C, 0:4, None].bc([C, 4, C]), ALU.subtract)
            S.tt("dve", t4, t4, neg[0:C, None, 0:C].bc([C, 4, C]), ALU.add)
            S.act(dst[0:C, 0:W], tmp[0:C, 0:W], AF.Exp)

    def tri_inverse(C, BMb, AMb, XAb, YAb, R0, R1, Rb0, Rb1, res):
        W = 4 * C
        idb = con("ident")[0:C, None, 0:C].bc([C, 4, C])
        S.tt("dve", R0[0:C, 0:W].re(H4, h=4), idb, BMb[0:C, 0:W].re(H4, h=4), ALU.subtract)
        S.tt("dve", Rb0[0:C, 0:W].re(H4, h=4), idb, BMb[0:C, 0:W].re(H4, h=4), ALU.subtract)
        Xc, Yc, Xn, Yn = BMb, AMb, XAb, YAb
        Rc, Rn, Rbc, Rbn = R0, R1, Rb0, Rb1
        k = 1
        while (1 << k) < C:
            last = (1 << (k + 1)) >= C
            yps = nps()
            for h in range(4):
                hc = slice(h * C, (h + 1) * C)
                S.mm(yps[0:C, hc], Xc[0:C, hc], Yc[0:C, hc])
            if not last:
                xps = nps()
                for h in range(4):
                    hc = slice(h * C, (h + 1) * C)
                    S.mm(xps[0:C, hc], Yc[0:C, hc], Xc[0:C, hc])
            S.copy("act", Yn[0:C, 0:W], yps[0:C, 0:W])
            if not last:
                S.copy("dve", Xn[0:C, 0:W], xps[0:C, 0:W])
            rps = nps()
            for h in range(4):
                hc = slice(h * C, (h + 1) * C)
                S.mm(rps[0:C, hc], Yn[0:C, hc], Rbc[0:C, hc])
            S.tt("dve", Rn[0:C, 0:W], rps[0:C, 0:W], Rc[0:C, 0:W], ALU.add)
            S.tt("dve", Rbn[0:C, 0:W], rps[0:C, 0:W], Rc[0:C, 0:W], ALU.add)
            Xc, Yc, Xn, Yn = Xn, Yn, Xc, Yc
            Rc, Rn, Rbc, Rbn = Rn, Rc, Rbn, Rbc
            k += 1
            yield
        res["R"] = (Rc, Rbc, Rn, Rbn)

    def alloc_mix_weights(l):
        S.push()
        WIN = S.sbuf("WIN", [128, 8, PW], BF16)
        WOUT = S.sbuf("WOUT", [128, 8, D], BF16)
        win_r = w_in[l].rearrange("(c p) n -> p c n", p=128)
        wout_r = w_out[l].rearrange("(c p) n -> p c n", p=128)
        for kc in range(8):
            S.dma("pool", WIN[:, kc, :], win_r[:, kc, :])
        for kc in range(8):
            S.dma("pool", WOUT[:, kc, :], wout_r[:, kc, :])
        return WIN, WOUT

    def phase_a(l, WIN, WOUT):
        S.push()
        NEXPA = S.sbuf("NEXPA", [128, 4])
        S.act(NEXPA[:, :], GP[:, l, 1, 0:4], AF.Exp)
        S.ts("dve", NEXPA[:, :], NEXPA[:, :], -1.0, ALU.mult)
        if do_prompt:
            prompt_part(l, WIN, WOUT, NEXPA)
        if do_sample:
            sample_part(l, WIN, WOUT, NEXPA)
        S.pop()
        S.pop()

    def prompt_part(l, WIN, WOUT, NEXPA):
        S.push()
        C = 128
        MASKB = S.sbuf("MASKB", [128, 2, 256], BF16)
        S.copy("dve", MASKB[:, 0, :], con("swa_p0"))
        S.copy("dve", MASKB[:, 1, :], con("swa_p"))
        XBts = [S.sbuf("XBt%d" % i, [128, 8, 128], BF16) for i in range(2)]
        CV = [S.sbuf("CV%d" % i, [128, 4, 131]) for i in range(3)]
        (Tq, Tk, Tv, ZS, EB, DT, DTB, R0, R1, OT, SST) = [S.sbuf("T%d" % i, [128, 512]) for i in range(11)]
        (TvB, QN, KN, KBG, KDEC, VB, QG, PT, BM, AM, XA, YA, Rb0, Rb1, NWT, VNEW, SSb) = \
            [S.sbuf("B%d" % i, [128, 512], BF16) for i in range(17)]
        QKV = [Tq, Tk, Tv]
        G = {}
        for nm, w in (("BETA", 4), ("G1", 4), ("G", 4), ("LNB", 4), ("GCC", 8), ("EGC", 4), ("EDL", 4), ("BEG", 4),
                      ("GL", 4), ("G2", 8)):
            G[nm] = S.sbuf(nm, [128, w])
        G["GT"] = S.sbuf("GT", [4, 256])
        G["BD"] = S.sbuf("BD", [4, 1024])
        QB = S.sbuf("QB", [128, 4, 128], BF16)
        KD = [S.sbuf("KD%d" % i, [128, 2, 128], BF16) for i in range(2)]
        VD = [S.sbuf("VD%d" % i, [128, 128], BF16) for i in range(2)]
        PUN = S.sbuf("PUN", [128, 4, 256], BF16)
        PTT = S.sbuf("PTT", [128, 4, 2, 128], BF16)
        ST = {}
        for nm in ("RM", "NEGM", "RSUM", "SK", "RINV"):
            ST[nm] = S.sbuf(nm, [128, 8])
        S.memset("dve", SST[:, :], 0.0)
        S.memset("dve", SSb[:, :], 0.0)
        for i in range(3):
            S.memset("pool", CV[i][:, :, 0:3], 0.0)
        for i in range(2):
            S.memset("pool", KD[i][:, :, :], 0.0)
            S.memset("pool", VD[i][:, :], 0.0)

        MIXA = S.sbuf("MIXA", [128, 4, 128], BF16)
        MIXB = S.sbuf("MIXB", [128, 4, 128], BF16)
        KVO = S.sbuf("KVO", [128, 256])

        for blk in range(NP // C):
            t0 = blk * C
            cur, prv = blk % 2, (blk + 1) % 2
            last_blk = blk == NP // C - 1
            XBt = XBts[blk % 2]
            if blk == 0:
                S.copy("act", XBt[:, :, :], X[:, :, t0:t0 + C])

            def proj_fm(dst, col0, M=128):
                for kc in range(8):
                    S.mm(dst, WIN[:, kc, col0:col0 + M], XBt[:, kc, :], start=(kc == 0), stop=(kc == 7))

            def chain_a():
                for grp in range(3):
                    ps = nps()
                    for h in range(4):
                        proj_fm(ps[:, h * 128:(h + 1) * 128], grp * 512 + h * 128)
                    S.copy("act", CV[grp][:, :, 3:131], v4(ps))
                CT = (EB, DT, DTB)
                cwb = lambda grp, j: CW[:, l, grp * 4:(grp + 1) * 4, j:j + 1].bc([128, 4, 128])
                for grp in range(3):
                    S.tt("dve", v4(QKV[grp]), CV[grp][:, :, 0:128], cwb(grp, 0), ALU.mult)
                for j in range(1, 4):
                    for grp in range(3):
                        S.tt("dve", v4(CT[grp]), CV[grp][:, :, j:j + 128], cwb(grp, j), ALU.mult)
                    for grp in range(3):
                        S.tt("dve", v4(QKV[grp]), v4(QKV[grp]), v4(CT[grp]), ALU.add)
                for grp in range(3):
                    S.copy("pool", CV[grp][:, :, 0:3], CV[grp][:, :, 128:131])
                zps = nps()
                for h in range(4):
                    proj_fm(zps[:, h * 128:(h + 1) * 128], C_Z + h * 128)
                for grp in range(3):
                    S.act(TvB[:, :] if grp == 2 else QKV[grp][:, :], QKV[grp][:, :], AF.Silu)
                S.act(ZS[:, :], zps[:, :], AF.Silu)
                yield
                SQ1, RSQ = R0, R1
                for (src, dst, scl) in ((QKV[0], QN, 128.0 ** -0.5), (QKV[1], KN, 1.0)):
                    S.act(SQ1[:, :], src[:, :], AF.Square)
                    sps = nps()
                    S.mm(sps[:, :], con("ones"), SQ1[:, :])
                    rsqrt(RSQ[:, :], sps[:, :], EPS)
                    S.stt("dve", dst[:, :], src[:, :], scl, RSQ[:, :], ALU.mult, ALU.mult)
                    yield
                gates_and_decay(l, C, XBt, WIN, NEXPA, con("tri_p"), con("ones"), G)
                S.act(G["GL"][:, :], G["GCC"][:, 4:8], AF.Exp)
                yield
                decay_mats(C, G, con("neg_incl_p"), con("neg_strict_p"), DT, DTB, EB, Tq, Tk)
                yield
                tp_ = nps()
                tpb = tp_[:, :].bitcast(BF16)
                for h in range(4):
                    S.tr(tpb[:, h * 128:(h + 1) * 128], KN[:, h * 128:(h + 1) * 128], IDB)
                for h in range(4):
                    S.tr(tpb[:, 512 + h * 128:512 + (h + 1) * 128], TvB[:, h * 128:(h + 1) * 128], IDB)
                S.tt("dve", v4(KBG), tpb[:, 0:512].re(H4, h=4), G["BEG"][:, :, None].bc([128, 4, 128]), ALU.mult)
                S.tt("dve", v4(KDEC), tpb[:, 0:512].re(H4, h=4), G["EDL"][:, :, None].bc([128, 4, 128]), ALU.mult)
                S.tt("dve", v4(VB), tpb[:, 512:1024].re(H4, h=4), G["BETA"][:, :, None].bc([128, 4, 128]), ALU.mult)
                S.tt("dve", QG[:, :], QN[:, :], EB[:, :], ALU.mult)
                yield
                kk = nps()
                kq = nps()
                for h in range(4):
                    hc = slice(h * 128, (h + 1) * 128)
                    S.mm(kk[:, hc], KN[:, hc], KN[:, hc])
                    S.mm(kq[:, hc], KN[:, hc], QN[:, hc])
                BMf, AMf = Tq, Tk
                S.tt("dve", BMf[:, :], kk[:, :], DTB[:, :], ALU.mult)
                S.tt("dve", BM[:, :], kk[:, :], DTB[:, :], ALU.mult)
                S.tt("dve", PT[:, :], kq[:, :], DT[:, :], ALU.mult)
                yield
                ap_ = nps()
                for h in range(4):
                    hc = slice(h * 128, (h + 1) * 128)
                    S.tr(ap_[:, hc], BMf[:, hc], con("ident"))
                S.copy("act", AMf[:, :], ap_[:, :])
                S.copy("act", AM[:, :], ap_[:, :])
                yield
                res = {}
                yield from tri_inverse(C, BM, AM, XA, YA, R0, R1, Rb0, Rb1, res)
                Rf, R, Rsp, Rbsp = res["R"]
                e_ps = nps()
                for h in range(4):
                    hc = slice(h * 128, (h + 1) * 128)
                    S.mm(e_ps[:, hc], AMf[:, hc], Rf[:, hc])
                S.tt("dve", v4(Rsp), con("ident")[:, None, :].bc([128, 4, 128]), v4(Rf), ALU.subtract)
                S.tt("dve", XA[:, :], Rsp[:, :], e_ps[:, :], ALU.subtract)
                tb_ = nps()
                tbb = tb_[:, :].bitcast(BF16)
                for h in range(4):
                    hc = slice(h * 128, (h + 1) * 128)
                    S.tr(tbb[:, hc], R[:, hc], IDB)
                S.copy("act", YA[:, :], tbb[:, 0:512])
                yield
                c_ps = nps()
                for h in range(4):
                    hc = slice(h * 128, (h + 1) * 128)
                    S.mm(c_ps[:, hc], YA[:, hc], XA[:, hc])
                S.tt("dve", Rsp[:, :], Rf[:, :], c_ps[:, :], ALU.add)
                S.tt("dve", Rbsp[:, :], Rf[:, :], c_ps[:, :], ALU.add)
                R = Rbsp
                yield
                wps = nps()
                for h in range(4):
                    hc = slice(h * 128, (h + 1) * 128)
                    S.mm(wps[:, hc], KBG[:, hc], R[:, hc])
                S.act(NWT[:, :], wps[:, :], AF.Copy, scale=-1.0)
                yield
                vps = nps()
                for h in range(4):
                    hc = slice(h * 128, (h + 1) * 128)
                    S.mm(vps[:, hc], R[:, hc], VB[:, hc], start=True, stop=False)
                    S.mm(vps[:, hc], NWT[:, hc], SSb[:, hc], start=False, stop=True)
                S.copy("dve", VNEW[:, :], vps[:, :])
                yield
                ops_ = nps()
                for h in range(4):
                    hc = slice(h * 128, (h + 1) * 128)
                    S.mm(ops_[:, hc], SSb[:, hc], QG[:, hc], start=True, stop=False)
                    S.mm(ops_[:, hc], VNEW[:, hc], PT[:, hc], start=False, stop=True)
                S.copy("act", OT[:, :], ops_[:, :])
                sps = nps()
                for h in range(4):
                    hc = slice(h * 128, (h + 1) * 128)
                    S.mm(sps[:, hc], KDEC[:, hc], VNEW[:, hc])
                for h in range(4):
                    hc = slice(h * 128, (h + 1) * 128)
                    S.stt("dve", SST[:, hc], SST[:, hc], G["GL"][:, h:h + 1], sps[:, hc], ALU.mult, ALU.add)
                S.copy("act", SSb[:, :], SST[:, :])
                yield
                SQ1, RSQ = DT, DTB
                S.act(SQ1[:, :], OT[:, :], AF.Square)
                sps = nps()
                S.mm(sps[:, :], con("o128"), SQ1[:, :])
                rsqrt(RSQ[:, :], sps[:, :], EPS)
                S.stt("dve", OT[:, :], OT[:, :], NAW[:, l:l + 1], RSQ[:, :], ALU.mult, ALU.mult)
                S.tt("dve", MIXA[:, :, :], v4(OT), v4(ZS), ALU.mult)

            def chain_b():
                ps = nps()
                for h in range(4):
                    proj_fm(ps[:, h * 128:(h + 1) * 128], C_QB + h * 128)
                S.copy("act", QB[:, :, :], v4(ps))
                yield
                ps = nps()
                for g in range(2):
                    for e in range(2):
                        proj_fm(ps[64 * e:64 * e + 64, g * 128:(g + 1) * 128], C_KB + 64 * g, M=64)
                S.copy("dve", KD[cur][:, :, :], ps[:, 0:256].re("p (g t) -> p g t", g=2))
                ps = nps()
                for kc in range(8):
                    S.mm(ps[:, 0:128], XBt[:, kc, :], WIN[:, kc, C_VB:C_VB + 128], start=(kc == 0), stop=(kc == 7))
                S.copy("dve", VD[cur][:, :], ps[:, 0:128])
                if last_blk:
                    S.copy("dve", KVO[:, 128:256], ps[:, 0:128])
                    S.dma("sp", nv_p[l], KVO[:, 128:256])
                    ps = nps()
                    for kc in range(8):
                        S.mm(ps[:, 0:128], XBt[:, kc, :], WIN[:, kc, C_KB:C_KB + 128], start=(kc == 0), stop=(kc == 7))
                    S.copy("act", KVO[:, 0:128], ps[:, 0:128])
                    S.dma("sp", nk_p[l], KVO[:, 0:128])
                yield
                mk = MASKB[:, 0 if blk == 0 else 1, :]
                for half in range(2):
                    for pbl in range(2):
                        pb = half * 2 + pbl
                        sp_ = nps()
                        for hh in range(2):
                            h = pb * 2 + hh
                            c, e, g = h // 2, h % 2, h // 4
                            pr = slice(64 * e, 64 * e + 64)
                            for kb, kd in ((0, KD[prv]), (1, KD[cur])):
                                cs = slice(hh * 256 + kb * 128, hh * 256 + kb * 128 + 128)
                                S.mm(sp_[:, cs], QB[pr, c, :], kd[pr, g, :], start=True, stop=False)
                                S.mm(sp_[:, cs], IDB, mk[:, kb * 128:(kb + 1) * 128], start=False, stop=True)
                        hs = slice(2 * pb, 2 * pb + 2)
                        S.red("dve", ST["RM"][:, hs], sp_[:, :].re("p (a k) -> p a k", a=2), ALU.max)
                        S.ts("dve", ST["NEGM"][:, hs], ST["RM"][:, hs], 0.125, ALU.mult)
                        S.tt("dve", ST["NEGM"][:, hs], ST["NEGM"][:, hs], GP[:, l, 2, hs], ALU.max)
                        S.ts("dve", ST["NEGM"][:, hs], ST["NEGM"][:, hs], -1.0, ALU.mult)
                        for hh in range(2):
                            h = pb * 2 + hh
                            S.act(PUN[:, pbl * 2 + hh, :], sp_[:, hh * 256:(hh + 1) * 256], AF.Exp,
                                  bias=ST["NEGM"][:, h:h + 1], scale=0.125, accum=ST["RSUM"][:, h:h + 1])
                        yield
                    h4 = slice(half * 4, half * 4 + 4)
                    S.tt("dve", ST["SK"][:, h4], GP[:, l, 2, h4], ST["NEGM"][:, h4], ALU.add)
                    S.act(ST["SK"][:, h4], ST["SK"][:, h4], AF.Exp)
                    S.tt("dve", ST["RINV"][:, h4], ST["RSUM"][:, h4], ST["SK"][:, h4], ALU.add)
                    rv = ST["RINV"][:, h4]
                    S.op("dve", lambda e: e.reciprocal(out=rv.ap, in_=rv.ap), [rv], [rv])
                    S.tt("dve", PUN[:, :, :], PUN[:, :, :], rv[:, :, None].bc([128, 4, 256]), ALU.mult)
                    yield
                    tp = nps()
                    tpb = tp[:, :].bitcast(BF16)
                    for hh in range(4):
                        for kb in range(2):
                            S.tr(tpb[:, (hh * 2 + kb) * 128:(hh * 2 + kb + 1) * 128], PUN[:, hh, kb * 128:(kb + 1) * 128], IDB)
                    S.copy("act", PTT[:, :, :, :], tpb.re("p (h k q) -> p h k q", h=4, k=2))
                    yield
                    op_ = nps()
                    for hh in range(4):
                        h = half * 4 + hh
                        c, e, g = h // 2, h % 2, h // 4
                        dst = op_[64 * e:64 * e + 64, (c % 2) * 128:(c % 2) * 128 + 128]
                        S.mm(dst, VD[prv][:, g * 64:(g + 1) * 64], PTT[:, hh, 0, :], start=True, stop=False)
                        S.mm(dst, VD[cur][:, g * 64:(g + 1) * 64], PTT[:, hh, 1, :], start=False, stop=True)
                    S.copy("act", MIXB[:, 2 * half:2 + 2 * half, :], op_[:, 0:256].re("p (c t) -> p c t", c=2))
                    yield

            ga, gb = chain_a(), chain_b()
            next(ga)
            if not last_blk:
                S.copy("act", XBts[(blk + 1) % 2][:, :, :], X[:, :, t0 + C:t0 + 2 * C])
            for _ in range(int(FLAGS.get("b_lead", 2))):
                next(gb)
            interleave(ga, gb)

            for hb in range(2):
                mps = nps()
                for dcl in range(4):
                    dc = hb * 4 + dcl
                    for kc in range(8):
                        S.mm(mps[:, dcl * 128:(dcl + 1) * 128], WOUT[:, kc, dc * 128:(dc + 1) * 128],
                             (MIXA if kc < 4 else MIXB)[:, kc % 4, :], start=(kc == 0), stop=(kc == 7))
                xv = X[:, hb * 4:(hb + 1) * 4, t0:t0 + C]
                S.stt("dve", xv, xv, ALPHA, v4(mps), ALU.mult, ALU.add)
            interleave(
                layernorm_g(t0, 64, 0, l, Tq[:, :].re("p (c t) -> p c t", c=8), Tk[:, :].re("p (c t) -> p c t", c=8),
                            Tv[:, 0:64]),
                layernorm_g(t0 + 64, 64, 0, l, R0[:, :].re("p (c t) -> p c t", c=8), R1[:, :].re("p (c t) -> p c t", c=8),
                            OT[:, 0:64]))
            if last_blk:
                for cg in range(3):
                    ps = nps()
                    for kc in range(8):
                        S.mm(ps[:, :], XBt[:, kc, :], WIN[:, kc, cg * 512:(cg + 1) * 512], start=(kc == 0), stop=(kc == 7))
                    S.copy("act", QKV[cg][64:128, :], ps[64:128, :])
                    S.dma("sp", nsc_p[l][:, cg * 512:(cg + 1) * 512], QKV[cg][125:128, :])
            if dbg_hook is not None:
                dbg_hook(dict(locals(), dump=dump, S=S))
        S.dma("sp", nsd_p[l].rearrange("h d v -> d h v"), v4(SST))
        S.pop()

    def sample_part(l, WIN, WOUT, NEXPA):
        S.push()
        C = NS
        T0 = NP
        XBs = S.sbuf("XBs", [128, 8, C], BF16)
        MIX = S.sbuf("MIXs", [128, 8, C], BF16)
        QB = S.sbuf("QBs", [128, 4, C], BF16)
        KDn = S.sbuf("KDs", [128, 2, C], BF16)
        VDn = S.sbuf("VDs", [128, 128], BF16)
        S.memset("pool", VDn[:, :], 0.0)
        SMB = S.sbuf("SMB", [128, 16, C], BF16)
        S.dma("pool", SMB[:, :, :].re("p s t -> p (s t)"), smb_in)
        S.copy("act", XBs[:, :, :], X[:, :, T0:T0 + C])
        W4 = 4 * C

        def proj_fm(dst, col0, M=128):
            for kc in range(8):
                S.mm(dst, WIN[:, kc, col0:col0 + M], XBs[:, kc, :], start=(kc == 0), stop=(kc == 7))

        def proj_tm(dst, col0, n):
            for kc in range(8):
                S.mm(dst, XBs[:, kc, :], WIN[:, kc, col0:col0 + n], start=(kc == 0), stop=(kc == 7))

        S.push()
        R0 = S.sbuf("sR0", [128, W4])
        R1 = S.sbuf("sR1", [128, W4])
        Rb0 = S.sbuf("sRb0", [C, W4], BF16)
        Rb1 = S.sbuf("sRb1", [C, W4], BF16)
        PT = S.sbuf("sPT", [C, W4], BF16)
        QG = S.sbuf("sQG", [128, W4], BF16)
        ZS = S.sbuf("sZS", [128, W4])
        OT = S.sbuf("sOT", [128, W4])
        NWT = S.sbuf("sNWT", [128, W4], BF16)
        KBG = S.sbuf("sKBG", [C, 512], BF16)
        KDEC = S.sbuf("sKDEC", [C, 512], BF16)
        VB = S.sbuf("sVB", [C, 512], BF16)
        VNEW = S.sbuf("sVNEW", [C, 512], BF16)
        GLs = S.sbuf("sGL", [128, 64])
        SC1, SC2 = R0, R1
        G = {}
        for nm, w in (("BETA", 4), ("G1", 4), ("G", 4), ("LNB", 4), ("GCC", 8), ("EGC", 4), ("EDL", 4), ("BEG", 4),
                      ("G2", 8)):
            G[nm] = S.sbuf("s" + nm, [128, w])
        G["GT"] = S.sbuf("sGT", [4, 2 * C])
        G["BD"] = S.sbuf("sBD", [4, 8 * C])

        S.push()
        KC = S.sbuf("sKC", [128, 16, 2, 128], BF16)
        VC = S.sbuf("sVC", [128, 16, 128], BF16)
        QM = S.sbuf("sQM", [128, 4, 16, C], BF16)
        MSK = S.sbuf("sMSK", [128, 192], BF16)
        PUN = S.sbuf("sPUN", [C, 4, 192], BF16)
        PTC = S.sbuf("sPTC", [128, 4, C], BF16)
        PTN = S.sbuf("sPTN", [128, 4, C], BF16)
        S.memset("pool", PTN[:, :, :], 0.0)
        PTCM = S.sbuf("sPTCM", [128, 16, C], BF16)
        ST = {}
        for nm in ("RM", "NEGM", "RSUM", "SK", "RINV"):
            ST[nm] = S.sbuf("s" + nm, [C, 8])
        if FLAGS.get("no_kvc"):
            S.memset("pool", KC[:, :, :, :], 0.0)
            S.memset("pool", VC[:, :, :], 0.0)
        else:
            for sg in range(4):
                S.dma("pool", KC[:, sg * 4:(sg + 1) * 4, :, :].re("p s g k -> p (s g k)"),
                      ckT_in[l][:, sg * 4:(sg + 1) * 4, :, :].rearrange("p s g k -> p (s g k)"))
                S.dma("pool", VC[:, sg * 4:(sg + 1) * 4, :].re("p s k -> p (s k)"),
                      cvD_in[l][:, sg * 4:(sg + 1) * 4, :].rearrange("p s k -> p (s k)"))
        S.copy("dve", MSK[:, :], con("swa_s"))
        S.push()
        Tq, Tk, Tv, EB, DT, DTB = [S.sbuf("sT%d" % i, [128, W4]) for i in range(6)]
        TvB, QN, KN, BM, AM, XA, YA = [S.sbuf("sB%d" % i, [128, W4], BF16) for i in range(7)]
        QKV = [Tq, Tk, Tv]
        CVs = [S.sbuf("sCV%d" % i, [128, 4, 16, 7]) for i in range(3)]
        STG = S.sbuf("sSTG", [128, 12, 16, 3])
        OUTS = S.sbuf("sOUTS", [C, 1536])
        KVO = S.sbuf("sKVO", [C, 256])
        GM = S.sbuf("sGM", [C, 16, 4])
        res = {}

        def gen_ia():
            S.dma("sp", STG[:, :, :, :], scT_in[l])
            for grp in range(3):
                ps = nps()
                for h in range(4):
                    proj_fm(ps[:, h * C:(h + 1) * C], grp * 512 + h * 128)
                cv = CVs[grp]
                S.copy("dve", cv[:, :, :, 0:3], STG[:, grp * 4:(grp + 1) * 4, :, :])
                S.copy("act", cv[:, :, :, 3:7], ps[:, 0:W4].re("p (h s t) -> p h s t", h=4, s=16))
                for h in range(4):
                    ch = grp * 4 + h
                    tq = QKV[grp][:, h * C:(h + 1) * C].re("p (s t) -> p s t", s=16)
                    S.ts("dve", tq, cv[:, h, :, 0:4], CW[:, l, ch, 0:1], ALU.mult)
                    for j in range(1, 4):
                        S.stt("dve", tq, cv[:, h, :, j:j + 4], CW[:, l, ch, j:j + 1], tq, ALU.mult, ALU.add)
                S.act(TvB[:, :] if grp == 2 else QKV[grp][:, :], QKV[grp][:, :], AF.Silu)
                yield
            for cg in range(3):
                ps = nps()
                proj_tm(ps[0:C, :], cg * 512, 512)
                S.copy("act", OUTS[:, cg * 512:(cg + 1) * 512], ps[0:C, :])
            for r in range(1, 4):
                S.dma("sp", nsc_s[l][:, r - 1, :], OUTS[r:C:4, :])
            ps = nps()
            for h in range(4):
                proj_fm(ps[:, h * C:(h + 1) * C], C_Z + h * 128)
            S.act(ZS[:, :], ps[:, 0:W4], AF.Silu)
            yield
            SQ1, RSQ = DT, DTB
            for (src, dst, scl) in ((QKV[0], QN, 128.0 ** -0.5), (QKV[1], KN, 1.0)):
                S.act(SQ1[:, :], src[:, :], AF.Square)
                sps = nps()
                S.mm(sps[:, 0:W4], con("ones"), SQ1[:, :])
                rsqrt(RSQ[:, :], sps[:, 0:W4], EPS)
                S.stt("dve", dst[:, :], src[:, :], scl, RSQ[:, :], ALU.mult, ALU.mult)
                yield
            gates_and_decay(l, C, XBs, WIN, NEXPA, con("tri_s"), con("blk_s"), G)
            yield
            S.tt("dve", GM[:, :, :], G["G"][0:C, None, :].bc([C, 16, 4]), con("sm")[0:C, :, None].bc([C, 16, 4]), ALU.mult)
            gps_ = nps()
            S.mm(gps_[:, 0:64], con("ones")[0:C, :], GM[:, :, :].re("p s h -> p (s h)"))
            S.act(GLs[:, :], gps_[:, 0:64], AF.Exp)
            yield
            decay_mats(C, G, con("neg_incl_s"), con("neg_strict_s"), DT, DTB, EB, Tq, Tk)
            yield
            tp_ = nps()
            tpb = tp_[:, :].bitcast(BF16)
            for h in range(4):
                S.tr(tpb[0:C, h * 128:(h + 1) * 128], KN[:, h * C:(h + 1) * C], IDB)
            for h in range(4):
                S.tr(tpb[0:C, 512 + h * 128:512 + (h + 1) * 128], TvB[:, h * C:(h + 1) * C], IDB)
            S.tt("dve", v4(KBG), tpb[0:C, 0:512].re(H4, h=4), G["BEG"][0:C, :, None].bc([C, 4, 128]), ALU.mult)
            S.tt("dve", v4(KDEC), tpb[0:C, 0:512].re(H4, h=4), G["EDL"][0:C, :, None].bc([C, 4, 128]), ALU.mult)
            S.tt("dve", v4(VB), tpb[0:C, 512:1024].re(H4, h=4), G["BETA"][0:C, :, None].bc([C, 4, 128]), ALU.mult)
            S.tt("dve", QG[:, :], QN[:, :], EB[:, :], ALU.mult)
            yield
            kk = nps()
            kq = nps()
            for h in range(4):
                hc = slice(h * C, (h + 1) * C)
                S.mm(kk[0:C, hc], KN[:, hc], KN[:, hc])
                S.mm(kq[0:C, hc], KN[:, hc], QN[:, hc])
            S.tt("dve", BM[0:C, :], kk[0:C, 0:W4], DTB[0:C, :], ALU.mult)
            S.tt("dve", PT[:, :], kq[0:C, 0:W4], DT[0:C, :], ALU.mult)
            yield
            ap_ = nps()
            apb = ap_[:, :].bitcast(BF16)
            for h in range(4):
                hc = slice(h * C, (h + 1) * C)
                S.tr(apb[0:C, hc], BM[0:C, hc], IDB[0:C, 0:C])
            S.copy("act", AM[0:C, :], apb[0:C, 0:W4])
            yield
            yield from tri_inverse(C, BM, AM, XA, YA, R0, R1, Rb0, Rb1, res)
            R = res["R"][1]
            wps = nps()
            for h in range(4):
                S.mm(wps[:, h * C:(h + 1) * C], KBG[:, h * 128:(h + 1) * 128], R[:, h * C:(h + 1) * C])
            S.act(NWT[:, :], wps[:, 0:W4], AF.Copy, scale=-1.0)

        def gen_ii():
            ps = nps()
            for h in range(4):
                proj_fm(ps[:, h * C:(h + 1) * C], C_QB + h * 128)
            S.copy("act", QB[:, :, :], ps[:, 0:W4].re("p (h t) -> p h t", h=4))
            ps = nps()
            for g in range(2):
                for e in range(2):
                    proj_fm(ps[64 * e:64 * e + 64, g * C:(g + 1) * C], C_KB + 64 * g, M=64)
            S.copy("dve", KDn[:, :, :], ps[:, 0:2 * C].re("p (g t) -> p g t", g=2))
            yield
            ps = nps()
            proj_tm(ps[0:C, 0:128], C_KB, 128)
            proj_tm(ps[0:C, 128:256], C_VB, 128)
            S.copy("dve", VDn[0:C, :], ps[0:C, 128:256])
            S.copy("act", KVO[:, :], ps[0:C, 0:256])
            for t in range(4):
                S.dma("sp", nk_s[l][:, 124 + t, :], KVO[t:C:4, 0:128])
                S.dma("sp", nv_s[l][:, 124 + t, :], KVO[t:C:4, 128:256])
            if not FLAGS.get("no_d2d"):
                S.dma("sp", nk_s[l][:, 0:124, :], ck_raw[l][:, 4:128, :])
                S.dma("sp", nv_s[l][:, 0:124, :], cv_raw[l][:, 4:128, :])
            yield
            for c in range(4):
                S.tt("dve", QM[:, c, :, :], QB[:, c, None, :].bc([128, 16, C]), SMB[:, :, :], ALU.mult)
            yield
            LVL = int(FLAGS.get("swa_lvl", 9))
            for half in range(2):
                if LVL < 1:
                    continue
                for pbl in range(2):
                    pb = half * 2 + pbl
                    sp_ = nps()
                    for hh in range(2):
                        h = pb * 2 + hh
                        c, e, g = h // 2, h % 2, h // 4
                        pr = slice(64 * e, 64 * e + 64)
                        cs = slice(hh * 192, hh * 192 + 128)
                        for s_ in range(16):
                            S.mm(sp_[0:C, cs], QM[pr, c, s_, :], KC[pr, s_, g, :], start=(s_ == 0), stop=False)
                        S.mm(sp_[0:C, cs], IDB[:, 0:C], MSK[:, 0:128], start=False, stop=True)
                        cs2 = slice(hh * 192 + 128, hh * 192 + 192)
                        S.mm(sp_[0:C, cs2], QB[pr, c, :], KDn[pr, g, :], start=True, stop=False)
                        S.mm(sp_[0:C, cs2], IDB[:, 0:C], MSK[:, 128:192], start=False, stop=True)
                    hs = slice(2 * pb, 2 * pb + 2)
                    if LVL < 2:
                        continue
                    S.red("dve", ST["RM"][:, hs], sp_[0:C, 0:384].re("p (a k) -> p a k", a=2), ALU.max)
                    S.ts("dve", ST["NEGM"][:, hs], ST["RM"][:, hs], 0.125, ALU.mult)
                    S.tt("dve", ST["NEGM"][:, hs], ST["NEGM"][:, hs], GP[0:C, l, 2, hs], ALU.max)
                    S.ts("dve", ST["NEGM"][:, hs], ST["NEGM"][:, hs], -1.0, ALU.mult)
                    for hh in range(2):
                        h = pb * 2 + hh
                        S.act(PUN[:, pbl * 2 + hh, :], sp_[0:C, hh * 192:(hh + 1) * 192], AF.Exp,
                              bias=ST["NEGM"][:, h:h + 1], scale=0.125, accum=ST["RSUM"][:, h:h + 1])
                    yield
                if LVL < 2:
                    continue
                h4 = slice(half * 4, half * 4 + 4)
                S.tt("dve", ST["SK"][:, h4], GP[0:C, l, 2, h4], ST["NEGM"][:, h4], ALU.add)
                S.act(ST["SK"][:, h4], ST["SK"][:, h4], AF.Exp)
                S.tt("dve", ST["RINV"][:, h4], ST["RSUM"][:, h4], ST["SK"][:, h4], ALU.add)
                rv = ST["RINV"][:, h4]
                S.op("dve", lambda e: e.reciprocal(out=rv.ap, in_=rv.ap), [rv], [rv])
                S.tt("dve", PUN[:, :, :], PUN[:, :, :], rv[:, :, None].bc([C, 4, 192]), ALU.mult)
                yield
                if LVL < 3:
                    continue
                tp = nps()
                tpb = tp[:, :].bitcast(BF16)
                for hh in range(4):
                    S.tr(tpb[:, hh * C:(hh + 1) * C], PUN[:, hh, 0:128], IDB[0:C, 0:C])
                    S.tr(tpb[0:C, 256 + hh * C:256 + (hh + 1) * C], PUN[:, hh, 128:192], IDB[0:C, 0:C])
                S.copy("act", PTC[:, :, :], tpb[:, 0:256].re("p (h q) -> p h q", h=4))
                S.copy("dve", PTN[0:C, :, :], tpb[0:C, 256:512].re("p (h q) -> p h q", h=4))
                yield
                if LVL < 4:
                    continue
                op_ = nps()
                for hh in range(4):
                    h = half * 4 + hh
                    c, e, g = h // 2, h % 2, h // 4
                    pr = slice(64 * e, 64 * e + 64)
                    base = (c % 2) * C
                    S.tt("dve", PTCM[:, :, :], PTC[:, hh, None, :].bc([128, 16, C]), SMB[:, :, :], ALU.mult)
                    for s_ in range(16):
                        S.mm(op_[pr, base:base + C], VC[:, s_, g * 64:(g + 1) * 64], PTCM[:, s_, :],
                             start=(s_ == 0), stop=False)
                    S.mm(op_[pr, base:base + C], VDn[:, g * 64:(g + 1) * 64], PTN[:, hh, :], start=False, stop=True)
                S.copy("act", MIX[:, 4 + 2 * half:6 + 2 * half, :], op_[:, 0:2 * C].re("p (c t) -> p c t", c=2))

        interleave(gen_ia(), gen_ii())
        R = res["R"][1]
        S.pop()
        S.pop()

        S.push()
        if FLAGS.get("stop") == "state":
            S.pop(); S.pop(); S.pop(); return
        SS = S.sbuf("sSS", [128, 16, 4, 128])
        SSB = S.sbuf("sSSB", [128, 16, 4, 128], BF16)
        NWTM = S.sbuf("sNWTM", [128, 16, C], BF16)
        VNM = S.sbuf("sVNM", [C, 8, 128], BF16)
        sd_r = sd_in[l].rearrange("s h d v -> d s h v")
        for sg in range(4):
            S.dma("sp", SS[:, sg * 4:(sg + 1) * 4, :, :], sd_r[:, sg * 4:(sg + 1) * 4, :, :])
            S.copy("dve" if sg % 2 else "act", SSB[:, sg * 4:(sg + 1) * 4, :, :], SS[:, sg * 4:(sg + 1) * 4, :, :])
        vps = nps()
        for h in range(4):
            S.tt("dve", NWTM[:, :, :], NWT[:, None, h * C:(h + 1) * C].bc([128, 16, C]), SMB[:, :, :], ALU.mult)
            hv = vps[0:C, h * 128:(h + 1) * 128]
            S.mm(hv, R[:, h * C:(h + 1) * C], VB[:, h * 128:(h + 1) * 128], start=True, stop=False)
            for s_ in range(16):
                S.mm(hv, NWTM[:, s_, :], SSB[:, s_, h, :], start=False, stop=(s_ == 15))
        S.copy("dve", VNEW[:, :], vps[0:C, :])
        ops_ = nps()
        for h in range(4):
            S.tt("dve", NWTM[:, :, :], QG[:, None, h * C:(h + 1) * C].bc([128, 16, C]), SMB[:, :, :], ALU.mult)
            for s_ in range(16):
                S.mm(ops_[:, h * C:(h + 1) * C], SSB[:, s_, h, :], NWTM[:, s_, :], start=(s_ == 0), stop=False)
            S.mm(ops_[:, h * C:(h + 1) * C], VNEW[:, h * 128:(h + 1) * 128], PT[:, h * C:(h + 1) * C],
                 start=False, stop=True)
        S.copy("act", OT[:, :], ops_[:, 0:W4])
        for h in range(4):
            for sg in range(4):
                if sg % 2 == 0:
                    S.tt("dve", VNM[:, :, :], VNEW[:, None, h * 128:(h + 1) * 128].bc([C, 8, 128]),
                         con("sm")[0:C, sg * 4:sg * 4 + 8, None].bc([C, 8, 128]), ALU.mult)
                sn = nps()
                for sl in range(4):
                    s_ = sg * 4 + sl
                    S.mm(sn[:, sl * 128:(sl + 1) * 128], KDEC[:, h * 128:(h + 1) * 128], VNM[:, s_ % 8, :])
                for sl in range(4):
                    s_ = sg * 4 + sl
                    S.stt("dve", SS[:, s_, h, :], SS[:, s_, h, :], GLs[:, s_ * 4 + h:s_ * 4 + h + 1],
                          sn[:, sl * 128:(sl + 1) * 128], ALU.mult, ALU.add)
        nsd_r = nsd_s[l].rearrange("s h d v -> d s h v")
        for sg in range(4):
            S.dma("sp", nsd_r[:, sg * 4:(sg + 1) * 4, :, :], SS[:, sg * 4:(sg + 1) * 4, :, :])
        S.pop()
        SQ1, RSQ = SC1, SC2
        S.act(SQ1[:, :], OT[:, :], AF.Square)
        sps = nps()
        S.mm(sps[:, 0:W4], con("o128"), SQ1[:, :])
        rsqrt(RSQ[:, :], sps[:, 0:W4], EPS)
        S.stt("dve", OT[:, :], OT[:, :], NAW[:, l:l + 1], RSQ[:, :], ALU.mult, ALU.mult)
        S.tt("dve", MIX[:, 0:4, :], OT[:, :].re(H4, h=4), ZS[:, :].re(H4, h=4), ALU.mult)
        S.pop()

        if FLAGS.get("stop") == "out":
            S.pop(); return
        S.push()
        DDs = S.sbuf("sDD", [128, 8, C])
        SQs = S.sbuf("sSQ", [128, 8, C])
        RSs = S.sbuf("sRS", [128, C])
        for hb in range(2):
            mps = nps()
            for dcl in range(4):
                dc = hb * 4 + dcl
                for kc in range(8):
                    S.mm(mps[:, dcl * C:(dcl + 1) * C], WOUT[:, kc, dc * 128:(dc + 1) * 128], MIX[:, kc, :],
                         start=(kc == 0), stop=(kc == 7))
            xv = X[:, hb * 4:(hb + 1) * 4, T0:T0 + C]
            S.stt("dve", xv, xv, ALPHA, mps[:, 0:W4].re(H4, h=4), ALU.mult, ALU.add)
        layernorm(T0, C, 0, l, DDs, SQs, RSs)
        S.pop()
        S.pop()

    def phase_b(l):
        S.push()
        LNW = 128
        XB = S.sbuf("XB", [128, 8, 1088], BF16)
        ACTB = S.sbuf("ACTB", [128, 11, 1088], BF16)
        WO = [S.sbuf("WO%d" % i, [128, 11, 1024], BF16) for i in range(2)]
        WG = [S.sbuf("WG%d" % i, [128, 8, 256], BF16) for i in range(2)]
        WU = [S.sbuf("WU%d" % i, [128, 8, 256], BF16) for i in range(2)]
        SIL = [S.sbuf("SIL%d" % i, [128, 512]) for i in range(2)]
        DD = [S.sbuf("DD%d" % i, [128, 8, LNW]) for i in range(2)]
        SQ = [S.sbuf("SQ%d" % i, [128, 8, LNW]) for i in range(2)]
        RS = [S.sbuf("RS%d" % i, [128, LNW]) for i in range(2)]
        LT = {"DD": DD, "SQ": SQ, "RS": RS}
        wfi_r = w_fi[l].rearrange("(c p) n -> p c n", p=128)
        wfo_r = w_fo[l].rearrange("(c p) n -> p c n", p=128)
        wcnt = [0]

        def ffn_pass(t0, cgs):
            ntok = sum(cgs)
            for c in range(8):
                S.copy("act" if c % 2 else "dve", XB[:, c, 0:ntok], X[:, c, t0:t0 + ntok])
            for half in range(2):
                wo = WO[half]
                S.dma("pool", wo[:, :, :], wfo_r[:, half * 11:(half + 1) * 11, :])
                for fp in range(6):
                    nf = 2 if fp < 5 else 1
                    fc0 = half * 11 + fp * 2
                    wg, wu = WG[wcnt[0] % 2], WU[wcnt[0] % 2]
                    wcnt[0] += 1
                    S.dma("pool", wg[:, :, 0:nf * 128], wfi_r[:, :, fc0 * 128:(fc0 + nf) * 128])
                    S.dma("pool", wu[:, :, 0:nf * 128], wfi_r[:, :, FF + fc0 * 128:FF + (fc0 + nf) * 128])
                    for f in range(nf):
                        fl = fp * 2 + f
                        cs = 0
                        for ci, cw in enumerate(cgs):
                            gp_, up_ = nps(), nps()
                            for kc in range(8):
                                S.mm(gp_[:, 0:cw], wg[:, kc, f * 128:(f + 1) * 128], XB[:, kc, cs:cs + cw],
                                     start=(kc == 0), stop=(kc == 7))
                            for kc in range(8):
                                S.mm(up_[:, 0:cw], wu[:, kc, f * 128:(f + 1) * 128], XB[:, kc, cs:cs + cw],
                                     start=(kc == 0), stop=(kc == 7))
                            sl = SIL[(fl + ci) % 2]
                            S.act(sl[:, 0:cw], gp_[:, 0:cw], AF.Silu)
                            S.tt("dve", ACTB[:, fl, cs:cs + cw], sl[:, 0:cw], up_[:, 0:cw], ALU.mult)
                            cs += cw
                        yield
                for dc in range(8):
                    cs = 0
                    for cw in cgs:
                        yp = nps()
                        for fl in range(11):
                            S.mm(yp[:, 0:cw], wo[:, fl, dc * 128:(dc + 1) * 128], ACTB[:, fl, cs:cs + cw],
                                 start=(fl == 0), stop=(fl == 10))
                        xv = X[:, dc, t0 + cs:t0 + cs + cw]
                        if half == 0:
                            S.stt("dve", xv, xv, ALPHA, yp[:, 0:cw], ALU.mult, ALU.add)
                        else:
                            S.tt("dve", xv, xv, yp[:, 0:cw], ALU.add)
                        cs += cw
                    yield

        def ln_pass(t0, ntok):
            pieces = []
            c = 0
            while c < ntok:
                n = min(LNW, ntok - c)
                pieces.append((t0 + c, n))
                c += n
            for k in range(0, len(pieces), 2):
                gens = [layernorm_g(pc[0], pc[1], 2, l, LT["DD"][j], LT["SQ"][j], LT["RS"][j],
                                    sub_eng="pool" if j else "dve", affine_eng="dve")
                        for j, pc in enumerate(pieces[k:k + 2])]
                while gens:
                    for g_ in list(gens):
                        try:
                            next(g_)
                        except StopIteration:
                            gens.remove(g_)
                    yield

        for _ in ffn_pass(0, (512, 512)):
            pass
        interleave(ffn_pass(1024, (512, 512, 64)), ln_pass(0, 1024))
        S.pop()
        nxt = alloc_mix_weights(l + 1) if (l + 1 < n_layers and do_a) else None
        S.push()
        DD = [S.sbuf("DDt%d" % i, [128, 8, LNW]) for i in range(2)]
        SQ = [S.sbuf("SQt%d" % i, [128, 8, LNW]) for i in range(2)]
        RS = [S.sbuf("RSt%d" % i, [128, LNW]) for i in range(2)]
        LT.update(DD=DD, SQ=SQ, RS=RS)
        for _ in ln_pass(1024, 1088):
            pass
        S.pop()
        return nxt

    wts = alloc_mix_weights(0) if do_a else None
    for l in range(n_layers):
        if do_a:
            phase_a(l, *wts)
            wts = None
        if do_b:
            wts = phase_b(l)
        elif do_a and l + 1 < n_layers:
            wts = alloc_mix_weights(l + 1)

    yT_r = yT.rearrange("(c p) t -> p c t", p=128)
    for c in range(8):
        S.dma("sp", yT_r[:, c, :], X[:, c, :])
    S.barrier()
    S.close()
    return nc


def make_in_maps(inp):
    f = np.float32
    x_prompt, x_sample = inp["x_prompt"], inp["x_sample"]
    conv_wT = np.ascontiguousarray(inp["conv_w"].reshape(L, 4, 12, 128).transpose(3, 0, 2, 1)).astype(f)
    lnp = np.stack([inp["ln1_g"], inp["ln1_b"], inp["ln2_g"], inp["ln2_b"]], axis=1)
    lnp = np.ascontiguousarray(lnp.reshape(L, 4, 8, 128).transpose(3, 0, 1, 2)).astype(f)
    gp = np.zeros((128, L, 3, 8), f)
    gp[:, :, 0, 0:4] = inp["dt_bias"][None]
    gp[:, :, 1, 0:4] = inp["a_log"][None]
    gp[:, :, 2, 0:8] = inp["sinks"][None]
    naw = np.ascontiguousarray(inp["norm_a_w"].T).astype(f)
    shared = {
        "w_in": inp["w_in"], "w_out": inp["w_out"], "w_ffn_in": inp["w_ffn_in"], "w_ffn_out": inp["w_ffn_out"],
        "conv_wT": conv_wT, "lnp": lnp, "gp": gp, "naw": naw, "consts": CONSTS, "smb": SMB_CONST,
    }
    maps = []
    for c in range(NCORES):
        sl = slice(NSQ * c, NSQ * (c + 1))
        xt = np.concatenate([x_prompt[c], x_sample[sl].reshape(NS, D)], axis=0).T
        sc = inp["state_conv"][:, sl]
        scT = sc.reshape(L, NSQ, 3, 12, 128).transpose(0, 4, 3, 1, 2)
        ck = inp["cache_swa_k"][:, sl]
        ckT = np.tile(ck.transpose(0, 4, 1, 3, 2), (1, 2, 1, 1, 1))
        cv = inp["cache_swa_v"][:, sl]
        cvD = cv.transpose(0, 2, 1, 3, 4).reshape(L, 128, NSQ, 128)
        m = dict(shared)
        m.update({
            "xT": np.ascontiguousarray(xt, f),
            "sd": np.ascontiguousarray(inp["state_delta"][:, sl], f),
            "scT": np.ascontiguousarray(scT, f),
            "ckT": np.ascontiguousarray(ckT, f),
            "cvD": np.ascontiguousarray(cvD, f),
            "ck_raw": np.ascontiguousarray(ck.reshape(L, NSQ, 128, 128), f),
            "cv_raw": np.ascontiguousarray(cv.reshape(L, NSQ, 128, 128), f),
        })
        maps.append(m)
    return maps


_NC_CACHE = {}


def kernel(**inputs):
    inp = {k: np.asarray(v) for k, v in inputs.items()}
    if "nc" not in _NC_CACHE:
        _NC_CACHE["nc"] = build()
    nc = _NC_CACHE["nc"]
    maps = make_in_maps(inp)
    res = run_bass_kernel_spmd(nc, maps, core_ids=list(range(NCORES))).results
    f = np.float32
    y_p = np.stack([res[c]["yT"][:, :NP].T for c in range(NCORES)]).astype(f)
    y_s = np.concatenate([res[c]["yT"][:, NP:].T.reshape(NSQ, 4, D) for c in range(NCORES)]).astype(f)
    nsd_p = np.stack([res[c]["nsd_p"] for c in range(NCORES)], axis=1).astype(f)
    nsc_p = np.stack([res[c]["nsc_p"] for c in range(NCORES)], axis=1).astype(f)
    nk_p = np.stack([res[c]["nk_p"].reshape(L, 128, 2, 64) for c in range(NCORES)], axis=1).astype(f)
    nv_p = np.stack([res[c]["nv_p"].reshape(L, 128, 2, 64) for c in range(NCORES)], axis=1).astype(f)
    nsd_s = np.concatenate([res[c]["nsd_s"] for c in range(NCORES)], axis=1).astype(f)
    nsc_s = np.concatenate([res[c]["nsc_s"] for c in range(NCORES)], axis=1).astype(f)
    nk_s = np.concatenate([res[c]["nk_s"].reshape(L, NSQ, 128, 2, 64) for c in range(NCORES)], axis=1).astype(f)
    nv_s = np.concatenate([res[c]["nv_s"].reshape(L, NSQ, 128, 2, 64) for c in range(NCORES)], axis=1).astype(f)
    return (y_p, y_s, nsd_p, nsc_p, nk_p, nv_p, nsd_s, nsc_s, nk_s, nv_s)
```

```python
import numpy as np
import concourse.bass as bass
import concourse.mybir as mybir
from concourse.bass_utils import run_bass_kernel_spmd

F32 = mybir.dt.float32
BF16 = mybir.dt.bfloat16
AF = mybir.ActivationFunctionType
ALU = mybir.AluOpType
AX = mybir.AxisListType

NCORES = 8
D = 1024
NP = 2048
NSQ = 16
NS = 64
T = NP + NS
L = 4
PW = 2824
FF = 2816
NFC = 22
ALPHA = float(8 ** 0.25)
EPS = 1e-6
C_Q, C_K, C_V, C_Z, C_B, C_A, C_QB, C_KB, C_VB = 0, 512, 1024, 1536, 2048, 2052, 2056, 2568, 2696
NEG = -30000.0
FLAGS = {}


class View:
    __slots__ = ("b", "ap")

    def __init__(self, b, ap):
        self.b = b
        self.ap = ap

    def __getitem__(self, idx):
        return View(self.b, self.ap[idx])

    def bc(self, shape):
        return View(self.b, self.ap.broadcast_to(list(shape)))

    def re(self, pat, **kw):
        return View(self.b, self.ap.rearrange(pat, **kw))

    def bitcast(self, dt):
        return View(self.b, self.ap.bitcast(dt))


class Buf:
    __slots__ = ("t", "w", "r", "name", "psum")

    def __init__(self, t, name="", psum=False):
        self.t = t
        self.w = None
        self.r = {}
        self.name = name
        self.psum = psum

    def __getitem__(self, idx):
        return View(self, self.t[idx])


class MultiBuf:
    def __init__(self, t, bounds, name=""):
        self.t = t
        self.bounds = list(bounds)
        self.doms = [Buf(t, "%s%d" % (name, i)) for i in range(len(bounds) - 1)]

    def __getitem__(self, idx):
        ap = self.t[idx]
        cs = idx[2] if isinstance(idx, tuple) and len(idx) > 2 else slice(None)
        start = cs.start or 0
        stop = cs.stop if cs.stop is not None else self.bounds[-1]
        doms = [d for d, a, b in zip(self.doms, self.bounds[:-1], self.bounds[1:]) if start < b and stop > a]
        return View(doms[0] if len(doms) == 1 else tuple(doms), ap)


class Eng:
    def __init__(self, name, obj, sem, self_ordered=False):
        self.name, self.obj, self.sem = name, obj, sem
        self.count = 0
        self.seen = {}
        self.self_ordered = self_ordered


class Sched:
    def __init__(self, nc, n_dma_sems=32):
        self.nc = nc
        self._scopes = [[]]
        self.engs = {}
        for name, obj, so in (("pe", nc.tensor, True), ("dve", nc.vector, False), ("act", nc.scalar, False),
                              ("pool", nc.gpsimd, False), ("sp", nc.sync, False)):
            self.engs[name] = Eng(name, obj, self._enter(nc.semaphore("s_" + name)), so)
        self.dma_slots = [[self._enter(nc.semaphore("d_%d" % i)), 0] for i in range(n_dma_sems)]
        self.q_slots = {"sp": self.dma_slots[:n_dma_sems - 12], "pool": self.dma_slots[n_dma_sems - 12:]}
        self.q_rr = {"sp": 0, "pool": 0}
        self.hist = {}
        self.nbuf = 0

    def _enter(self, cm):
        v = cm.__enter__()
        self._scopes[-1].append(cm)
        return v

    def push(self):
        self._scopes.append([])

    def pop(self):
        self.barrier()
        sc = self._scopes.pop()
        while sc:
            sc.pop().__exit__(None, None, None)

    def close(self):
        while self._scopes:
            sc = self._scopes.pop()
            while sc:
                sc.pop().__exit__(None, None, None)

    def sbuf(self, name, shape, dtype=F32):
        self.nbuf += 1
        return Buf(self._enter(self.nc.sbuf_tensor("%s_%d" % (name, self.nbuf), list(shape), dtype)), name)

    def psum(self, name, shape, dtype=F32):
        self.nbuf += 1
        return Buf(self._enter(self.nc.psum_tensor("%s_%d" % (name, self.nbuf), list(shape), dtype)), name, psum=True)

    @staticmethod
    def _add(deps, tok):
        if tok is None:
            return
        k = id(tok[0])
        if k not in deps or deps[k][1] < tok[1]:
            deps[k] = tok

    def _collect(self, reads, writes):
        deps = {}
        for b in reads:
            self._add(deps, b.w)
            if b.psum:
                for tok in b.r.values():
                    self._add(deps, tok)
        for b in writes:
            self._add(deps, b.w)
            for tok in b.r.values():
                self._add(deps, tok)
        return deps

    def _waits(self, eng, deps):
        for k, (sem, val) in sorted(deps.items(), key=lambda kv: -kv[1][1]):
            if eng.self_ordered and sem is eng.sem:
                continue
            if eng.seen.get(k, 0) < val:
                eng.obj.wait_ge(sem, val)
                eng.seen[k] = val
                for k2, v2 in self.hist.get((k, val), {}).items():
                    if eng.seen.get(k2, 0) < v2:
                        eng.seen[k2] = v2

    def _mark(self, tok, reads, writes):
        k = id(tok[0])
        for b in reads:
            old = b.r.get(k)
            if old is None or old[1] < tok[1]:
                b.r[k] = tok
        for b in writes:
            b.w = tok
            b.r = {}

    @staticmethod
    def _bufs(vs):
        out = []
        for v in vs:
            if isinstance(v, View):
                for b in (v.b if isinstance(v.b, tuple) else (v.b,)):
                    if b not in out:
                        out.append(b)
        return out

    def op(self, engname, fn, reads, writes, signal=True):
        eng = self.engs[engname]
        rb, wb = self._bufs(reads), self._bufs(writes)
        self._waits(eng, self._collect(rb, wb))
        ins = fn(eng.obj)
        if signal:
            eng.count += 1
            ins.then_inc(eng.sem, 1)
            self.hist[(id(eng.sem), eng.count)] = dict(eng.seen)
            self._mark((eng.sem, eng.count), rb, wb)
        else:
            self._mark((eng.sem, eng.count + 1), rb, wb)

    def dma(self, q, out, in_, **kw):
        eng = self.engs[q]
        rb, wb = self._bufs([in_]), self._bufs([out])
        deps = self._collect(rb, wb)
        slots = self.q_slots[q]
        slot = slots[self.q_rr[q]]
        self.q_rr[q] = (self.q_rr[q] + 1) % len(slots)
        if slot[1] > 0:
            self._add(deps, (slot[0], slot[1]))
        self._waits(eng, deps)
        o = out.ap if isinstance(out, View) else out
        i = in_.ap if isinstance(in_, View) else in_
        ins = eng.obj.dma_start(out=o, in_=i, **kw)
        slot[1] += 16
        ins.then_inc(slot[0], 16)
        self.hist[(id(slot[0]), slot[1])] = dict(eng.seen)
        self._mark((slot[0], slot[1]), rb, wb)

    def barrier(self):
        toks = {}
        for e in self.engs.values():
            if e.count > 0:
                self._add(toks, (e.sem, e.count))
        for s in self.dma_slots:
            if s[1] > 0:
                self._add(toks, (s[0], s[1]))
        for e in self.engs.values():
            so, e.self_ordered = e.self_ordered, False
            self._waits(e, dict(toks))
            e.self_ordered = so

    def mm(self, o, l, r, start=True, stop=True):
        self.op("pe", lambda e: e.matmul(o.ap, lhsT=l.ap, rhs=r.ap, start=start, stop=stop), [l, r], [o],
                signal=bool(stop))

    def tr(self, o, i, ident):
        self.op("pe", lambda e: e.transpose(o.ap, i.ap, ident.ap), [i, ident], [o])

    def act(self, o, i, func, bias=None, scale=None, accum=None):
        kw = {}
        rd = [i]
        if bias is not None:
            kw["bias"] = bias.ap if isinstance(bias, View) else bias
            rd.append(bias)
        if scale is not None:
            kw["scale"] = scale.ap if isinstance(scale, View) else scale
            rd.append(scale)
        wr = [o]
        if accum is not None:
            kw["accum_out"] = accum.ap
            wr.append(accum)
        self.op("act", lambda e: e.activation(out=o.ap, in_=i.ap, func=func, **kw), rd, wr)

    def tt(self, eng, o, a, b, op):
        self.op(eng, lambda e: e.tensor_tensor(out=o.ap, in0=a.ap, in1=b.ap, op=op), [a, b], [o])

    def ts(self, eng, o, a, s1, op0, s2=None, op1=None):
        rd = [a, s1, s2]
        v1 = s1.ap if isinstance(s1, View) else s1
        v2 = s2.ap if isinstance(s2, View) else s2
        if op1 is None:
            self.op(eng, lambda e: e.tensor_scalar(out=o.ap, in0=a.ap, scalar1=v1, scalar2=None, op0=op0), rd, [o])
        else:
            self.op(eng, lambda e: e.tensor_scalar(out=o.ap, in0=a.ap, scalar1=v1, scalar2=v2, op0=op0, op1=op1),
                    rd, [o])

    def stt(self, eng, o, a, s, b, op0, op1):
        sv = s.ap if isinstance(s, View) else s
        self.op(eng, lambda e: e.scalar_tensor_tensor(out=o.ap, in0=a.ap, scalar=sv, in1=b.ap, op0=op0, op1=op1),
                [a, s, b], [o])

    def copy(self, eng, o, i):
        if eng == "act":
            self.op("act", lambda e: e.copy(out=o.ap, in_=i.ap), [i], [o])
        else:
            self.op(eng, lambda e: e.tensor_copy(out=o.ap, in_=i.ap), [i], [o])

    def memset(self, eng, o, val):
        self.op(eng, lambda e: e.memset(o.ap, val), [], [o])

    def red(self, eng, o, i, op):
        self.op(eng, lambda e: e.tensor_reduce(out=o.ap, in_=i.ap, axis=AX.X, op=op), [i], [o])


CONST_SLOTS = {}


def _build_consts():
    parts = []
    off = [0]

    def add(name, arr):
        a = np.zeros((128, arr.shape[1]), np.float32)
        a[:arr.shape[0]] = arr
        CONST_SLOTS[name] = (off[0], arr.shape[1])
        off[0] += arr.shape[1]
        parts.append(a)

    j = np.arange(128)[:, None]
    i = np.arange(128)[None, :]
    add("ident", (j == i).astype(np.float32))
    add("ones", np.ones((128, 128), np.float32))
    add("o1024", np.full((128, 128), 1.0 / 1024, np.float32))
    add("o128", np.full((128, 128), 1.0 / 128, np.float32))
    add("tri_p", (j <= i).astype(np.float32))
    add("neg_incl_p", np.where(i >= j, 0.0, NEG).astype(np.float32))
    add("neg_strict_p", np.where(i > j, 0.0, NEG).astype(np.float32))
    same = (j // 4 == i // 4) & (j < 64) & (i < 64)
    add("tri_s", (same & (j <= i)).astype(np.float32))
    add("neg_incl_s", np.where(same & (i >= j), 0.0, NEG).astype(np.float32))
    add("neg_strict_s", np.where(same & (i > j), 0.0, NEG).astype(np.float32))
    add("blk_s", same.astype(np.float32))
    s16 = np.arange(16)[None, :]
    add("sm", ((j // 4 == s16) & (j < 64)).astype(np.float32))
    smb = np.zeros((16, 64), np.float32)
    for s in range(16):
        smb[s, 4 * s:4 * s + 4] = 1.0
    global SMB_CONST
    SMB_CONST = np.broadcast_to(smb.reshape(1, 1024), (128, 1024)).copy()
    r = np.arange(128)[:, None]
    c = np.arange(256)[None, :]
    band = (c > r) & (c <= r + 128)
    add("swa_p", np.where(band, 0.0, NEG * 8).astype(np.float32))
    add("swa_p0", np.where(band & (c >= 128), 0.0, NEG * 8).astype(np.float32))
    ms = np.full((128, 192), NEG * 8, np.float32)
    for q in range(64):
        t = q % 4
        ms[q, t + 1:128] = 0.0
        for t2 in range(t + 1):
            ms[q, 128 + (q // 4) * 4 + t2] = 0.0
    add("swa_s", ms)
    return np.concatenate(parts, axis=1)


CONSTS = _build_consts()
NCON = CONSTS.shape[1]


def build(n_layers=L, do_a=True, do_b=True, dbg_n=0, dbg_hook=None, do_prompt=True, do_sample=True):
    nc = bass.Bass("TRN2", target_bir_lowering=False)

    def din(name, shape):
        return nc.dram_tensor(name, list(shape), F32, kind="ExternalInput").ap()

    def dout(name, shape):
        return nc.dram_tensor(name, list(shape), F32, kind="ExternalOutput").ap()

    xT = din("xT", [D, T])
    sd_in = din("sd", [L, NSQ, 4, 128, 128])
    scT_in = din("scT", [L, 128, 12, NSQ, 3])
    ckT_in = din("ckT", [L, 128, NSQ, 2, 128])
    cvD_in = din("cvD", [L, 128, NSQ, 128])
    ck_raw = din("ck_raw", [L, NSQ, 128, 128])
    cv_raw = din("cv_raw", [L, NSQ, 128, 128])
    w_in = din("w_in", [L, D, PW])
    w_out = din("w_out", [L, D, D])
    w_fi = din("w_ffn_in", [L, D, 2 * FF])
    w_fo = din("w_ffn_out", [L, FF, D])
    convw_in = din("conv_wT", [128, L, 12, 4])
    lnp_in = din("lnp", [128, L, 4, 8])
    gp_in = din("gp", [128, L, 3, 8])
    naw_in = din("naw", [128, L])
    consts_in = din("consts", [128, NCON])
    smb_in = din("smb", [128, 1024])

    yT = dout("yT", [D, T])
    nsd_p = dout("nsd_p", [L, 4, 128, 128])
    nsc_p = dout("nsc_p", [L, 3, 1536])
    nk_p = dout("nk_p", [L, 128, 128])
    nv_p = dout("nv_p", [L, 128, 128])
    nsd_s = dout("nsd_s", [L, NSQ, 4, 128, 128])
    nsc_s = dout("nsc_s", [L, NSQ, 3, 1536])
    nk_s = dout("nk_s", [L, NSQ, 128, 128])
    nv_s = dout("nv_s", [L, NSQ, 128, 128])
    dbg = dout("dbg", [128, dbg_n]) if dbg_n else None

    S = Sched(nc)
    X = MultiBuf(S.sbuf("X", [128, 8, T]).t, list(range(0, T + 1, 64)), "X")
    CON = S.sbuf("CON", [128, NCON])
    CONB = S.sbuf("CONB", [128, 256], BF16)
    CW = S.sbuf("CW", [128, L, 12, 4])
    LNP = S.sbuf("LNP", [128, L, 4, 8])
    GP = S.sbuf("GP", [128, L, 3, 8])
    NAW = S.sbuf("NAW", [128, L])
    PS = [S.psum("PS%d" % i, [128, 512]) for i in range(8)]
    ps_rr = [0]

    def nps():
        p = PS[ps_rr[0]]
        ps_rr[0] = (ps_rr[0] + 1) % 8
        return p

    def con(name):
        o, n = CONST_SLOTS[name]
        return CON[:, o:o + n]

    xT_r = xT.rearrange("(c p) t -> p c t", p=128)
    for c in range(8):
        S.dma("sp", X[:, c, :], xT_r[:, c, :])
    S.dma("sp", CON[:, :], consts_in)
    S.dma("sp", CW[:], convw_in)
    S.dma("sp", LNP[:], lnp_in)
    S.dma("sp", GP[:], gp_in)
    S.dma("sp", NAW[:], naw_in)
    S.copy("dve", CONB[:, 0:128], con("ident"))
    S.copy("dve", CONB[:, 128:256], con("ones"))
    IDB = CONB[:, 0:128]

    dbg_off = [0]

    def dump(view, n):
        if dbg is None:
            return
        p = view.ap.shape[0]
        S.dma("sp", dbg[0:p, dbg_off[0]:dbg_off[0] + n], view)
        dbg_off[0] += n

    def interleave(*gens):
        gens = list(gens)
        while gens:
            for g_ in list(gens):
                try:
                    next(g_)
                except StopIteration:
                    gens.remove(g_)

    def rsqrt(o, i, eps):
        S.act(o, i, AF.Ln, bias=eps)
        S.act(o, o, AF.Exp, scale=-0.5)

    def layernorm_g(cols, ncol, gi, l, DD, SQ, RS, sub_eng="dve", affine_eng="act"):
        c0 = cols
        xs = X[:, :, c0:c0 + ncol]
        S.red("dve", RS[:, 0:ncol], xs.re("p c t -> p t c"), ALU.add)
        mp = nps()
        S.mm(mp[:, 0:ncol], con("o1024"), RS[:, 0:ncol])
        S.copy("act", RS[:, 0:ncol], mp[:, 0:ncol])
        yield
        S.tt(sub_eng, DD[:, :, 0:ncol], xs, RS[:, None, 0:ncol].bc([128, 8, ncol]), ALU.subtract)
        S.act(SQ[:, :, 0:ncol], DD[:, :, 0:ncol], AF.Square)
        yield
        S.red("dve", RS[:, 0:ncol], SQ[:, :, 0:ncol].re("p c t -> p t c"), ALU.add)
        vp = nps()
        S.mm(vp[:, 0:ncol], con("o1024"), RS[:, 0:ncol])
        rsqrt(RS[:, 0:ncol], vp[:, 0:ncol], EPS)
        yield
        S.tt("dve", DD[:, :, 0:ncol], DD[:, :, 0:ncol], RS[:, None, 0:ncol].bc([128, 8, ncol]), ALU.mult)
        yield
        if affine_eng == "act":
            for c in range(8):
                S.act(X[:, c, c0:c0 + ncol], DD[:, c, 0:ncol], AF.Identity,
                      bias=LNP[:, l, gi + 1, c:c + 1], scale=LNP[:, l, gi, c:c + 1])
        else:
            S.tt(affine_eng, DD[:, :, 0:ncol], DD[:, :, 0:ncol], LNP[:, l, gi, :, None].bc([128, 8, ncol]), ALU.mult)
            S.tt(affine_eng, xs, DD[:, :, 0:ncol], LNP[:, l, gi + 1, :, None].bc([128, 8, ncol]), ALU.add)

    def layernorm(cols, ncol, gi, l, DD, SQ, RS, sub_eng="dve"):
        for _ in layernorm_g(cols, ncol, gi, l, DD, SQ, RS, sub_eng):
            pass


    H4 = "p (h t) -> p h t"

    def v4(ps, n=128):
        return ps[:, 0:4 * n].re(H4, h=4)

    def gates_and_decay(l, C, XBt, WIN, NEXPA, tri, blk_ones, G):
        gps = nps()
        for kc in range(8):
            S.mm(gps[0:C, 0:8], XBt[:, kc, 0:C], WIN[:, kc, C_B:C_B + 8], start=(kc == 0), stop=(kc == 7))
        S.act(G["LNB"][0:C, :], gps[0:C, 0:4], AF.Exp, scale=-1.0)
        S.tt("dve", G["G1"][0:C, :], gps[0:C, 4:8], GP[0:C, l, 0, 0:4], ALU.add)
        S.act(G["G1"][0:C, :], G["G1"][0:C, :], AF.Exp)
        S.act(G["LNB"][0:C, :], G["LNB"][0:C, :], AF.Ln, bias=1.0)
        S.act(G["G1"][0:C, :], G["G1"][0:C, :], AF.Ln, bias=1.0)
        S.act(G["BETA"][0:C, :], G["LNB"][0:C, :], AF.Exp, scale=-1.0)
        S.tt("dve", G["G"][0:C, :], G["G1"][0:C, :], NEXPA[0:C, :], ALU.mult)
        cps = nps()
        S.mm(cps[0:C, 0:4], tri[0:C, 0:C], G["G"][0:C, :])
        S.mm(cps[0:C, 4:8], blk_ones[0:C, 0:C], G["G"][0:C, :])
        S.copy("dve", G["GCC"][0:C, :], cps[0:C, 0:8])
        S.act(G["EGC"][0:C, :], G["GCC"][0:C, 0:4], AF.Exp)
        S.tt("dve", G["EDL"][0:C, :], G["GCC"][0:C, 4:8], G["GCC"][0:C, 0:4], ALU.subtract)
        S.act(G["EDL"][0:C, :], G["EDL"][0:C, :], AF.Exp)
        S.tt("dve", G["BEG"][0:C, :], G["BETA"][0:C, :], G["EGC"][0:C, :], ALU.mult)
        S.copy("dve", G["G2"][0:C, 0:4], G["GCC"][0:C, 0:4])
        S.tt("dve", G["G2"][0:C, 4:8], G["GCC"][0:C, 0:4], G["LNB"][0:C, :], ALU.subtract)
        tp = nps()
        S.tr(tp[0:4, 0:C], G["G2"][0:C, 0:4], con("ident")[0:C, 0:C])
        S.tr(tp[0:4, C:2 * C], G["G2"][0:C, 4:8], con("ident")[0:C, 0:C])
        S.copy("act", G["GT"][0:4, 0:2 * C], tp[0:4, 0:2 * C])
        S.tt("dve", G["BD"][0:4, 0:8 * C].re("p (a h i) -> p a h i", a=2, h=4),
             G["GT"][0:4, 0:2 * C].re("p (a i) -> p a i", a=2)[:, :, None, :].bc([4, 2, 4, C]),
             con("ident")[0:4, None, 0:4, None].bc([4, 2, 4, C]), ALU.mult)

    def decay_mats(C, G, neg_incl, neg_strict, DT, DTB, EB, TMP1, TMP2):
        W = 4 * C
        for a_, (neg, tmp, dst) in enumerate(((neg_incl, TMP1, DT), (neg_strict, TMP2, DTB))):
            bps = nps()
            S.mm(bps[:, 0:W], con("ones")[0:4, :], G["BD"][0:4, a_ * W:(a_ + 1) * W])
            if a_ == 0:
                S.act(EB[:, 0:W], bps[:, 0:W], AF.Exp)
            t4 = tmp[0:C, 0:W].re(H4, h=4)
            S.tt("dve", t4, bps[0:C, 0:W].re(H4, h=4), G["GCC"][0:C, 0:4, None].bc([C, 4, C]), ALU.subtract)
            S.tt("dve", t4, t4, neg[0:C, None, 0:C].bc([C, 4, C]), ALU.add)
            S.act(dst[0:C, 0:W], tmp[0:C, 0:W], AF.Exp)

    def tri_inverse(C, BMb, AMb, XAb, YAb, R0, R1, Rb0, Rb1, res):
        W = 4 * C
        idb = con("ident")[0:C, None, 0:C].bc([C, 4, C])
        S.tt("dve", R0[0:C, 0:W].re(H4, h=4), idb, BMb[0:C, 0:W].re(H4, h=4), ALU.subtract)
        S.tt("dve", Rb0[0:C, 0:W].re(H4, h=4), idb, BMb[0:C, 0:W].re(H4, h=4), ALU.subtract)
        Xc, Yc, Xn, Yn = BMb, AMb, XAb, YAb
        Rc, Rn, Rbc, Rbn = R0, R1, Rb0, Rb1
        k = 1
        while (1 << k) < C:
            last = (1 << (k + 1)) >= C
            yps = nps()
            for h in range(4):
                hc = slice(h * C, (h + 1) * C)
                S.mm(yps[0:C, hc], Xc[0:C, hc], Yc[0:C, hc])
            if not last:
                xps = nps()
                for h in range(4):
                    hc = slice(h * C, (h + 1) * C)
                    S.mm(xps[0:C, hc], Yc[0:C, hc], Xc[0:C, hc])
            S.copy("act", Yn[0:C, 0:W], yps[0:C, 0:W])
            if not last:
                S.copy("act", Xn[0:C, 0:W], xps[0:C, 0:W])
            rps = nps()
            for h in range(4):
                hc = slice(h * C, (h + 1) * C)
                S.mm(rps[0:C, hc], Yn[0:C, hc], Rbc[0:C, hc])
            S.tt("dve", Rn[0:C, 0:W], rps[0:C, 0:W], Rc[0:C, 0:W], ALU.add)
            S.tt("dve", Rbn[0:C, 0:W], rps[0:C, 0:W], Rc[0:C, 0:W], ALU.add)
            Xc, Yc, Xn, Yn = Xn, Yn, Xc, Yc
            Rc, Rn, Rbc, Rbn = Rn, Rc, Rbn, Rbc
            k += 1
            yield
        res["R"] = (Rc, Rbc, Rn, Rbn)

    def alloc_mix_weights(l):
        S.push()
        WIN = S.sbuf("WIN", [128, 8, PW], BF16)
        WOUT = S.sbuf("WOUT", [128, 8, D], BF16)
        win_r = w_in[l].rearrange("(c p) n -> p c n", p=128)
        wout_r = w_out[l].rearrange("(c p) n -> p c n", p=128)
        for kc in range(8):
            S.dma("pool", WIN[:, kc, :], win_r[:, kc, :])
        for kc in range(8):
            S.dma("pool", WOUT[:, kc, :], wout_r[:, kc, :])
        return WIN, WOUT

    def phase_a(l, WIN, WOUT):
        S.push()
        NEXPA = S.sbuf("NEXPA", [128, 4])
        S.act(NEXPA[:, :], GP[:, l, 1, 0:4], AF.Exp)
        S.ts("dve", NEXPA[:, :], NEXPA[:, :], -1.0, ALU.mult)
        if do_prompt:
            prompt_part(l, WIN, WOUT, NEXPA)
        if do_sample:
            sample_part(l, WIN, WOUT, NEXPA)
        S.pop()
        S.pop()

    def prompt_part(l, WIN, WOUT, NEXPA):
        S.push()
        C = 128
        MASKB = S.sbuf("MASKB", [128, 2, 256], BF16)
        S.copy("dve", MASKB[:, 0, :], con("swa_p0"))
        S.copy("dve", MASKB[:, 1, :], con("swa_p"))
        XBts = [S.sbuf("XBt%d" % i, [128, 8, 128], BF16) for i in range(2)]
        CV = [S.sbuf("CV%d" % i, [128, 4, 131]) for i in range(3)]
        (Tq, Tk, Tv, ZS, EB, DT, DTB, R0, R1, OT, SST) = [S.sbuf("T%d" % i, [128, 512]) for i in range(11)]
        (TvB, QN, KN, KBG, KDEC, VB, QG, PT, BM, AM, XA, YA, Rb0, Rb1, NWT, VNEW, SSb) = \
            [S.sbuf("B%d" % i, [128, 512], BF16) for i in range(17)]
        QKV = [Tq, Tk, Tv]
        G = {}
        for nm, w in (("BETA", 4), ("G1", 4), ("G", 4), ("LNB", 4), ("GCC", 8), ("EGC", 4), ("EDL", 4), ("BEG", 4),
                      ("GL", 4), ("G2", 8)):
            G[nm] = S.sbuf(nm, [128, w])
        G["GT"] = S.sbuf("GT", [4, 256])
        G["BD"] = S.sbuf("BD", [4, 1024])
        QB = S.sbuf("QB", [128, 4, 128], BF16)
        KD = [S.sbuf("KD%d" % i, [128, 2, 128], BF16) for i in range(2)]
        VD = [S.sbuf("VD%d" % i, [128, 128], BF16) for i in range(2)]
        PUN = S.sbuf("PUN", [128, 4, 256], BF16)
        PTT = S.sbuf("PTT", [128, 4, 2, 128], BF16)
        ST = {}
        for nm in ("RM", "NEGM", "RSUM", "SK", "RINV"):
            ST[nm] = S.sbuf(nm, [128, 8])
        S.memset("dve", SST[:, :], 0.0)
        S.memset("dve", SSb[:, :], 0.0)
        for i in range(3):
            S.memset("pool", CV[i][:, :, 0:3], 0.0)
        for i in range(2):
            S.memset("pool", KD[i][:, :, :], 0.0)
            S.memset("pool", VD[i][:, :], 0.0)

        MIXA = S.sbuf("MIXA", [128, 4, 128], BF16)
        MIXB = S.sbuf("MIXB", [128, 4, 128], BF16)
        KVO = S.sbuf("KVO", [128, 256])

        for blk in range(NP // C):
            t0 = blk * C
            cur, prv = blk % 2, (blk + 1) % 2
            last_blk = blk == NP // C - 1
            XBt = XBts[blk % 2]
            if blk == 0:
                S.copy("act", XBt[:, :, :], X[:, :, t0:t0 + C])

            def proj_fm(dst, col0, M=128):
                for kc in range(8):
                    S.mm(dst, WIN[:, kc, col0:col0 + M], XBt[:, kc, :], start=(kc == 0), stop=(kc == 7))

            def chain_a():
                for grp in range(3):
                    ps = nps()
                    for h in range(4):
                        proj_fm(ps[:, h * 128:(h + 1) * 128], grp * 512 + h * 128)
                    S.copy("act", CV[grp][:, :, 3:131], v4(ps))
                CT = (EB, DT, DTB)
                cwb = lambda grp, j: CW[:, l, grp * 4:(grp + 1) * 4, j:j + 1].bc([128, 4, 128])
                for grp in range(3):
                    S.tt("dve", v4(QKV[grp]), CV[grp][:, :, 0:128], cwb(grp, 0), ALU.mult)
                for j in range(1, 4):
                    for grp in range(3):
                        S.tt("dve", v4(CT[grp]), CV[grp][:, :, j:j + 128], cwb(grp, j), ALU.mult)
                    for grp in range(3):
                        S.tt("dve", v4(QKV[grp]), v4(QKV[grp]), v4(CT[grp]), ALU.add)
                for grp in range(3):
                    S.copy("pool", CV[grp][:, :, 0:3], CV[grp][:, :, 128:131])
                zps = nps()
                for h in range(4):
                    proj_fm(zps[:, h * 128:(h + 1) * 128], C_Z + h * 128)
                for grp in range(3):
                    S.act(TvB[:, :] if grp == 2 else QKV[grp][:, :], QKV[grp][:, :], AF.Silu)
                S.act(ZS[:, :], zps[:, :], AF.Silu)
                yield
                SQ1, RSQ = R0, R1
                for (src, dst, scl) in ((QKV[0], QN, 128.0 ** -0.5), (QKV[1], KN, 1.0)):
                    S.act(SQ1[:, :], src[:, :], AF.Square)
                    sps = nps()
                    S.mm(sps[:, :], con("ones"), SQ1[:, :])
                    rsqrt(RSQ[:, :], sps[:, :], EPS)
                    S.stt("dve", dst[:, :], src[:, :], scl, RSQ[:, :], ALU.mult, ALU.mult)
                    yield
                gates_and_decay(l, C, XBt, WIN, NEXPA, con("tri_p"), con("ones"), G)
                S.act(G["GL"][:, :], G["GCC"][:, 4:8], AF.Exp)
                yield
                decay_mats(C, G, con("neg_incl_p"), con("neg_strict_p"), DT, DTB, EB, Tq, Tk)
                yield
                tp_ = nps()
                tpb = tp_[:, :].bitcast(BF16)
                for h in range(4):
                    S.tr(tpb[:, h * 128:(h + 1) * 128], KN[:, h * 128:(h + 1) * 128], IDB)
                for h in range(4):
                    S.tr(tpb[:, 512 + h * 128:512 + (h + 1) * 128], TvB[:, h * 128:(h + 1) * 128], IDB)
                S.tt("dve", v4(KBG), tpb[:, 0:512].re(H4, h=4), G["BEG"][:, :, None].bc([128, 4, 128]), ALU.mult)
                S.tt("dve", v4(KDEC), tpb[:, 0:512].re(H4, h=4), G["EDL"][:, :, None].bc([128, 4, 128]), ALU.mult)
                S.tt("dve", v4(VB), tpb[:, 512:1024].re(H4, h=4), G["BETA"][:, :, None].bc([128, 4, 128]), ALU.mult)
                S.tt("dve", QG[:, :], QN[:, :], EB[:, :], ALU.mult)
                yield
                kk = nps()
                kq = nps()
                for h in range(4):
                    hc = slice(h * 128, (h + 1) * 128)
                    S.mm(kk[:, hc], KN[:, hc], KN[:, hc])
                    S.mm(kq[:, hc], KN[:, hc], QN[:, hc])
                BMf, AMf = Tq, Tk
                S.tt("dve", BMf[:, :], kk[:, :], DTB[:, :], ALU.mult)
                S.tt("dve", BM[:, :], kk[:, :], DTB[:, :], ALU.mult)
                S.tt("dve", PT[:, :], kq[:, :], DT[:, :], ALU.mult)
                yield
                ap_ = nps()
                for h in range(4):
                    hc = slice(h * 128, (h + 1) * 128)
                    S.tr(ap_[:, hc], BMf[:, hc], con("ident"))
                S.copy("act", AMf[:, :], ap_[:, :])
                S.copy("act", AM[:, :], ap_[:, :])
                yield
                res = {}
                yield from tri_inverse(C, BM, AM, XA, YA, R0, R1, Rb0, Rb1, res)
                Rf, R, Rsp, Rbsp = res["R"]
                e_ps = nps()
                for h in range(4):
                    hc = slice(h * 128, (h + 1) * 128)
                    S.mm(e_ps[:, hc], AMf[:, hc], Rf[:, hc])
                S.tt("dve", v4(Rsp), con("ident")[:, None, :].bc([128, 4, 128]), v4(Rf), ALU.subtract)
                S.tt("dve", XA[:, :], Rsp[:, :], e_ps[:, :], ALU.subtract)
                tb_ = nps()
                tbb = tb_[:, :].bitcast(BF16)
                for h in range(4):
                    hc = slice(h * 128, (h + 1) * 128)
                    S.tr(tbb[:, hc], R[:, hc], IDB)
                S.copy("act", YA[:, :], tbb[:, 0:512])
                yield
                c_ps = nps()
                for h in range(4):
                    hc = slice(h * 128, (h + 1) * 128)
                    S.mm(c_ps[:, hc], YA[:, hc], XA[:, hc])
                S.tt("dve", Rsp[:, :], Rf[:, :], c_ps[:, :], ALU.add)
                S.tt("dve", Rbsp[:, :], Rf[:, :], c_ps[:, :], ALU.add)
                R = Rbsp
                yield
                wps = nps()
                for h in range(4):
                    hc = slice(h * 128, (h + 1) * 128)
                    S.mm(wps[:, hc], KBG[:, hc], R[:, hc])
                S.act(NWT[:, :], wps[:, :], AF.Copy, scale=-1.0)
                yield
                vps = nps()
                for h in range(4):
                    hc = slice(h * 128, (h + 1) * 128)
                    S.mm(vps[:, hc], R[:, hc], VB[:, hc], start=True, stop=False)
                    S.mm(vps[:, hc], NWT[:, hc], SSb[:, hc], start=False, stop=True)
                S.copy("dve", VNEW[:, :], vps[:, :])
                yield
                ops_ = nps()
                for h in range(4):
                    hc = slice(h * 128, (h + 1) * 128)
                    S.mm(ops_[:, hc], SSb[:, hc], QG[:, hc], start=True, stop=False)
                    S.mm(ops_[:, hc], VNEW[:, hc], PT[:, hc], start=False, stop=True)
                S.copy("act", OT[:, :], ops_[:, :])
                sps = nps()
                for h in range(4):
                    hc = slice(h * 128, (h + 1) * 128)
                    S.mm(sps[:, hc], KDEC[:, hc], VNEW[:, hc])
                for h in range(4):
                    hc = slice(h * 128, (h + 1) * 128)
                    S.stt("dve", SST[:, hc], SST[:, hc], G["GL"][:, h:h + 1], sps[:, hc], ALU.mult, ALU.add)
                S.copy("act", SSb[:, :], SST[:, :])
                yield
                SQ1, RSQ = DT, DTB
                S.act(SQ1[:, :], OT[:, :], AF.Square)
                sps = nps()
                S.mm(sps[:, :], con("o128"), SQ1[:, :])
                rsqrt(RSQ[:, :], sps[:, :], EPS)
                S.stt("dve", OT[:, :], OT[:, :], NAW[:, l:l + 1], RSQ[:, :], ALU.mult, ALU.mult)
                S.tt("dve", MIXA[:, :, :], v4(OT), v4(ZS), ALU.mult)

            def chain_b():
                ps = nps()
                for h in range(4):
                    proj_fm(ps[:, h * 128:(h + 1) * 128], C_QB + h * 128)
                S.copy("act", QB[:, :, :], v4(ps))
                yield
                ps = nps()
                for g in range(2):
                    for e in range(2):
                        proj_fm(ps[64 * e:64 * e + 64, g * 128:(g + 1) * 128], C_KB + 64 * g, M=64)
                S.copy("dve", KD[cur][:, :, :], ps[:, 0:256].re("p (g t) -> p g t", g=2))
                ps = nps()
                for kc in range(8):
                    S.mm(ps[:, 0:128], XBt[:, kc, :], WIN[:, kc, C_VB:C_VB + 128], start=(kc == 0), stop=(kc == 7))
                S.copy("dve", VD[cur][:, :], ps[:, 0:128])
                if last_blk:
                    S.copy("dve", KVO[:, 128:256], ps[:, 0:128])
                    S.dma("sp", nv_p[l], KVO[:, 128:256])
                    ps = nps()
                    for kc in range(8):
                        S.mm(ps[:, 0:128], XBt[:, kc, :], WIN[:, kc, C_KB:C_KB + 128], start=(kc == 0), stop=(kc == 7))
                    S.copy("act", KVO[:, 0:128], ps[:, 0:128])
                    S.dma("sp", nk_p[l], KVO[:, 0:128])
                yield
                mk = MASKB[:, 0 if blk == 0 else 1, :]
                for half in range(2):
                    for pbl in range(2):
                        pb = half * 2 + pbl
                        sp_ = nps()
                        for hh in range(2):
                            h = pb * 2 + hh
                            c, e, g = h // 2, h % 2, h // 4
                            pr = slice(64 * e, 64 * e + 64)
                            for kb, kd in ((0, KD[prv]), (1, KD[cur])):
                                cs = slice(hh * 256 + kb * 128, hh * 256 + kb * 128 + 128)
                                S.mm(sp_[:, cs], QB[pr, c, :], kd[pr, g, :], start=True, stop=False)
                                S.mm(sp_[:, cs], IDB, mk[:, kb * 128:(kb + 1) * 128], start=False, stop=True)
                        hs = slice(2 * pb, 2 * pb + 2)
                        S.red("dve", ST["RM"][:, hs], sp_[:, :].re("p (a k) -> p a k", a=2), ALU.max)
                        S.ts("dve", ST["NEGM"][:, hs], ST["RM"][:, hs], 0.125, ALU.mult)
                        S.tt("dve", ST["NEGM"][:, hs], ST["NEGM"][:, hs], GP[:, l, 2, hs], ALU.max)
                        S.ts("dve", ST["NEGM"][:, hs], ST["NEGM"][:, hs], -1.0, ALU.mult)
                        for hh in range(2):
                            h = pb * 2 + hh
                            S.act(PUN[:, pbl * 2 + hh, :], sp_[:, hh * 256:(hh + 1) * 256], AF.Exp,
                                  bias=ST["NEGM"][:, h:h + 1], scale=0.125, accum=ST["RSUM"][:, h:h + 1])
                        yield
                    h4 = slice(half * 4, half * 4 + 4)
                    S.tt("dve", ST["SK"][:, h4], GP[:, l, 2, h4], ST["NEGM"][:, h4], ALU.add)
                    S.act(ST["SK"][:, h4], ST["SK"][:, h4], AF.Exp)
                    S.tt("dve", ST["RINV"][:, h4], ST["RSUM"][:, h4], ST["SK"][:, h4], ALU.add)
                    rv = ST["RINV"][:, h4]
                    S.op("dve", lambda e: e.reciprocal(out=rv.ap, in_=rv.ap), [rv], [rv])
                    S.tt("dve", PUN[:, :, :], PUN[:, :, :], rv[:, :, None].bc([128, 4, 256]), ALU.mult)
                    yield
                    tp = nps()
                    tpb = tp[:, :].bitcast(BF16)
                    for hh in range(4):
                        for kb in range(2):
                            S.tr(tpb[:, (hh * 2 + kb) * 128:(hh * 2 + kb + 1) * 128], PUN[:, hh, kb * 128:(kb + 1) * 128], IDB)
                    S.copy("act", PTT[:, :, :, :], tpb.re("p (h k q) -> p h k q", h=4, k=2))
                    yield
                    op_ = nps()
                    for hh in range(4):
                        h = half * 4 + hh
                        c, e, g = h // 2, h % 2, h // 4
                        dst = op_[64 * e:64 * e + 64, (c % 2) * 128:(c % 2) * 128 + 128]
                        S.mm(dst, VD[prv][:, g * 64:(g + 1) * 64], PTT[:, hh, 0, :], start=True, stop=False)
                        S.mm(dst, VD[cur][:, g * 64:(g + 1) * 64], PTT[:, hh, 1, :], start=False, stop=True)
                    S.copy("act", MIXB[:, 2 * half:2 + 2 * half, :], op_[:, 0:256].re("p (c t) -> p c t", c=2))
                    yield

            ga, gb = chain_a(), chain_b()
            next(ga)
            if not last_blk:
                S.copy("act", XBts[(blk + 1) % 2][:, :, :], X[:, :, t0 + C:t0 + 2 * C])
            for _ in range(int(FLAGS.get("b_lead", 2))):
                next(gb)
            interleave(ga, gb)

            for hb in range(2):
                mps = nps()
                for dcl in range(4):
                    dc = hb * 4 + dcl
                    for kc in range(8):
                        S.mm(mps[:, dcl * 128:(dcl + 1) * 128], WOUT[:, kc, dc * 128:(dc + 1) * 128],
                             (MIXA if kc < 4 else MIXB)[:, kc % 4, :], start=(kc == 0), stop=(kc == 7))
                xv = X[:, hb * 4:(hb + 1) * 4, t0:t0 + C]
                S.stt("dve", xv, xv, ALPHA, v4(mps), ALU.mult, ALU.add)
            interleave(
                layernorm_g(t0, 64, 0, l, Tq[:, :].re("p (c t) -> p c t", c=8), Tk[:, :].re("p (c t) -> p c t", c=8),
                            Tv[:, 0:64]),
                layernorm_g(t0 + 64, 64, 0, l, R0[:, :].re("p (c t) -> p c t", c=8), R1[:, :].re("p (c t) -> p c t", c=8),
                            OT[:, 0:64]))
            if last_blk:
                for cg in range(3):
                    ps = nps()
                    for kc in range(8):
                        S.mm(ps[:, :], XBt[:, kc, :], WIN[:, kc, cg * 512:(cg + 1) * 512], start=(kc == 0), stop=(kc == 7))
                    S.copy("act", QKV[cg][64:128, :], ps[64:128, :])
                    S.dma("sp", nsc_p[l][:, cg * 512:(cg + 1) * 512], QKV[cg][125:128, :])
            if dbg_hook is not None:
                dbg_hook(dict(locals(), dump=dump, S=S))
        S.dma("sp", nsd_p[l].rearrange("h d v -> d h v"), v4(SST))
        S.pop()

    def sample_part(l, WIN, WOUT, NEXPA):
        S.push()
        C = NS
        T0 = NP
        XBs = S.sbuf("XBs", [128, 8, C], BF16)
        MIX = S.sbuf("MIXs", [128, 8, C], BF16)
        QB = S.sbuf("QBs", [128, 4, C], BF16)
        KDn = S.sbuf("KDs", [128, 2, C], BF16)
        VDn = S.sbuf("VDs", [128, 128], BF16)
        S.memset("pool", VDn[:, :], 0.0)
        SMB = S.sbuf("SMB", [128, 16, C], BF16)
        S.dma("pool", SMB[:, :, :].re("p s t -> p (s t)"), smb_in)
        S.copy("act", XBs[:, :, :], X[:, :, T0:T0 + C])
        W4 = 4 * C

        def proj_fm(dst, col0, M=128):
            for kc in range(8):
                S.mm(dst, WIN[:, kc, col0:col0 + M], XBs[:, kc, :], start=(kc == 0), stop=(kc == 7))

        def proj_tm(dst, col0, n):
            for kc in range(8):
                S.mm(dst, XBs[:, kc, :], WIN[:, kc, col0:col0 + n], start=(kc == 0), stop=(kc == 7))

        S.push()
        R0 = S.sbuf("sR0", [128, W4])
        R1 = S.sbuf("sR1", [128, W4])
        Rb0 = S.sbuf("sRb0", [C, W4], BF16)
        Rb1 = S.sbuf("sRb1", [C, W4], BF16)
        PT = S.sbuf("sPT", [C, W4], BF16)
        QG = S.sbuf("sQG", [128, W4], BF16)
        ZS = S.sbuf("sZS", [128, W4])
        OT = S.sbuf("sOT", [128, W4])
        NWT = S.sbuf("sNWT", [128, W4], BF16)
        KBG = S.sbuf("sKBG", [C, 512], BF16)
        KDEC = S.sbuf("sKDEC", [C, 512], BF16)
        VB = S.sbuf("sVB", [C, 512], BF16)
        VNEW = S.sbuf("sVNEW", [C, 512], BF16)
        GLs = S.sbuf("sGL", [128, 64])
        SC1, SC2 = R0, R1
        G = {}
        for nm, w in (("BETA", 4), ("G1", 4), ("G", 4), ("LNB", 4), ("GCC", 8), ("EGC", 4), ("EDL", 4), ("BEG", 4),
                      ("G2", 8)):
            G[nm] = S.sbuf("s" + nm, [128, w])
        G["GT"] = S.sbuf("sGT", [4, 2 * C])
        G["BD"] = S.sbuf("sBD", [4, 8 * C])

        S.push()
        KC = S.sbuf("sKC", [128, 16, 2, 128], BF16)
        VC = S.sbuf("sVC", [128, 16, 128], BF16)
        QM = S.sbuf("sQM", [128, 4, 16, C], BF16)
        MSK = S.sbuf("sMSK", [128, 192], BF16)
        PUN = S.sbuf("sPUN", [C, 4, 192], BF16)
        PTC = S.sbuf("sPTC", [128, 4, C], BF16)
        PTN = S.sbuf("sPTN", [128, 4, C], BF16)
        S.memset("pool", PTN[:, :, :], 0.0)
        PTCM = S.sbuf("sPTCM", [128, 16, C], BF16)
        ST = {}
        for nm in ("RM", "NEGM", "RSUM", "SK", "RINV"):
            ST[nm] = S.sbuf("s" + nm, [C, 8])
        if FLAGS.get("no_kvc"):
            S.memset("pool", KC[:, :, :, :], 0.0)
            S.memset("pool", VC[:, :, :], 0.0)
        else:
            for sg in range(4):
                S.dma("pool", KC[:, sg * 4:(sg + 1) * 4, :, :].re("p s g k -> p (s g k)"),
                      ckT_in[l][:, sg * 4:(sg + 1) * 4, :, :].rearrange("p s g k -> p (s g k)"))
                S.dma("pool", VC[:, sg * 4:(sg + 1) * 4, :].re("p s k -> p (s k)"),
                      cvD_in[l][:, sg * 4:(sg + 1) * 4, :].rearrange("p s k -> p (s k)"))
        S.copy("dve", MSK[:, :], con("swa_s"))
        S.push()
        Tq, Tk, Tv, EB, DT, DTB = [S.sbuf("sT%d" % i, [128, W4]) for i in range(6)]
        TvB, QN, KN, BM, AM, XA, YA = [S.sbuf("sB%d" % i, [128, W4], BF16) for i in range(7)]
        QKV = [Tq, Tk, Tv]
        CVs = [S.sbuf("sCV%d" % i, [128, 4, 16, 7]) for i in range(3)]
        STG = S.sbuf("sSTG", [128, 12, 16, 3])
        OUTS = S.sbuf("sOUTS", [C, 1536])
        KVO = S.sbuf("sKVO", [C, 256])
        GM = S.sbuf("sGM", [C, 16, 4])
        res = {}

        def gen_ia():
            S.dma("sp", STG[:, :, :, :], scT_in[l])
            for grp in range(3):
                ps = nps()
                for h in range(4):
                    proj_fm(ps[:, h * C:(h + 1) * C], grp * 512 + h * 128)
                cv = CVs[grp]
                S.copy("dve", cv[:, :, :, 0:3], STG[:, grp * 4:(grp + 1) * 4, :, :])
                S.copy("act", cv[:, :, :, 3:7], ps[:, 0:W4].re("p (h s t) -> p h s t", h=4, s=16))
                for h in range(4):
                    ch = grp * 4 + h
                    tq = QKV[grp][:, h * C:(h + 1) * C].re("p (s t) -> p s t", s=16)
                    S.ts("dve", tq, cv[:, h, :, 0:4], CW[:, l, ch, 0:1], ALU.mult)
                    for j in range(1, 4):
                        S.stt("dve", tq, cv[:, h, :, j:j + 4], CW[:, l, ch, j:j + 1], tq, ALU.mult, ALU.add)
                S.act(TvB[:, :] if grp == 2 else QKV[grp][:, :], QKV[grp][:, :], AF.Silu)
                yield
            for cg in range(3):
                ps = nps()
                proj_tm(ps[0:C, :], cg * 512, 512)
                S.copy("act", OUTS[:, cg * 512:(cg + 1) * 512], ps[0:C, :])
            for r in range(1, 4):
                S.dma("sp", nsc_s[l][:, r - 1, :], OUTS[r:C:4, :])
            ps = nps()
            for h in range(4):
                proj_fm(ps[:, h * C:(h + 1) * C], C_Z + h * 128)
            S.act(ZS[:, :], ps[:, 0:W4], AF.Silu)
            yield
            SQ1, RSQ = DT, DTB
            for (src, dst, scl) in ((QKV[0], QN, 128.0 ** -0.5), (QKV[1], KN, 1.0)):
                S.act(SQ1[:, :], src[:, :], AF.Square)
                sps = nps()
                S.mm(sps[:, 0:W4], con("ones"), SQ1[:, :])
                rsqrt(RSQ[:, :], sps[:, 0:W4], EPS)
                S.stt("dve", dst[:, :], src[:, :], scl, RSQ[:, :], ALU.mult, ALU.mult)
                yield
            gates_and_decay(l, C, XBs, WIN, NEXPA, con("tri_s"), con("blk_s"), G)
            yield
            S.tt("dve", GM[:, :, :], G["G"][0:C, None, :].bc([C, 16, 4]), con("sm")[0:C, :, None].bc([C, 16, 4]), ALU.mult)
            gps_ = nps()
            S.mm(gps_[:, 0:64], con("ones")[0:C, :], GM[:, :, :].re("p s h -> p (s h)"))
            S.act(GLs[:, :], gps_[:, 0:64], AF.Exp)
            yield
            decay_mats(C, G, con("neg_incl_s"), con("neg_strict_s"), DT, DTB, EB, Tq, Tk)
            yield
            tp_ = nps()
            tpb = tp_[:, :].bitcast(BF16)
            for h in range(4):
                S.tr(tpb[0:C, h * 128:(h + 1) * 128], KN[:, h * C:(h + 1) * C], IDB)
            for h in range(4):
                S.tr(tpb[0:C, 512 + h * 128:512 + (h + 1) * 128], TvB[:, h * C:(h + 1) * C], IDB)
            S.tt("dve", v4(KBG), tpb[0:C, 0:512].re(H4, h=4), G["BEG"][0:C, :, None].bc([C, 4, 128]), ALU.mult)
            S.tt("dve", v4(KDEC), tpb[0:C, 0:512].re(H4, h=4), G["EDL"][0:C, :, None].bc([C, 4, 128]), ALU.mult)
            S.tt("dve", v4(VB), tpb[0:C, 512:1024].re(H4, h=4), G["BETA"][0:C, :, None].bc([C, 4, 128]), ALU.mult)
            S.tt("dve", QG[:, :], QN[:, :], EB[:, :], ALU.mult)
            yield
            kk = nps()
            kq = nps()
            for h in range(4):
                hc = slice(h * C, (h + 1) * C)
                S.mm(kk[0:C, hc], KN[:, hc], KN[:, hc])
                S.mm(kq[0:C, hc], KN[:, hc], QN[:, hc])
            S.tt("dve", BM[0:C, :], kk[0:C, 0:W4], DTB[0:C, :], ALU.mult)
            S.tt("dve", PT[:, :], kq[0:C, 0:W4], DT[0:C, :], ALU.mult)
            yield
            ap_ = nps()
            apb = ap_[:, :].bitcast(BF16)
            for h in range(4):
                hc = slice(h * C, (h + 1) * C)
                S.tr(apb[0:C, hc], BM[0:C, hc], IDB[0:C, 0:C])
            S.copy("act", AM[0:C, :], apb[0:C, 0:W4])
            yield
            yield from tri_inverse(C, BM, AM, XA, YA, R0, R1, Rb0, Rb1, res)
            R = res["R"][1]
            wps = nps()
            for h in range(4):
                S.mm(wps[:, h * C:(h + 1) * C], KBG[:, h * 128:(h + 1) * 128], R[:, h * C:(h + 1) * C])
            S.act(NWT[:, :], wps[:, 0:W4], AF.Copy, scale=-1.0)

        def gen_ii():
            ps = nps()
            for h in range(4):
                proj_fm(ps[:, h * C:(h + 1) * C], C_QB + h * 128)
            S.copy("act", QB[:, :, :], ps[:, 0:W4].re("p (h t) -> p h t", h=4))
            ps = nps()
            for g in range(2):
                for e in range(2):
                    proj_fm(ps[64 * e:64 * e + 64, g * C:(g + 1) * C], C_KB + 64 * g, M=64)
            S.copy("dve", KDn[:, :, :], ps[:, 0:2 * C].re("p (g t) -> p g t", g=2))
            yield
            ps = nps()
            proj_tm(ps[0:C, 0:128], C_KB, 128)
            proj_tm(ps[0:C, 128:256], C_VB, 128)
            S.copy("dve", VDn[0:C, :], ps[0:C, 128:256])
            S.copy("act", KVO[:, :], ps[0:C, 0:256])
            for t in range(4):
                S.dma("sp", nk_s[l][:, 124 + t, :], KVO[t:C:4, 0:128])
                S.dma("sp", nv_s[l][:, 124 + t, :], KVO[t:C:4, 128:256])
            if not FLAGS.get("no_d2d"):
                S.dma("sp", nk_s[l][:, 0:124, :], ck_raw[l][:, 4:128, :])
                S.dma("sp", nv_s[l][:, 0:124, :], cv_raw[l][:, 4:128, :])
            yield
            for c in range(4):
                S.tt("dve", QM[:, c, :, :], QB[:, c, None, :].bc([128, 16, C]), SMB[:, :, :], ALU.mult)
            yield
            LVL = int(FLAGS.get("swa_lvl", 9))
            for half in range(2):
                if LVL < 1:
                    continue
                for pbl in range(2):
                    pb = half * 2 + pbl
                    sp_ = nps()
                    for hh in range(2):
                        h = pb * 2 + hh
                        c, e, g = h // 2, h % 2, h // 4
                        pr = slice(64 * e, 64 * e + 64)
                        cs = slice(hh * 192, hh * 192 + 128)
                        for s_ in range(16):
                            S.mm(sp_[0:C, cs], QM[pr, c, s_, :], KC[pr, s_, g, :], start=(s_ == 0), stop=False)
                        S.mm(sp_[0:C, cs], IDB[:, 0:C], MSK[:, 0:128], start=False, stop=True)
                        cs2 = slice(hh * 192 + 128, hh * 192 + 192)
                        S.mm(sp_[0:C, cs2], QB[pr, c, :], KDn[pr, g, :], start=True, stop=False)
                        S.mm(sp_[0:C, cs2], IDB[:, 0:C], MSK[:, 128:192], start=False, stop=True)
                    hs = slice(2 * pb, 2 * pb + 2)
                    if LVL < 2:
                        continue
                    S.red("dve", ST["RM"][:, hs], sp_[0:C, 0:384].re("p (a k) -> p a k", a=2), ALU.max)
                    S.ts("dve", ST["NEGM"][:, hs], ST["RM"][:, hs], 0.125, ALU.mult)
                    S.tt("dve", ST["NEGM"][:, hs], ST["NEGM"][:, hs], GP[0:C, l, 2, hs], ALU.max)
                    S.ts("dve", ST["NEGM"][:, hs], ST["NEGM"][:, hs], -1.0, ALU.mult)
                    for hh in range(2):
                        h = pb * 2 + hh
                        S.act(PUN[:, pbl * 2 + hh, :], sp_[0:C, hh * 192:(hh + 1) * 192], AF.Exp,
                              bias=ST["NEGM"][:, h:h + 1], scale=0.125, accum=ST["RSUM"][:, h:h + 1])
                    yield
                if LVL < 2:
                    continue
                h4 = slice(half * 4, half * 4 + 4)
                S.tt("dve", ST["SK"][:, h4], GP[0:C, l, 2, h4], ST["NEGM"][:, h4], ALU.add)
                S.act(ST["SK"][:, h4], ST["SK"][:, h4], AF.Exp)
                S.tt("dve", ST["RINV"][:, h4], ST["RSUM"][:, h4], ST["SK"][:, h4], ALU.add)
                rv = ST["RINV"][:, h4]
                S.op("dve", lambda e: e.reciprocal(out=rv.ap, in_=rv.ap), [rv], [rv])
                S.tt("dve", PUN[:, :, :], PUN[:, :, :], rv[:, :, None].bc([C, 4, 192]), ALU.mult)
                yield
                if LVL < 3:
                    continue
                tp = nps()
                tpb = tp[:, :].bitcast(BF16)
                for hh in range(4):
                    S.tr(tpb[:, hh * C:(hh + 1) * C], PUN[:, hh, 0:128], IDB[0:C, 0:C])
                    S.tr(tpb[0:C, 256 + hh * C:256 + (hh + 1) * C], PUN[:, hh, 128:192], IDB[0:C, 0:C])
                S.copy("act", PTC[:, :, :], tpb[:, 0:256].re("p (h q) -> p h q", h=4))
                S.copy("dve", PTN[0:C, :, :], tpb[0:C, 256:512].re("p (h q) -> p h q", h=4))
                yield
                if LVL < 4:
                    continue
                op_ = nps()
                for hh in range(4):
                    h = half * 4 + hh
                    c, e, g = h // 2, h % 2, h // 4
                    pr = slice(64 * e, 64 * e + 64)
                    base = (c % 2) * C
                    S.tt("dve", PTCM[:, :, :], PTC[:, hh, None, :].bc([128, 16, C]), SMB[:, :, :], ALU.mult)
                    for s_ in range(16):
                        S.mm(op_[pr, base:base + C], VC[:, s_, g * 64:(g + 1) * 64], PTCM[:, s_, :],
                             start=(s_ == 0), stop=False)
                    S.mm(op_[pr, base:base + C], VDn[:, g * 64:(g + 1) * 64], PTN[:, hh, :], start=False, stop=True)
                S.copy("act", MIX[:, 4 + 2 * half:6 + 2 * half, :], op_[:, 0:2 * C].re("p (c t) -> p c t", c=2))

        interleave(gen_ia(), gen_ii())
        R = res["R"][1]
        S.pop()
        S.pop()

        S.push()
        if FLAGS.get("stop") == "state":
            S.pop(); S.pop(); S.pop(); return
        SS = S.sbuf("sSS", [128, 16, 4, 128])
        SSB = S.sbuf("sSSB", [128, 16, 4, 128], BF16)
        NWTM = S.sbuf("sNWTM", [128, 16, C], BF16)
        VNM = S.sbuf("sVNM", [C, 8, 128], BF16)
        sd_r = sd_in[l].rearrange("s h d v -> d s h v")
        for sg in range(4):
            S.dma("sp", SS[:, sg * 4:(sg + 1) * 4, :, :], sd_r[:, sg * 4:(sg + 1) * 4, :, :])
            S.copy("dve" if sg % 2 else "act", SSB[:, sg * 4:(sg + 1) * 4, :, :], SS[:, sg * 4:(sg + 1) * 4, :, :])
        vps = nps()
        for h in range(4):
            S.tt("dve", NWTM[:, :, :], NWT[:, None, h * C:(h + 1) * C].bc([128, 16, C]), SMB[:, :, :], ALU.mult)
            hv = vps[0:C, h * 128:(h + 1) * 128]
            S.mm(hv, R[:, h * C:(h + 1) * C], VB[:, h * 128:(h + 1) * 128], start=True, stop=False)
            for s_ in range(16):
                S.mm(hv, NWTM[:, s_, :], SSB[:, s_, h, :], start=False, stop=(s_ == 15))
        S.copy("dve", VNEW[:, :], vps[0:C, :])
        ops_ = nps()
        for h in range(4):
            S.tt("dve", NWTM[:, :, :], QG[:, None, h * C:(h + 1) * C].bc([128, 16, C]), SMB[:, :, :], ALU.mult)
            for s_ in range(16):
                S.mm(ops_[:, h * C:(h + 1) * C], SSB[:, s_, h, :], NWTM[:, s_, :], start=(s_ == 0), stop=False)
            S.mm(ops_[:, h * C:(h + 1) * C], VNEW[:, h * 128:(h + 1) * 128], PT[:, h * C:(h + 1) * C],
                 start=False, stop=True)
        S.copy("act", OT[:, :], ops_[:, 0:W4])
        for h in range(4):
            for sg in range(4):
                if sg % 2 == 0:
                    S.tt("dve", VNM[:, :, :], VNEW[:, None, h * 128:(h + 1) * 128].bc([C, 8, 128]),
                         con("sm")[0:C, sg * 4:sg * 4 + 8, None].bc([C, 8, 128]), ALU.mult)
                sn = nps()
                for sl in range(4):
                    s_ = sg * 4 + sl
                    S.mm(sn[:, sl * 128:(sl + 1) * 128], KDEC[:, h * 128:(h + 1) * 128], VNM[:, s_ % 8, :])
                for sl in range(4):
                    s_ = sg * 4 + sl
                    S.stt("dve", SS[:, s_, h, :], SS[:, s_, h, :], GLs[:, s_ * 4 + h:s_ * 4 + h + 1],
                          sn[:, sl * 128:(sl + 1) * 128], ALU.mult, ALU.add)
        nsd_r = nsd_s[l].rearrange("s h d v -> d s h v")
        for sg in range(4):
            S.dma("sp", nsd_r[:, sg * 4:(sg + 1) * 4, :, :], SS[:, sg * 4:(sg + 1) * 4, :, :])
        S.pop()
        SQ1, RSQ = SC1, SC2
        S.act(SQ1[:, :], OT[:, :], AF.Square)
        sps = nps()
        S.mm(sps[:, 0:W4], con("o128"), SQ1[:, :])
        rsqrt(RSQ[:, :], sps[:, 0:W4], EPS)
        S.stt("dve", OT[:, :], OT[:, :], NAW[:, l:l + 1], RSQ[:, :], ALU.mult, ALU.mult)
        S.tt("dve", MIX[:, 0:4, :], OT[:, :].re(H4, h=4), ZS[:, :].re(H4, h=4), ALU.mult)
        S.pop()

        if FLAGS.get("stop") == "out":
            S.pop(); return
        S.push()
        DDs = S.sbuf("sDD", [128, 8, C])
        SQs = S.sbuf("sSQ", [128, 8, C])
        RSs = S.sbuf("sRS", [128, C])
        for hb in range(2):
            mps = nps()
            for dcl in range(4):
                dc = hb * 4 + dcl
                for kc in range(8):
                    S.mm(mps[:, dcl * C:(dcl + 1) * C], WOUT[:, kc, dc * 128:(dc + 1) * 128], MIX[:, kc, :],
                         start=(kc == 0), stop=(kc == 7))
            xv = X[:, hb * 4:(hb + 1) * 4, T0:T0 + C]
            S.stt("dve", xv, xv, ALPHA, mps[:, 0:W4].re(H4, h=4), ALU.mult, ALU.add)
        layernorm(T0, C, 0, l, DDs, SQs, RSs)
        S.pop()
        S.pop()

    early_store = [False]

    def phase_b(l):
        S.push()
        LNW = 128
        XB = S.sbuf("XB", [128, 8, 1088], BF16)
        ACTB = S.sbuf("ACTB", [128, 11, 1088], BF16)
        WO = [S.sbuf("WO%d" % i, [128, 11, 1024], BF16) for i in range(2)]
        WG = [S.sbuf("WG%d" % i, [128, 8, 256], BF16) for i in range(2)]
        WU = [S.sbuf("WU%d" % i, [128, 8, 256], BF16) for i in range(2)]
        SIL = [S.sbuf("SIL%d" % i, [128, 512]) for i in range(2)]
        DD = [S.sbuf("DD%d" % i, [128, 8, LNW]) for i in range(2)]
        SQ = [S.sbuf("SQ%d" % i, [128, 8, LNW]) for i in range(2)]
        RS = [S.sbuf("RS%d" % i, [128, LNW]) for i in range(2)]
        LT = {"DD": DD, "SQ": SQ, "RS": RS}
        wfi_r = w_fi[l].rearrange("(c p) n -> p c n", p=128)
        wfo_r = w_fo[l].rearrange("(c p) n -> p c n", p=128)
        wcnt = [0]

        def ffn_pass(t0, cgs):
            ntok = sum(cgs)
            for c in range(8):
                S.copy("act" if c % 2 else "dve", XB[:, c, 0:ntok], X[:, c, t0:t0 + ntok])
            for half in range(2):
                wo = WO[half]
                S.dma("pool", wo[:, :, :], wfo_r[:, half * 11:(half + 1) * 11, :])
                for fp in range(6):
                    nf = 2 if fp < 5 else 1
                    fc0 = half * 11 + fp * 2
                    wg, wu = WG[wcnt[0] % 2], WU[wcnt[0] % 2]
                    wcnt[0] += 1
                    S.dma("pool", wg[:, :, 0:nf * 128], wfi_r[:, :, fc0 * 128:(fc0 + nf) * 128])
                    S.dma("pool", wu[:, :, 0:nf * 128], wfi_r[:, :, FF + fc0 * 128:FF + (fc0 + nf) * 128])
                    for f in range(nf):
                        fl = fp * 2 + f
                        cs = 0
                        for ci, cw in enumerate(cgs):
                            gp_, up_ = nps(), nps()
                            for kc in range(8):
                                S.mm(gp_[:, 0:cw], wg[:, kc, f * 128:(f + 1) * 128], XB[:, kc, cs:cs + cw],
                                     start=(kc == 0), stop=(kc == 7))
                            for kc in range(8):
                                S.mm(up_[:, 0:cw], wu[:, kc, f * 128:(f + 1) * 128], XB[:, kc, cs:cs + cw],
                                     start=(kc == 0), stop=(kc == 7))
                            sl = SIL[(fl + ci) % 2]
                            S.act(sl[:, 0:cw], gp_[:, 0:cw], AF.Silu)
                            S.tt("dve", ACTB[:, fl, cs:cs + cw], sl[:, 0:cw], up_[:, 0:cw], ALU.mult)
                            cs += cw
                        yield
                for dc in range(8):
                    cs = 0
                    for cw in cgs:
                        yp = nps()
                        for fl in range(11):
                            S.mm(yp[:, 0:cw], wo[:, fl, dc * 128:(dc + 1) * 128], ACTB[:, fl, cs:cs + cw],
                                 start=(fl == 0), stop=(fl == 10))
                        xv = X[:, dc, t0 + cs:t0 + cs + cw]
                        if half == 0:
                            S.stt("dve", xv, xv, ALPHA, yp[:, 0:cw], ALU.mult, ALU.add)
                        else:
                            S.tt("dve", xv, xv, yp[:, 0:cw], ALU.add)
                        cs += cw
                    yield

        def ln_pass(t0, ntok):
            pieces = []
            c = 0
            while c < ntok:
                n = min(LNW, ntok - c)
                pieces.append((t0 + c, n))
                c += n
            for k in range(0, len(pieces), 2):
                gens = [layernorm_g(pc[0], pc[1], 2, l, LT["DD"][j], LT["SQ"][j], LT["RS"][j],
                                    sub_eng="pool" if j else "dve", affine_eng="dve")
                        for j, pc in enumerate(pieces[k:k + 2])]
                while gens:
                    for g_ in list(gens):
                        try:
                            next(g_)
                        except StopIteration:
                            gens.remove(g_)
                    yield

        for _ in ffn_pass(0, (512, 512)):
            pass
        interleave(ffn_pass(1024, (512, 512, 64)), ln_pass(0, 1024))
        if l == n_layers - 1:
            yT_e = yT.rearrange("(c p) t -> p c t", p=128)
            for c in range(8):
                S.dma("sp", yT_e[:, c, 0:1024], X[:, c, 0:1024])
            early_store[0] = True
        S.pop()
        nxt = alloc_mix_weights(l + 1) if (l + 1 < n_layers and do_a) else None
        S.push()
        DD = [S.sbuf("DDt%d" % i, [128, 8, LNW]) for i in range(2)]
        SQ = [S.sbuf("SQt%d" % i, [128, 8, LNW]) for i in range(2)]
        RS = [S.sbuf("RSt%d" % i, [128, LNW]) for i in range(2)]
        LT.update(DD=DD, SQ=SQ, RS=RS)
        for _ in ln_pass(1024, 1088):
            pass
        S.pop()
        return nxt

    wts = alloc_mix_weights(0) if do_a else None
    for l in range(n_layers):
        if do_a:
            phase_a(l, *wts)
            wts = None
        if do_b:
            wts = phase_b(l)
        elif do_a and l + 1 < n_layers:
            wts = alloc_mix_weights(l + 1)

    yT_r = yT.rearrange("(c p) t -> p c t", p=128)
    c_lo = 1024 if early_store[0] else 0
    for c in range(8):
        S.dma("sp", yT_r[:, c, c_lo:T], X[:, c, c_lo:T])
    S.barrier()
    S.close()
    return nc


def make_in_maps(inp):
    f = np.float32
    x_prompt, x_sample = inp["x_prompt"], inp["x_sample"]
    conv_wT = np.ascontiguousarray(inp["conv_w"].reshape(L, 4, 12, 128).transpose(3, 0, 2, 1)).astype(f)
    lnp = np.stack([inp["ln1_g"], inp["ln1_b"], inp["ln2_g"], inp["ln2_b"]], axis=1)
    lnp = np.ascontiguousarray(lnp.reshape(L, 4, 8, 128).transpose(3, 0, 1, 2)).astype(f)
    gp = np.zeros((128, L, 3, 8), f)
    gp[:, :, 0, 0:4] = inp["dt_bias"][None]
    gp[:, :, 1, 0:4] = inp["a_log"][None]
    gp[:, :, 2, 0:8] = inp["sinks"][None]
    naw = np.ascontiguousarray(inp["norm_a_w"].T).astype(f)
    shared = {
        "w_in": inp["w_in"], "w_out": inp["w_out"], "w_ffn_in": inp["w_ffn_in"], "w_ffn_out": inp["w_ffn_out"],
        "conv_wT": conv_wT, "lnp": lnp, "gp": gp, "naw": naw, "consts": CONSTS, "smb": SMB_CONST,
    }
    maps = []
    for c in range(NCORES):
        sl = slice(NSQ * c, NSQ * (c + 1))
        xt = np.concatenate([x_prompt[c], x_sample[sl].reshape(NS, D)], axis=0).T
        sc = inp["state_conv"][:, sl]
        scT = sc.reshape(L, NSQ, 3, 12, 128).transpose(0, 4, 3, 1, 2)
        ck = inp["cache_swa_k"][:, sl]
        ckT = np.tile(ck.transpose(0, 4, 1, 3, 2), (1, 2, 1, 1, 1))
        cv = inp["cache_swa_v"][:, sl]
        cvD = cv.transpose(0, 2, 1, 3, 4).reshape(L, 128, NSQ, 128)
        m = dict(shared)
        m.update({
            "xT": np.ascontiguousarray(xt, f),
            "sd": np.ascontiguousarray(inp["state_delta"][:, sl], f),
            "scT": np.ascontiguousarray(scT, f),
            "ckT": np.ascontiguousarray(ckT, f),
            "cvD": np.ascontiguousarray(cvD, f),
            "ck_raw": np.ascontiguousarray(ck.reshape(L, NSQ, 128, 128), f),
            "cv_raw": np.ascontiguousarray(cv.reshape(L, NSQ, 128, 128), f),
        })
        maps.append(m)
    return maps


_NC_CACHE = {}


def kernel(**inputs):
    inp = {k: np.asarray(v) for k, v in inputs.items()}
    if "nc" not in _NC_CACHE:
        _NC_CACHE["nc"] = build()
    nc = _NC_CACHE["nc"]
    maps = make_in_maps(inp)
    res = run_bass_kernel_spmd(nc, maps, core_ids=list(range(NCORES))).results
    f = np.float32
    y_p = np.stack([res[c]["yT"][:, :NP].T for c in range(NCORES)]).astype(f)
    y_s = np.concatenate([res[c]["yT"][:, NP:].T.reshape(NSQ, 4, D) for c in range(NCORES)]).astype(f)
    nsd_p = np.stack([res[c]["nsd_p"] for c in range(NCORES)], axis=1).astype(f)
    nsc_p = np.stack([res[c]["nsc_p"] for c in range(NCORES)], axis=1).astype(f)
    nk_p = np.stack([res[c]["nk_p"].reshape(L, 128, 2, 64) for c in range(NCORES)], axis=1).astype(f)
    nv_p = np.stack([res[c]["nv_p"].reshape(L, 128, 2, 64) for c in range(NCORES)], axis=1).astype(f)
    nsd_s = np.concatenate([res[c]["nsd_s"] for c in range(NCORES)], axis=1).astype(f)
    nsc_s = np.concatenate([res[c]["nsc_s"] for c in range(NCORES)], axis=1).astype(f)
    nk_s = np.concatenate([res[c]["nk_s"].reshape(L, NSQ, 128, 2, 64) for c in range(NCORES)], axis=1).astype(f)
    nv_s = np.concatenate([res[c]["nv_s"].reshape(L, NSQ, 128, 2, 64) for c in range(NCORES)], axis=1).astype(f)
    return (y_p, y_s, nsd_p, nsc_p, nk_p, nv_p, nsd_s, nsc_s, nk_s, nv_s)
```

```python
import numpy as np
import concourse.bass as bass
import concourse.mybir as mybir
from concourse.bass_utils import run_bass_kernel_spmd

F32 = mybir.dt.float32
BF16 = mybir.dt.bfloat16
AF = mybir.ActivationFunctionType
ALU = mybir.AluOpType
AX = mybir.AxisListType

NCORES = 8
D = 1024
NP = 2048
NSQ = 16
NS = 64
T = NP + NS
L = 4
PW = 2824
FF = 2816
NFC = 22
ALPHA = float(8 ** 0.25)
EPS = 1e-6
C_Q, C_K, C_V, C_Z, C_B, C_A, C_QB, C_KB, C_VB = 0, 512, 1024, 1536, 2048, 2052, 2056, 2568, 2696
NEG = -30000.0
FLAGS = {}


class View:
    __slots__ = ("b", "ap")

    def __init__(self, b, ap):
        self.b = b
        self.ap = ap

    def __getitem__(self, idx):
        return View(self.b, self.ap[idx])

    def bc(self, shape):
        return View(self.b, self.ap.broadcast_to(list(shape)))

    def re(self, pat, **kw):
        return View(self.b, self.ap.rearrange(pat, **kw))

    def bitcast(self, dt):
        return View(self.b, self.ap.bitcast(dt))


class Buf:
    __slots__ = ("t", "w", "r", "name", "psum")

    def __init__(self, t, name="", psum=False):
        self.t = t
        self.w = None
        self.r = {}
        self.name = name
        self.psum = psum

    def __getitem__(self, idx):
        return View(self, self.t[idx])


class MultiBuf:
    def __init__(self, t, bounds, name=""):
        self.t = t
        self.bounds = list(bounds)
        self.doms = [Buf(t, "%s%d" % (name, i)) for i in range(len(bounds) - 1)]

    def __getitem__(self, idx):
        ap = self.t[idx]
        cs = idx[2] if isinstance(idx, tuple) and len(idx) > 2 else slice(None)
        start = cs.start or 0
        stop = cs.stop if cs.stop is not None else self.bounds[-1]
        doms = [d for d, a, b in zip(self.doms, self.bounds[:-1], self.bounds[1:]) if start < b and stop > a]
        return View(doms[0] if len(doms) == 1 else tuple(doms), ap)


class Eng:
    def __init__(self, name, obj, sem, self_ordered=False):
        self.name, self.obj, self.sem = name, obj, sem
        self.count = 0
        self.seen = {}
        self.self_ordered = self_ordered


class Sched:
    def __init__(self, nc, n_dma_sems=32):
        self.nc = nc
        self._scopes = [[]]
        self.engs = {}
        for name, obj, so in (("pe", nc.tensor, True), ("dve", nc.vector, False), ("act", nc.scalar, False),
                              ("pool", nc.gpsimd, False), ("sp", nc.sync, False)):
            self.engs[name] = Eng(name, obj, self._enter(nc.semaphore("s_" + name)), so)
        self.dma_slots = [[self._enter(nc.semaphore("d_%d" % i)), 0] for i in range(n_dma_sems)]
        self.q_slots = {"sp": self.dma_slots[:n_dma_sems - 12], "pool": self.dma_slots[n_dma_sems - 12:]}
        self.q_rr = {"sp": 0, "pool": 0}
        self.hist = {}
        self.nbuf = 0

    def _enter(self, cm):
        v = cm.__enter__()
        self._scopes[-1].append(cm)
        return v

    def push(self):
        self._scopes.append([])

    def pop(self):
        self.barrier()
        sc = self._scopes.pop()
        while sc:
            sc.pop().__exit__(None, None, None)

    def close(self):
        while self._scopes:
            sc = self._scopes.pop()
            while sc:
                sc.pop().__exit__(None, None, None)

    def sbuf(self, name, shape, dtype=F32):
        self.nbuf += 1
        return Buf(self._enter(self.nc.sbuf_tensor("%s_%d" % (name, self.nbuf), list(shape), dtype)), name)

    def psum(self, name, shape, dtype=F32):
        self.nbuf += 1
        return Buf(self._enter(self.nc.psum_tensor("%s_%d" % (name, self.nbuf), list(shape), dtype)), name, psum=True)

    @staticmethod
    def _add(deps, tok):
        if tok is None:
            return
        k = id(tok[0])
        if k not in deps or deps[k][1] < tok[1]:
            deps[k] = tok

    def _collect(self, reads, writes):
        deps = {}
        for b in reads:
            self._add(deps, b.w)
            if b.psum:
                for tok in b.r.values():
                    self._add(deps, tok)
        for b in writes:
            self._add(deps, b.w)
            for tok in b.r.values():
                self._add(deps, tok)
        return deps

    def _waits(self, eng, deps):
        for k, (sem, val) in sorted(deps.items(), key=lambda kv: -kv[1][1]):
            if eng.self_ordered and sem is eng.sem:
                continue
            if eng.seen.get(k, 0) < val:
                eng.obj.wait_ge(sem, val)
                eng.seen[k] = val
                for k2, v2 in self.hist.get((k, val), {}).items():
                    if eng.seen.get(k2, 0) < v2:
                        eng.seen[k2] = v2

    def _mark(self, tok, reads, writes):
        k = id(tok[0])
        for b in reads:
            old = b.r.get(k)
            if old is None or old[1] < tok[1]:
                b.r[k] = tok
        for b in writes:
            b.w = tok
            b.r = {}

    @staticmethod
    def _bufs(vs):
        out = []
        for v in vs:
            if isinstance(v, View):
                for b in (v.b if isinstance(v.b, tuple) else (v.b,)):
                    if b not in out:
                        out.append(b)
        return out

    def op(self, engname, fn, reads, writes, signal=True):
        eng = self.engs[engname]
        rb, wb = self._bufs(reads), self._bufs(writes)
        self._waits(eng, self._collect(rb, wb))
        ins = fn(eng.obj)
        if signal:
            eng.count += 1
            ins.then_inc(eng.sem, 1)
            self.hist[(id(eng.sem), eng.count)] = dict(eng.seen)
            self._mark((eng.sem, eng.count), rb, wb)
        else:
            self._mark((eng.sem, eng.count + 1), rb, wb)

    def dma(self, q, out, in_, **kw):
        eng = self.engs[q]
        rb, wb = self._bufs([in_]), self._bufs([out])
        deps = self._collect(rb, wb)
        slots = self.q_slots[q]
        slot = slots[self.q_rr[q]]
        self.q_rr[q] = (self.q_rr[q] + 1) % len(slots)
        if slot[1] > 0:
            self._add(deps, (slot[0], slot[1]))
        self._waits(eng, deps)
        o = out.ap if isinstance(out, View) else out
        i = in_.ap if isinstance(in_, View) else in_
        ins = eng.obj.dma_start(out=o, in_=i, **kw)
        slot[1] += 16
        ins.then_inc(slot[0], 16)
        self.hist[(id(slot[0]), slot[1])] = dict(eng.seen)
        self._mark((slot[0], slot[1]), rb, wb)

    def barrier(self):
        toks = {}
        for e in self.engs.values():
            if e.count > 0:
                self._add(toks, (e.sem, e.count))
        for s in self.dma_slots:
            if s[1] > 0:
                self._add(toks, (s[0], s[1]))
        for e in self.engs.values():
            so, e.self_ordered = e.self_ordered, False
            self._waits(e, dict(toks))
            e.self_ordered = so

    def mm(self, o, l, r, start=True, stop=True):
        self.op("pe", lambda e: e.matmul(o.ap, lhsT=l.ap, rhs=r.ap, start=start, stop=stop), [l, r], [o],
                signal=bool(stop))

    def tr(self, o, i, ident):
        self.op("pe", lambda e: e.transpose(o.ap, i.ap, ident.ap), [i, ident], [o])

    def act(self, o, i, func, bias=None, scale=None, accum=None):
        kw = {}
        rd = [i]
        if bias is not None:
            kw["bias"] = bias.ap if isinstance(bias, View) else bias
            rd.append(bias)
        if scale is not None:
            kw["scale"] = scale.ap if isinstance(scale, View) else scale
            rd.append(scale)
        wr = [o]
        if accum is not None:
            kw["accum_out"] = accum.ap
            wr.append(accum)
        self.op("act", lambda e: e.activation(out=o.ap, in_=i.ap, func=func, **kw), rd, wr)

    def tt(self, eng, o, a, b, op):
        self.op(eng, lambda e: e.tensor_tensor(out=o.ap, in0=a.ap, in1=b.ap, op=op), [a, b], [o])

    def ts(self, eng, o, a, s1, op0, s2=None, op1=None):
        rd = [a, s1, s2]
        v1 = s1.ap if isinstance(s1, View) else s1
        v2 = s2.ap if isinstance(s2, View) else s2
        if op1 is None:
            self.op(eng, lambda e: e.tensor_scalar(out=o.ap, in0=a.ap, scalar1=v1, scalar2=None, op0=op0), rd, [o])
        else:
            self.op(eng, lambda e: e.tensor_scalar(out=o.ap, in0=a.ap, scalar1=v1, scalar2=v2, op0=op0, op1=op1),
                    rd, [o])

    def stt(self, eng, o, a, s, b, op0, op1):
        sv = s.ap if isinstance(s, View) else s
        self.op(eng, lambda e: e.scalar_tensor_tensor(out=o.ap, in0=a.ap, scalar=sv, in1=b.ap, op0=op0, op1=op1),
                [a, s, b], [o])

    def copy(self, eng, o, i):
        if eng == "act":
            self.op("act", lambda e: e.copy(out=o.ap, in_=i.ap), [i], [o])
        else:
            self.op(eng, lambda e: e.tensor_copy(out=o.ap, in_=i.ap), [i], [o])

    def memset(self, eng, o, val):
        self.op(eng, lambda e: e.memset(o.ap, val), [], [o])

    def red(self, eng, o, i, op):
        self.op(eng, lambda e: e.tensor_reduce(out=o.ap, in_=i.ap, axis=AX.X, op=op), [i], [o])


CONST_SLOTS = {}


def _build_consts():
    parts = []
    off = [0]

    def add(name, arr):
        a = np.zeros((128, arr.shape[1]), np.float32)
        a[:arr.shape[0]] = arr
        CONST_SLOTS[name] = (off[0], arr.shape[1])
        off[0] += arr.shape[1]
        parts.append(a)

    j = np.arange(128)[:, None]
    i = np.arange(128)[None, :]
    add("ident", (j == i).astype(np.float32))
    add("ones", np.ones((128, 128), np.float32))
    add("o1024", np.full((128, 128), 1.0 / 1024, np.float32))
    add("o128", np.full((128, 128), 1.0 / 128, np.float32))
    add("tri_p", (j <= i).astype(np.float32))
    add("neg_incl_p", np.where(i >= j, 0.0, NEG).astype(np.float32))
    add("neg_strict_p", np.where(i > j, 0.0, NEG).astype(np.float32))
    same = (j // 4 == i // 4) & (j < 64) & (i < 64)
    add("tri_s", (same & (j <= i)).astype(np.float32))
    add("neg_incl_s", np.where(same & (i >= j), 0.0, NEG).astype(np.float32))
    add("neg_strict_s", np.where(same & (i > j), 0.0, NEG).astype(np.float32))
    add("blk_s", same.astype(np.float32))
    s16 = np.arange(16)[None, :]
    add("sm", ((j // 4 == s16) & (j < 64)).astype(np.float32))
    smb = np.zeros((16, 64), np.float32)
    for s in range(16):
        smb[s, 4 * s:4 * s + 4] = 1.0
    global SMB_CONST
    SMB_CONST = np.broadcast_to(smb.reshape(1, 1024), (128, 1024)).copy()
    r = np.arange(128)[:, None]
    c = np.arange(256)[None, :]
    band = (c > r) & (c <= r + 128)
    add("swa_p", np.where(band, 0.0, NEG * 8).astype(np.float32))
    add("swa_p0", np.where(band & (c >= 128), 0.0, NEG * 8).astype(np.float32))
    ms = np.full((128, 192), NEG * 8, np.float32)
    for q in range(64):
        t = q % 4
        ms[q, t + 1:128] = 0.0
        for t2 in range(t + 1):
            ms[q, 128 + (q // 4) * 4 + t2] = 0.0
    add("swa_s", ms)
    return np.concatenate(parts, axis=1)


CONSTS = _build_consts()
NCON = CONSTS.shape[1]


def build(n_layers=L, do_a=True, do_b=True, dbg_n=0, dbg_hook=None, do_prompt=True, do_sample=True):
    nc = bass.Bass("TRN2", target_bir_lowering=False)

    def din(name, shape):
        return nc.dram_tensor(name, list(shape), F32, kind="ExternalInput").ap()

    def dout(name, shape):
        return nc.dram_tensor(name, list(shape), F32, kind="ExternalOutput").ap()

    xT = din("xT", [D, T])
    sd_in = din("sd", [L, NSQ, 4, 128, 128])
    scT_in = din("scT", [L, 128, 12, NSQ, 3])
    ckT_in = din("ckT", [L, 128, NSQ, 2, 128])
    cvD_in = din("cvD", [L, 128, NSQ, 128])
    ck_raw = din("ck_raw", [L, NSQ, 128, 128])
    cv_raw = din("cv_raw", [L, NSQ, 128, 128])
    w_in = din("w_in", [L, D, PW])
    w_out = din("w_out", [L, D, D])
    w_fi = din("w_ffn_in", [L, D, 2 * FF])
    w_fo = din("w_ffn_out", [L, FF, D])
    convw_in = din("conv_wT", [128, L, 12, 4])
    lnp_in = din("lnp", [128, L, 4, 8])
    gp_in = din("gp", [128, L, 3, 8])
    naw_in = din("naw", [128, L])
    consts_in = din("consts", [128, NCON])
    smb_in = din("smb", [128, 1024])

    yT = dout("yT", [D, T])
    nsd_p = dout("nsd_p", [L, 4, 128, 128])
    nsc_p = dout("nsc_p", [L, 3, 1536])
    nk_p = dout("nk_p", [L, 128, 128])
    nv_p = dout("nv_p", [L, 128, 128])
    nsd_s = dout("nsd_s", [L, NSQ, 4, 128, 128])
    nsc_s = dout("nsc_s", [L, NSQ, 3, 1536])
    nk_s = dout("nk_s", [L, NSQ, 128, 128])
    nv_s = dout("nv_s", [L, NSQ, 128, 128])
    dbg = dout("dbg", [128, dbg_n]) if dbg_n else None

    S = Sched(nc)
    X = MultiBuf(S.sbuf("X", [128, 8, T]).t, list(range(0, T + 1, 64)), "X")
    CON = S.sbuf("CON", [128, NCON])
    CONB = S.sbuf("CONB", [128, 256], BF16)
    CW = S.sbuf("CW", [128, L, 12, 4])
    LNP = S.sbuf("LNP", [128, L, 4, 8])
    GP = S.sbuf("GP", [128, L, 3, 8])
    NAW = S.sbuf("NAW", [128, L])
    PS = [S.psum("PS%d" % i, [128, 512]) for i in range(8)]
    ps_rr = [0]

    def nps():
        p = PS[ps_rr[0]]
        ps_rr[0] = (ps_rr[0] + 1) % 8
        return p

    def con(name):
        o, n = CONST_SLOTS[name]
        return CON[:, o:o + n]

    xT_r = xT.rearrange("(c p) t -> p c t", p=128)
    for c in range(8):
        S.dma("sp", X[:, c, :], xT_r[:, c, :])
    S.dma("sp", CON[:, :], consts_in)
    S.dma("sp", CW[:], convw_in)
    S.dma("sp", LNP[:], lnp_in)
    S.dma("sp", GP[:], gp_in)
    S.dma("sp", NAW[:], naw_in)
    S.copy("dve", CONB[:, 0:128], con("ident"))
    S.copy("dve", CONB[:, 128:256], con("ones"))
    IDB = CONB[:, 0:128]

    dbg_off = [0]

    def dump(view, n):
        if dbg is None:
            return
        p = view.ap.shape[0]
        S.dma("sp", dbg[0:p, dbg_off[0]:dbg_off[0] + n], view)
        dbg_off[0] += n

    def interleave(*gens):
        gens = list(gens)
        while gens:
            for g_ in list(gens):
                try:
                    next(g_)
                except StopIteration:
                    gens.remove(g_)

    def rsqrt(o, i, eps):
        S.act(o, i, AF.Ln, bias=eps)
        S.act(o, o, AF.Exp, scale=-0.5)

    def layernorm_g(cols, ncol, gi, l, DD, SQ, RS, sub_eng="dve", affine_eng="act"):
        c0 = cols
        xs = X[:, :, c0:c0 + ncol]
        S.red("dve", RS[:, 0:ncol], xs.re("p c t -> p t c"), ALU.add)
        mp = nps()
        S.mm(mp[:, 0:ncol], con("o1024"), RS[:, 0:ncol])
        S.copy("act", RS[:, 0:ncol], mp[:, 0:ncol])
        yield
        S.tt(sub_eng, DD[:, :, 0:ncol], xs, RS[:, None, 0:ncol].bc([128, 8, ncol]), ALU.subtract)
        S.act(SQ[:, :, 0:ncol], DD[:, :, 0:ncol], AF.Square)
        yield
        S.red("dve", RS[:, 0:ncol], SQ[:, :, 0:ncol].re("p c t -> p t c"), ALU.add)
        vp = nps()
        S.mm(vp[:, 0:ncol], con("o1024"), RS[:, 0:ncol])
        rsqrt(RS[:, 0:ncol], vp[:, 0:ncol], EPS)
        yield
        S.tt("dve", DD[:, :, 0:ncol], DD[:, :, 0:ncol], RS[:, None, 0:ncol].bc([128, 8, ncol]), ALU.mult)
        yield
        if affine_eng == "act":
            for c in range(8):
                S.act(X[:, c, c0:c0 + ncol], DD[:, c, 0:ncol], AF.Identity,
                      bias=LNP[:, l, gi + 1, c:c + 1], scale=LNP[:, l, gi, c:c + 1])
        else:
            S.tt(affine_eng, DD[:, :, 0:ncol], DD[:, :, 0:ncol], LNP[:, l, gi, :, None].bc([128, 8, ncol]), ALU.mult)
            S.tt(affine_eng, xs, DD[:, :, 0:ncol], LNP[:, l, gi + 1, :, None].bc([128, 8, ncol]), ALU.add)

    def layernorm(cols, ncol, gi, l, DD, SQ, RS, sub_eng="dve"):
        for _ in layernorm_g(cols, ncol, gi, l, DD, SQ, RS, sub_eng):
            pass


    H4 = "p (h t) -> p h t"

    def v4(ps, n=128):
        return ps[:, 0:4 * n].re(H4, h=4)

    def gates_and_decay(l, C, XBt, WIN, NEXPA, tri, blk_ones, G):
        gps = nps()
        for kc in range(8):
            S.mm(gps[0:C, 0:8], XBt[:, kc, 0:C], WIN[:, kc, C_B:C_B + 8], start=(kc == 0), stop=(kc == 7))
        S.act(G["LNB"][0:C, :], gps[0:C, 0:4], AF.Exp, scale=-1.0)
        S.tt("dve", G["G1"][0:C, :], gps[0:C, 4:8], GP[0:C, l, 0, 0:4], ALU.add)
        S.act(G["G1"][0:C, :], G["G1"][0:C, :], AF.Exp)
        S.act(G["LNB"][0:C, :], G["LNB"][0:C, :], AF.Ln, bias=1.0)
        S.act(G["G1"][0:C, :], G["G1"][0:C, :], AF.Ln, bias=1.0)
        S.act(G["BETA"][0:C, :], G["LNB"][0:C, :], AF.Exp, scale=-1.0)
        S.tt("dve", G["G"][0:C, :], G["G1"][0:C, :], NEXPA[0:C, :], ALU.mult)
        cps = nps()
        S.mm(cps[0:C, 0:4], tri[0:C, 0:C], G["G"][0:C, :])
        S.mm(cps[0:C, 4:8], blk_ones[0:C, 0:C], G["G"][0:C, :])
        S.copy("dve", G["GCC"][0:C, :], cps[0:C, 0:8])
        S.act(G["EGC"][0:C, :], G["GCC"][0:C, 0:4], AF.Exp)
        S.tt("dve", G["EDL"][0:C, :], G["GCC"][0:C, 4:8], G["GCC"][0:C, 0:4], ALU.subtract)
        S.act(G["EDL"][0:C, :], G["EDL"][0:C, :], AF.Exp)
        S.tt("dve", G["BEG"][0:C, :], G["BETA"][0:C, :], G["EGC"][0:C, :], ALU.mult)
        S.copy("dve", G["G2"][0:C, 0:4], G["GCC"][0:C, 0:4])
        S.tt("dve", G["G2"][0:C, 4:8], G["GCC"][0:C, 0:4], G["LNB"][0:C, :], ALU.subtract)
        tp = nps()
        S.tr(tp[0:4, 0:C], G["G2"][0:C, 0:4], con("ident")[0:C, 0:C])
        S.tr(tp[0:4, C:2 * C], G["G2"][0:C, 4:8], con("ident")[0:C, 0:C])
        S.copy("act", G["GT"][0:4, 0:2 * C], tp[0:4, 0:2 * C])
        S.tt("dve", G["BD"][0:4, 0:8 * C].re("p (a h i) -> p a h i", a=2, h=4),
             G["GT"][0:4, 0:2 * C].re("p (a i) -> p a i", a=2)[:, :, None, :].bc([4, 2, 4, C]),
             con("ident")[0:4, None, 0:4, None].bc([4, 2, 4, C]), ALU.mult)

    def decay_mats(C, G, neg_incl, neg_strict, DT, DTB, EB, TMP1, TMP2):
        W = 4 * C
        for a_, (neg, tmp, dst) in enumerate(((neg_incl, TMP1, DT), (neg_strict, TMP2, DTB))):
            bps = nps()
            S.mm(bps[:, 0:W], con("ones")[0:4, :], G["BD"][0:4, a_ * W:(a_ + 1) * W])
            if a_ == 0:
                S.act(EB[:, 0:W], bps[:, 0:W], AF.Exp)
            t4 = tmp[0:C, 0:W].re(H4, h=4)
            S.tt("dve", t4, bps[0:C, 0:W].re(H4, h=4), G["GCC"][0:C, 0:4, None].bc([C, 4, C]), ALU.subtract)
            S.tt("dve", t4, t4, neg[0:C, None, 0:C].bc([C, 4, C]), ALU.add)
            S.act(dst[0:C, 0:W], tmp[0:C, 0:W], AF.Exp)

    def tri_inverse(C, BMb, AMb, XAb, YAb, R0, R1, Rb0, Rb1, res):
        W = 4 * C
        idb = con("ident")[0:C, None, 0:C].bc([C, 4, C])
        S.tt("dve", R0[0:C, 0:W].re(H4, h=4), idb, BMb[0:C, 0:W].re(H4, h=4), ALU.subtract)
        S.tt("dve", Rb0[0:C, 0:W].re(H4, h=4), idb, BMb[0:C, 0:W].re(H4, h=4), ALU.subtract)
        Xc, Yc, Xn, Yn = BMb, AMb, XAb, YAb
        Rc, Rn, Rbc, Rbn = R0, R1, Rb0, Rb1
        k = 1
        while (1 << k) < C:
            last = (1 << (k + 1)) >= C
            yps = nps()
            for h in range(4):
                hc = slice(h * C, (h + 1) * C)
                S.mm(yps[0:C, hc], Xc[0:C, hc], Yc[0:C, hc])
            if not last:
                xps = nps()
                for h in range(4):
                    hc = slice(h * C, (h + 1) * C)
                    S.mm(xps[0:C, hc], Yc[0:C, hc], Xc[0:C, hc])
            S.copy("act", Yn[0:C, 0:W], yps[0:C, 0:W])
            if not last:
                S.copy("act", Xn[0:C, 0:W], xps[0:C, 0:W])
            rps = nps()
            for h in range(4):
                hc = slice(h * C, (h + 1) * C)
                S.mm(rps[0:C, hc], Yn[0:C, hc], Rbc[0:C, hc])
            S.tt("dve", Rn[0:C, 0:W], rps[0:C, 0:W], Rc[0:C, 0:W], ALU.add)
            S.tt("dve", Rbn[0:C, 0:W], rps[0:C, 0:W], Rc[0:C, 0:W], ALU.add)
            Xc, Yc, Xn, Yn = Xn, Yn, Xc, Yc
            Rc, Rn, Rbc, Rbn = Rn, Rc, Rbn, Rbc
            k += 1
            yield
        res["R"] = (Rc, Rbc, Rn, Rbn)

    def alloc_mix_weights(l):
        S.push()
        WIN = S.sbuf("WIN", [128, 8, PW], BF16)
        WOUT = S.sbuf("WOUT", [128, 8, D], BF16)
        win_r = w_in[l].rearrange("(c p) n -> p c n", p=128)
        wout_r = w_out[l].rearrange("(c p) n -> p c n", p=128)
        for kc in range(8):
            S.dma("pool", WIN[:, kc, :], win_r[:, kc, :])
        for kc in range(8):
            S.dma("pool", WOUT[:, kc, :], wout_r[:, kc, :])
        return WIN, WOUT

    def phase_a(l, WIN, WOUT):
        S.push()
        NEXPA = S.sbuf("NEXPA", [128, 4])
        S.act(NEXPA[:, :], GP[:, l, 1, 0:4], AF.Exp)
        S.ts("dve", NEXPA[:, :], NEXPA[:, :], -1.0, ALU.mult)
        if do_prompt:
            prompt_part(l, WIN, WOUT, NEXPA)
        if do_sample:
            sample_part(l, WIN, WOUT, NEXPA)
        S.pop()
        S.pop()

    def prompt_part(l, WIN, WOUT, NEXPA):
        S.push()
        C = 128
        MASKB = S.sbuf("MASKB", [128, 2, 256], BF16)
        S.copy("dve", MASKB[:, 0, :], con("swa_p0"))
        S.copy("dve", MASKB[:, 1, :], con("swa_p"))
        XBts = [S.sbuf("XBt%d" % i, [128, 8, 128], BF16) for i in range(2)]
        CV = [S.sbuf("CV%d" % i, [128, 4, 131]) for i in range(3)]
        (Tq, Tk, Tv, ZS, EB, DT, DTB, R0, R1, OT, SST) = [S.sbuf("T%d" % i, [128, 512]) for i in range(11)]
        (TvB, QN, KN, KBG, KDEC, VB, QG, PT, BM, AM, XA, YA, Rb0, Rb1, NWT, VNEW, SSb) = \
            [S.sbuf("B%d" % i, [128, 512], BF16) for i in range(17)]
        QKV = [Tq, Tk, Tv]
        G = {}
        for nm, w in (("BETA", 4), ("G1", 4), ("G", 4), ("LNB", 4), ("GCC", 8), ("EGC", 4), ("EDL", 4), ("BEG", 4),
                      ("GL", 4), ("G2", 8)):
            G[nm] = S.sbuf(nm, [128, w])
        G["GT"] = S.sbuf("GT", [4, 256])
        G["BD"] = S.sbuf("BD", [4, 1024])
        QB = S.sbuf("QB", [128, 4, 128], BF16)
        KD = [S.sbuf("KD%d" % i, [128, 2, 128], BF16) for i in range(2)]
        VD = [S.sbuf("VD%d" % i, [128, 128], BF16) for i in range(2)]
        PUN = S.sbuf("PUN", [128, 4, 256], BF16)
        PTT = S.sbuf("PTT", [128, 4, 2, 128], BF16)
        ST = {}
        for nm in ("RM", "NEGM", "RSUM", "SK", "RINV"):
            ST[nm] = S.sbuf(nm, [128, 8])
        S.memset("dve", SST[:, :], 0.0)
        S.memset("dve", SSb[:, :], 0.0)
        for i in range(3):
            S.memset("pool", CV[i][:, :, 0:3], 0.0)
        for i in range(2):
            S.memset("pool", KD[i][:, :, :], 0.0)
            S.memset("pool", VD[i][:, :], 0.0)

        MIXA = S.sbuf("MIXA", [128, 4, 128], BF16)
        MIXB = S.sbuf("MIXB", [128, 4, 128], BF16)
        KVO = S.sbuf("KVO", [128, 256])

        for blk in range(NP // C):
            t0 = blk * C
            cur, prv = blk % 2, (blk + 1) % 2
            last_blk = blk == NP // C - 1
            XBt = XBts[blk % 2]
            if blk == 0:
                S.copy("act", XBt[:, :, :], X[:, :, t0:t0 + C])

            def proj_fm(dst, col0, M=128):
                for kc in range(8):
                    S.mm(dst, WIN[:, kc, col0:col0 + M], XBt[:, kc, :], start=(kc == 0), stop=(kc == 7))

            def chain_a():
                for grp in range(3):
                    ps = nps()
                    for h in range(4):
                        proj_fm(ps[:, h * 128:(h + 1) * 128], grp * 512 + h * 128)
                    S.copy("act", CV[grp][:, :, 3:131], v4(ps))
                CT = (EB, DT, DTB)
                cwb = lambda grp, j: CW[:, l, grp * 4:(grp + 1) * 4, j:j + 1].bc([128, 4, 128])
                for grp in range(3):
                    S.tt("dve", v4(QKV[grp]), CV[grp][:, :, 0:128], cwb(grp, 0), ALU.mult)
                for j in range(1, 4):
                    for grp in range(3):
                        S.tt("dve", v4(CT[grp]), CV[grp][:, :, j:j + 128], cwb(grp, j), ALU.mult)
                    for grp in range(3):
                        S.tt("dve", v4(QKV[grp]), v4(QKV[grp]), v4(CT[grp]), ALU.add)
                for grp in range(3):
                    S.copy("pool", CV[grp][:, :, 0:3], CV[grp][:, :, 128:131])
                zps = nps()
                for h in range(4):
                    proj_fm(zps[:, h * 128:(h + 1) * 128], C_Z + h * 128)
                for grp in range(3):
                    S.act(TvB[:, :] if grp == 2 else QKV[grp][:, :], QKV[grp][:, :], AF.Silu)
                S.act(ZS[:, :], zps[:, :], AF.Silu)
                yield
                SQ1, RSQ = R0, R1
                for (src, dst, scl) in ((QKV[0], QN, 128.0 ** -0.5), (QKV[1], KN, 1.0)):
                    S.act(SQ1[:, :], src[:, :], AF.Square)
                    sps = nps()
                    S.mm(sps[:, :], con("ones"), SQ1[:, :])
                    rsqrt(RSQ[:, :], sps[:, :], EPS)
                    S.stt("dve", dst[:, :], src[:, :], scl, RSQ[:, :], ALU.mult, ALU.mult)
                    yield
                gates_and_decay(l, C, XBt, WIN, NEXPA, con("tri_p"), con("ones"), G)
                S.act(G["GL"][:, :], G["GCC"][:, 4:8], AF.Exp)
                yield
                decay_mats(C, G, con("neg_incl_p"), con("neg_strict_p"), DT, DTB, EB, Tq, Tk)
                yield
                tp_ = nps()
                tpb = tp_[:, :].bitcast(BF16)
                for h in range(4):
                    S.tr(tpb[:, h * 128:(h + 1) * 128], KN[:, h * 128:(h + 1) * 128], IDB)
                for h in range(4):
                    S.tr(tpb[:, 512 + h * 128:512 + (h + 1) * 128], TvB[:, h * 128:(h + 1) * 128], IDB)
                S.tt("dve", v4(KBG), tpb[:, 0:512].re(H4, h=4), G["BEG"][:, :, None].bc([128, 4, 128]), ALU.mult)
                S.tt("dve", v4(KDEC), tpb[:, 0:512].re(H4, h=4), G["EDL"][:, :, None].bc([128, 4, 128]), ALU.mult)
                S.tt("dve", v4(VB), tpb[:, 512:1024].re(H4, h=4), G["BETA"][:, :, None].bc([128, 4, 128]), ALU.mult)
                S.tt("dve", QG[:, :], QN[:, :], EB[:, :], ALU.mult)
                yield
                kk = nps()
                kq = nps()
                for h in range(4):
                    hc = slice(h * 128, (h + 1) * 128)
                    S.mm(kk[:, hc], KN[:, hc], KN[:, hc])
                    S.mm(kq[:, hc], KN[:, hc], QN[:, hc])
                BMf, AMf = Tq, Tk
                S.tt("dve", BMf[:, :], kk[:, :], DTB[:, :], ALU.mult)
                S.tt("dve", BM[:, :], kk[:, :], DTB[:, :], ALU.mult)
                S.tt("dve", PT[:, :], kq[:, :], DT[:, :], ALU.mult)
                yield
                ap_ = nps()
                for h in range(4):
                    hc = slice(h * 128, (h + 1) * 128)
                    S.tr(ap_[:, hc], BMf[:, hc], con("ident"))
                S.copy("act", AMf[:, :], ap_[:, :])
                S.copy("act", AM[:, :], ap_[:, :])
                yield
                res = {}
                yield from tri_inverse(C, BM, AM, XA, YA, R0, R1, Rb0, Rb1, res)
                Rf, R, Rsp, Rbsp = res["R"]
                e_ps = nps()
                for h in range(4):
                    hc = slice(h * 128, (h + 1) * 128)
                    S.mm(e_ps[:, hc], AMf[:, hc], Rf[:, hc])
                S.tt("dve", v4(Rsp), con("ident")[:, None, :].bc([128, 4, 128]), v4(Rf), ALU.subtract)
                S.tt("dve", XA[:, :], Rsp[:, :], e_ps[:, :], ALU.subtract)
                tb_ = nps()
                tbb = tb_[:, :].bitcast(BF16)
                for h in range(4):
                    hc = slice(h * 128, (h + 1) * 128)
                    S.tr(tbb[:, hc], R[:, hc], IDB)
                S.copy("act", YA[:, :], tbb[:, 0:512])
                yield
                c_ps = nps()
                for h in range(4):
                    hc = slice(h * 128, (h + 1) * 128)
                    S.mm(c_ps[:, hc], YA[:, hc], XA[:, hc])
                S.tt("dve", Rsp[:, :], Rf[:, :], c_ps[:, :], ALU.add)
                S.tt("dve", Rbsp[:, :], Rf[:, :], c_ps[:, :], ALU.add)
                R = Rbsp
                yield
                wps = nps()
                for h in range(4):
                    hc = slice(h * 128, (h + 1) * 128)
                    S.mm(wps[:, hc], KBG[:, hc], R[:, hc])
                S.act(NWT[:, :], wps[:, :], AF.Copy, scale=-1.0)
                yield
                vps = nps()
                for h in range(4):
                    hc = slice(h * 128, (h + 1) * 128)
                    S.mm(vps[:, hc], R[:, hc], VB[:, hc], start=True, stop=False)
                    S.mm(vps[:, hc], NWT[:, hc], SSb[:, hc], start=False, stop=True)
                S.copy("dve", VNEW[:, :], vps[:, :])
                yield
                ops_ = nps()
                for h in range(4):
                    hc = slice(h * 128, (h + 1) * 128)
                    S.mm(ops_[:, hc], SSb[:, hc], QG[:, hc], start=True, stop=False)
                    S.mm(ops_[:, hc], VNEW[:, hc], PT[:, hc], start=False, stop=True)
                S.copy("act", OT[:, :], ops_[:, :])
                sps = nps()
                for h in range(4):
                    hc = slice(h * 128, (h + 1) * 128)
                    S.mm(sps[:, hc], KDEC[:, hc], VNEW[:, hc])
                for h in range(4):
                    hc = slice(h * 128, (h + 1) * 128)
                    S.stt("dve", SST[:, hc], SST[:, hc], G["GL"][:, h:h + 1], sps[:, hc], ALU.mult, ALU.add)
                S.copy("act", SSb[:, :], SST[:, :])
                yield
                SQ1, RSQ = DT, DTB
                S.act(SQ1[:, :], OT[:, :], AF.Square)
                sps = nps()
                S.mm(sps[:, :], con("o128"), SQ1[:, :])
                rsqrt(RSQ[:, :], sps[:, :], EPS)
                S.stt("dve", OT[:, :], OT[:, :], NAW[:, l:l + 1], RSQ[:, :], ALU.mult, ALU.mult)
                S.tt("dve", MIXA[:, :, :], v4(OT), v4(ZS), ALU.mult)

            def chain_b():
                ps = nps()
                for h in range(4):
                    proj_fm(ps[:, h * 128:(h + 1) * 128], C_QB + h * 128)
                S.copy("act", QB[:, :, :], v4(ps))
                yield
                ps = nps()
                for g in range(2):
                    for e in range(2):
                        proj_fm(ps[64 * e:64 * e + 64, g * 128:(g + 1) * 128], C_KB + 64 * g, M=64)
                S.copy("act", KD[cur][:, :, :], ps[:, 0:256].re("p (g t) -> p g t", g=2))
                ps = nps()
                for kc in range(8):
                    S.mm(ps[:, 0:128], XBt[:, kc, :], WIN[:, kc, C_VB:C_VB + 128], start=(kc == 0), stop=(kc == 7))
                S.copy("act", VD[cur][:, :], ps[:, 0:128])
                if last_blk:
                    S.copy("dve", KVO[:, 128:256], ps[:, 0:128])
                    S.dma("sp", nv_p[l], KVO[:, 128:256])
                    ps = nps()
                    for kc in range(8):
                        S.mm(ps[:, 0:128], XBt[:, kc, :], WIN[:, kc, C_KB:C_KB + 128], start=(kc == 0), stop=(kc == 7))
                    S.copy("act", KVO[:, 0:128], ps[:, 0:128])
                    S.dma("sp", nk_p[l], KVO[:, 0:128])
                yield
                mk = MASKB[:, 0 if blk == 0 else 1, :]
                for half in range(2):
                    for pbl in range(2):
                        pb = half * 2 + pbl
                        sp_ = nps()
                        for hh in range(2):
                            h = pb * 2 + hh
                            c, e, g = h // 2, h % 2, h // 4
                            pr = slice(64 * e, 64 * e + 64)
                            for kb, kd in ((0, KD[prv]), (1, KD[cur])):
                                cs = slice(hh * 256 + kb * 128, hh * 256 + kb * 128 + 128)
                                S.mm(sp_[:, cs], QB[pr, c, :], kd[pr, g, :], start=True, stop=False)
                                S.mm(sp_[:, cs], IDB, mk[:, kb * 128:(kb + 1) * 128], start=False, stop=True)
                        hs = slice(2 * pb, 2 * pb + 2)
                        S.red("dve", ST["RM"][:, hs], sp_[:, :].re("p (a k) -> p a k", a=2), ALU.max)
                        S.ts("dve", ST["NEGM"][:, hs], ST["RM"][:, hs], 0.125, ALU.mult)
                        S.tt("dve", ST["NEGM"][:, hs], ST["NEGM"][:, hs], GP[:, l, 2, hs], ALU.max)
                        S.ts("dve", ST["NEGM"][:, hs], ST["NEGM"][:, hs], -1.0, ALU.mult)
                        for hh in range(2):
                            h = pb * 2 + hh
                            S.act(PUN[:, pbl * 2 + hh, :], sp_[:, hh * 256:(hh + 1) * 256], AF.Exp,
                                  bias=ST["NEGM"][:, h:h + 1], scale=0.125, accum=ST["RSUM"][:, h:h + 1])
                        yield
                    h4 = slice(half * 4, half * 4 + 4)
                    S.tt("dve", ST["SK"][:, h4], GP[:, l, 2, h4], ST["NEGM"][:, h4], ALU.add)
                    S.act(ST["SK"][:, h4], ST["SK"][:, h4], AF.Exp)
                    S.tt("dve", ST["RINV"][:, h4], ST["RSUM"][:, h4], ST["SK"][:, h4], ALU.add)
                    rv = ST["RINV"][:, h4]
                    S.op("dve", lambda e: e.reciprocal(out=rv.ap, in_=rv.ap), [rv], [rv])
                    S.tt("dve", PUN[:, :, :], PUN[:, :, :], rv[:, :, None].bc([128, 4, 256]), ALU.mult)
                    yield
                    tp = nps()
                    tpb = tp[:, :].bitcast(BF16)
                    for hh in range(4):
                        for kb in range(2):
                            S.tr(tpb[:, (hh * 2 + kb) * 128:(hh * 2 + kb + 1) * 128], PUN[:, hh, kb * 128:(kb + 1) * 128], IDB)
                    S.copy("act", PTT[:, :, :, :], tpb.re("p (h k q) -> p h k q", h=4, k=2))
                    yield
                    op_ = nps()
                    for hh in range(4):
                        h = half * 4 + hh
                        c, e, g = h // 2, h % 2, h // 4
                        dst = op_[64 * e:64 * e + 64, (c % 2) * 128:(c % 2) * 128 + 128]
                        S.mm(dst, VD[prv][:, g * 64:(g + 1) * 64], PTT[:, hh, 0, :], start=True, stop=False)
                        S.mm(dst, VD[cur][:, g * 64:(g + 1) * 64], PTT[:, hh, 1, :], start=False, stop=True)
                    S.copy("act", MIXB[:, 2 * half:2 + 2 * half, :], op_[:, 0:256].re("p (c t) -> p c t", c=2))
                    yield

            ga, gb = chain_a(), chain_b()
            next(ga)
            if not last_blk:
                S.copy("act", XBts[(blk + 1) % 2][:, :, :], X[:, :, t0 + C:t0 + 2 * C])
            for _ in range(int(FLAGS.get("b_lead", 2))):
                next(gb)
            interleave(ga, gb)

            for hb in range(2):
                mps = nps()
                for dcl in range(4):
                    dc = hb * 4 + dcl
                    for kc in range(8):
                        S.mm(mps[:, dcl * 128:(dcl + 1) * 128], WOUT[:, kc, dc * 128:(dc + 1) * 128],
                             (MIXA if kc < 4 else MIXB)[:, kc % 4, :], start=(kc == 0), stop=(kc == 7))
                xv = X[:, hb * 4:(hb + 1) * 4, t0:t0 + C]
                S.stt("dve", xv, xv, ALPHA, v4(mps), ALU.mult, ALU.add)
            interleave(
                layernorm_g(t0, 64, 0, l, Tq[:, :].re("p (c t) -> p c t", c=8), Tk[:, :].re("p (c t) -> p c t", c=8),
                            Tv[:, 0:64]),
                layernorm_g(t0 + 64, 64, 0, l, R0[:, :].re("p (c t) -> p c t", c=8), R1[:, :].re("p (c t) -> p c t", c=8),
                            OT[:, 0:64]))
            if last_blk:
                for cg in range(3):
                    ps = nps()
                    for kc in range(8):
                        S.mm(ps[:, :], XBt[:, kc, :], WIN[:, kc, cg * 512:(cg + 1) * 512], start=(kc == 0), stop=(kc == 7))
                    S.copy("act", QKV[cg][64:128, :], ps[64:128, :])
                    S.dma("sp", nsc_p[l][:, cg * 512:(cg + 1) * 512], QKV[cg][125:128, :])
            if dbg_hook is not None:
                dbg_hook(dict(locals(), dump=dump, S=S))
        S.dma("sp", nsd_p[l].rearrange("h d v -> d h v"), v4(SST))
        S.pop()

    def sample_part(l, WIN, WOUT, NEXPA):
        S.push()
        C = NS
        T0 = NP
        XBs = S.sbuf("XBs", [128, 8, C], BF16)
        MIX = S.sbuf("MIXs", [128, 8, C], BF16)
        QB = S.sbuf("QBs", [128, 4, C], BF16)
        KDn = S.sbuf("KDs", [128, 2, C], BF16)
        VDn = S.sbuf("VDs", [128, 128], BF16)
        S.memset("pool", VDn[:, :], 0.0)
        SMB = S.sbuf("SMB", [128, 16, C], BF16)
        S.dma("pool", SMB[:, :, :].re("p s t -> p (s t)"), smb_in)
        S.copy("act", XBs[:, :, :], X[:, :, T0:T0 + C])
        W4 = 4 * C

        def proj_fm(dst, col0, M=128):
            for kc in range(8):
                S.mm(dst, WIN[:, kc, col0:col0 + M], XBs[:, kc, :], start=(kc == 0), stop=(kc == 7))

        def proj_tm(dst, col0, n):
            for kc in range(8):
                S.mm(dst, XBs[:, kc, :], WIN[:, kc, col0:col0 + n], start=(kc == 0), stop=(kc == 7))

        S.push()
        R0 = S.sbuf("sR0", [128, W4])
        R1 = S.sbuf("sR1", [128, W4])
        Rb0 = S.sbuf("sRb0", [C, W4], BF16)
        Rb1 = S.sbuf("sRb1", [C, W4], BF16)
        PT = S.sbuf("sPT", [C, W4], BF16)
        QG = S.sbuf("sQG", [128, W4], BF16)
        ZS = S.sbuf("sZS", [128, W4])
        OT = S.sbuf("sOT", [128, W4])
        NWT = S.sbuf("sNWT", [128, W4], BF16)
        KBG = S.sbuf("sKBG", [C, 512], BF16)
        KDEC = S.sbuf("sKDEC", [C, 512], BF16)
        VB = S.sbuf("sVB", [C, 512], BF16)
        VNEW = S.sbuf("sVNEW", [C, 512], BF16)
        GLs = S.sbuf("sGL", [128, 64])
        SC1, SC2 = R0, R1
        G = {}
        for nm, w in (("BETA", 4), ("G1", 4), ("G", 4), ("LNB", 4), ("GCC", 8), ("EGC", 4), ("EDL", 4), ("BEG", 4),
                      ("G2", 8)):
            G[nm] = S.sbuf("s" + nm, [128, w])
        G["GT"] = S.sbuf("sGT", [4, 2 * C])
        G["BD"] = S.sbuf("sBD", [4, 8 * C])

        S.push()
        KC = S.sbuf("sKC", [128, 16, 2, 128], BF16)
        VC = S.sbuf("sVC", [128, 16, 128], BF16)
        QM = S.sbuf("sQM", [128, 4, 16, C], BF16)
        MSK = S.sbuf("sMSK", [128, 192], BF16)
        PUN = S.sbuf("sPUN", [C, 4, 192], BF16)
        PTC = S.sbuf("sPTC", [128, 4, C], BF16)
        PTN = S.sbuf("sPTN", [128, 4, C], BF16)
        S.memset("pool", PTN[:, :, :], 0.0)
        PTCM = S.sbuf("sPTCM", [128, 16, C], BF16)
        ST = {}
        for nm in ("RM", "NEGM", "RSUM", "SK", "RINV"):
            ST[nm] = S.sbuf("s" + nm, [C, 8])
        if FLAGS.get("no_kvc"):
            S.memset("pool", KC[:, :, :, :], 0.0)
            S.memset("pool", VC[:, :, :], 0.0)
        else:
            for sg in range(4):
                S.dma("pool", KC[:, sg * 4:(sg + 1) * 4, :, :].re("p s g k -> p (s g k)"),
                      ckT_in[l][:, sg * 4:(sg + 1) * 4, :, :].rearrange("p s g k -> p (s g k)"))
                S.dma("pool", VC[:, sg * 4:(sg + 1) * 4, :].re("p s k -> p (s k)"),
                      cvD_in[l][:, sg * 4:(sg + 1) * 4, :].rearrange("p s k -> p (s k)"))
        S.copy("dve", MSK[:, :], con("swa_s"))
        S.push()
        Tq, Tk, Tv, EB, DT, DTB = [S.sbuf("sT%d" % i, [128, W4]) for i in range(6)]
        TvB, QN, KN, BM, AM, XA, YA = [S.sbuf("sB%d" % i, [128, W4], BF16) for i in range(7)]
        QKV = [Tq, Tk, Tv]
        CVs = [S.sbuf("sCV%d" % i, [128, 4, 16, 7]) for i in range(3)]
        STG = S.sbuf("sSTG", [128, 12, 16, 3])
        OUTS = S.sbuf("sOUTS", [C, 1536])
        KVO = S.sbuf("sKVO", [C, 256])
        GM = S.sbuf("sGM", [C, 16, 4])
        res = {}

        def gen_ia():
            S.dma("sp", STG[:, :, :, :], scT_in[l])
            for grp in range(3):
                ps = nps()
                for h in range(4):
                    proj_fm(ps[:, h * C:(h + 1) * C], grp * 512 + h * 128)
                cv = CVs[grp]
                S.copy("dve", cv[:, :, :, 0:3], STG[:, grp * 4:(grp + 1) * 4, :, :])
                S.copy("act", cv[:, :, :, 3:7], ps[:, 0:W4].re("p (h s t) -> p h s t", h=4, s=16))
                for h in range(4):
                    ch = grp * 4 + h
                    tq = QKV[grp][:, h * C:(h + 1) * C].re("p (s t) -> p s t", s=16)
                    S.ts("dve", tq, cv[:, h, :, 0:4], CW[:, l, ch, 0:1], ALU.mult)
                    for j in range(1, 4):
                        S.stt("dve", tq, cv[:, h, :, j:j + 4], CW[:, l, ch, j:j + 1], tq, ALU.mult, ALU.add)
                S.act(TvB[:, :] if grp == 2 else QKV[grp][:, :], QKV[grp][:, :], AF.Silu)
                yield
            for cg in range(3):
                ps = nps()
                proj_tm(ps[0:C, :], cg * 512, 512)
                S.copy("act", OUTS[:, cg * 512:(cg + 1) * 512], ps[0:C, :])
            for r in range(1, 4):
                S.dma("sp", nsc_s[l][:, r - 1, :], OUTS[r:C:4, :])
            ps = nps()
            for h in range(4):
                proj_fm(ps[:, h * C:(h + 1) * C], C_Z + h * 128)
            S.act(ZS[:, :], ps[:, 0:W4], AF.Silu)
            yield
            SQ1, RSQ = DT, DTB
            for (src, dst, scl) in ((QKV[0], QN, 128.0 ** -0.5), (QKV[1], KN, 1.0)):
                S.act(SQ1[:, :], src[:, :], AF.Square)
                sps = nps()
                S.mm(sps[:, 0:W4], con("ones"), SQ1[:, :])
                rsqrt(RSQ[:, :], sps[:, 0:W4], EPS)
                S.stt("dve", dst[:, :], src[:, :], scl, RSQ[:, :], ALU.mult, ALU.mult)
                yield
            gates_and_decay(l, C, XBs, WIN, NEXPA, con("tri_s"), con("blk_s"), G)
            yield
            S.tt("dve", GM[:, :, :], G["G"][0:C, None, :].bc([C, 16, 4]), con("sm")[0:C, :, None].bc([C, 16, 4]), ALU.mult)
            gps_ = nps()
            S.mm(gps_[:, 0:64], con("ones")[0:C, :], GM[:, :, :].re("p s h -> p (s h)"))
            S.act(GLs[:, :], gps_[:, 0:64], AF.Exp)
            yield
            decay_mats(C, G, con("neg_incl_s"), con("neg_strict_s"), DT, DTB, EB, Tq, Tk)
            yield
            tp_ = nps()
            tpb = tp_[:, :].bitcast(BF16)
            for h in range(4):
                S.tr(tpb[0:C, h * 128:(h + 1) * 128], KN[:, h * C:(h + 1) * C], IDB)
            for h in range(4):
                S.tr(tpb[0:C, 512 + h * 128:512 + (h + 1) * 128], TvB[:, h * C:(h + 1) * C], IDB)
            S.tt("dve", v4(KBG), tpb[0:C, 0:512].re(H4, h=4), G["BEG"][0:C, :, None].bc([C, 4, 128]), ALU.mult)
            S.tt("dve", v4(KDEC), tpb[0:C, 0:512].re(H4, h=4), G["EDL"][0:C, :, None].bc([C, 4, 128]), ALU.mult)
            S.tt("dve", v4(VB), tpb[0:C, 512:1024].re(H4, h=4), G["BETA"][0:C, :, None].bc([C, 4, 128]), ALU.mult)
            S.tt("dve", QG[:, :], QN[:, :], EB[:, :], ALU.mult)
            yield
            kk = nps()
            kq = nps()
            for h in range(4):
                hc = slice(h * C, (h + 1) * C)
                S.mm(kk[0:C, hc], KN[:, hc], KN[:, hc])
                S.mm(kq[0:C, hc], KN[:, hc], QN[:, hc])
            S.tt("dve", BM[0:C, :], kk[0:C, 0:W4], DTB[0:C, :], ALU.mult)
            S.tt("dve", PT[:, :], kq[0:C, 0:W4], DT[0:C, :], ALU.mult)
            yield
            ap_ = nps()
            apb = ap_[:, :].bitcast(BF16)
            for h in range(4):
                hc = slice(h * C, (h + 1) * C)
                S.tr(apb[0:C, hc], BM[0:C, hc], IDB[0:C, 0:C])
            S.copy("act", AM[0:C, :], apb[0:C, 0:W4])
            yield
            yield from tri_inverse(C, BM, AM, XA, YA, R0, R1, Rb0, Rb1, res)
            R = res["R"][1]
            wps = nps()
            for h in range(4):
                S.mm(wps[:, h * C:(h + 1) * C], KBG[:, h * 128:(h + 1) * 128], R[:, h * C:(h + 1) * C])
            S.act(NWT[:, :], wps[:, 0:W4], AF.Copy, scale=-1.0)

        def gen_ii():
            ps = nps()
            for h in range(4):
                proj_fm(ps[:, h * C:(h + 1) * C], C_QB + h * 128)
            S.copy("act", QB[:, :, :], ps[:, 0:W4].re("p (h t) -> p h t", h=4))
            ps = nps()
            for g in range(2):
                for e in range(2):
                    proj_fm(ps[64 * e:64 * e + 64, g * C:(g + 1) * C], C_KB + 64 * g, M=64)
            S.copy("dve", KDn[:, :, :], ps[:, 0:2 * C].re("p (g t) -> p g t", g=2))
            yield
            ps = nps()
            proj_tm(ps[0:C, 0:128], C_KB, 128)
            proj_tm(ps[0:C, 128:256], C_VB, 128)
            S.copy("dve", VDn[0:C, :], ps[0:C, 128:256])
            S.copy("act", KVO[:, :], ps[0:C, 0:256])
            for t in range(4):
                S.dma("sp", nk_s[l][:, 124 + t, :], KVO[t:C:4, 0:128])
                S.dma("sp", nv_s[l][:, 124 + t, :], KVO[t:C:4, 128:256])
            if not FLAGS.get("no_d2d"):
                S.dma("sp", nk_s[l][:, 0:124, :], ck_raw[l][:, 4:128, :])
                S.dma("sp", nv_s[l][:, 0:124, :], cv_raw[l][:, 4:128, :])
            yield
            for c in range(4):
                S.tt("dve", QM[:, c, :, :], QB[:, c, None, :].bc([128, 16, C]), SMB[:, :, :], ALU.mult)
            yield
            LVL = int(FLAGS.get("swa_lvl", 9))
            for half in range(2):
                if LVL < 1:
                    continue
                for pbl in range(2):
                    pb = half * 2 + pbl
                    sp_ = nps()
                    for hh in range(2):
                        h = pb * 2 + hh
                        c, e, g = h // 2, h % 2, h // 4
                        pr = slice(64 * e, 64 * e + 64)
                        cs = slice(hh * 192, hh * 192 + 128)
                        for s_ in range(16):
                            S.mm(sp_[0:C, cs], QM[pr, c, s_, :], KC[pr, s_, g, :], start=(s_ == 0), stop=False)
                        S.mm(sp_[0:C, cs], IDB[:, 0:C], MSK[:, 0:128], start=False, stop=True)
                        cs2 = slice(hh * 192 + 128, hh * 192 + 192)
                        S.mm(sp_[0:C, cs2], QB[pr, c, :], KDn[pr, g, :], start=True, stop=False)
                        S.mm(sp_[0:C, cs2], IDB[:, 0:C], MSK[:, 128:192], start=False, stop=True)
                    hs = slice(2 * pb, 2 * pb + 2)
                    if LVL < 2:
                        continue
                    S.red("dve", ST["RM"][:, hs], sp_[0:C, 0:384].re("p (a k) -> p a k", a=2), ALU.max)
                    S.ts("dve", ST["NEGM"][:, hs], ST["RM"][:, hs], 0.125, ALU.mult)
                    S.tt("dve", ST["NEGM"][:, hs], ST["NEGM"][:, hs], GP[0:C, l, 2, hs], ALU.max)
                    S.ts("dve", ST["NEGM"][:, hs], ST["NEGM"][:, hs], -1.0, ALU.mult)
                    for hh in range(2):
                        h = pb * 2 + hh
                        S.act(PUN[:, pbl * 2 + hh, :], sp_[0:C, hh * 192:(hh + 1) * 192], AF.Exp,
                              bias=ST["NEGM"][:, h:h + 1], scale=0.125, accum=ST["RSUM"][:, h:h + 1])
                    yield
                if LVL < 2:
                    continue
                h4 = slice(half * 4, half * 4 + 4)
                S.tt("dve", ST["SK"][:, h4], GP[0:C, l, 2, h4], ST["NEGM"][:, h4], ALU.add)
                S.act(ST["SK"][:, h4], ST["SK"][:, h4], AF.Exp)
                S.tt("dve", ST["RINV"][:, h4], ST["RSUM"][:, h4], ST["SK"][:, h4], ALU.add)
                rv = ST["RINV"][:, h4]
                S.op("dve", lambda e: e.reciprocal(out=rv.ap, in_=rv.ap), [rv], [rv])
                S.tt("dve", PUN[:, :, :], PUN[:, :, :], rv[:, :, None].bc([C, 4, 192]), ALU.mult)
                yield
                if LVL < 3:
                    continue
                tp = nps()
                tpb = tp[:, :].bitcast(BF16)
                for hh in range(4):
                    S.tr(tpb[:, hh * C:(hh + 1) * C], PUN[:, hh, 0:128], IDB[0:C, 0:C])
                    S.tr(tpb[0:C, 256 + hh * C:256 + (hh + 1) * C], PUN[:, hh, 128:192], IDB[0:C, 0:C])
                S.copy("act", PTC[:, :, :], tpb[:, 0:256].re("p (h q) -> p h q", h=4))
                S.copy("dve", PTN[0:C, :, :], tpb[0:C, 256:512].re("p (h q) -> p h q", h=4))
                yield
                if LVL < 4:
                    continue
                op_ = nps()
                for hh in range(4):
                    h = half * 4 + hh
                    c, e, g = h // 2, h % 2, h // 4
                    pr = slice(64 * e, 64 * e + 64)
                    base = (c % 2) * C
                    S.tt("dve", PTCM[:, :, :], PTC[:, hh, None, :].bc([128, 16, C]), SMB[:, :, :], ALU.mult)
                    for s_ in range(16):
                        S.mm(op_[pr, base:base + C], VC[:, s_, g * 64:(g + 1) * 64], PTCM[:, s_, :],
                             start=(s_ == 0), stop=False)
                    S.mm(op_[pr, base:base + C], VDn[:, g * 64:(g + 1) * 64], PTN[:, hh, :], start=False, stop=True)
                S.copy("act", MIX[:, 4 + 2 * half:6 + 2 * half, :], op_[:, 0:2 * C].re("p (c t) -> p c t", c=2))

        interleave(gen_ia(), gen_ii())
        R = res["R"][1]
        S.pop()
        S.pop()

        S.push()
        if FLAGS.get("stop") == "state":
            S.pop(); S.pop(); S.pop(); return
        SS = S.sbuf("sSS", [128, 16, 4, 128])
        SSB = S.sbuf("sSSB", [128, 16, 4, 128], BF16)
        NWTM = S.sbuf("sNWTM", [128, 16, C], BF16)
        VNM = S.sbuf("sVNM", [C, 8, 128], BF16)
        sd_r = sd_in[l].rearrange("s h d v -> d s h v")
        for sg in range(4):
            S.dma("sp", SS[:, sg * 4:(sg + 1) * 4, :, :], sd_r[:, sg * 4:(sg + 1) * 4, :, :])
            S.copy("dve" if sg % 2 else "act", SSB[:, sg * 4:(sg + 1) * 4, :, :], SS[:, sg * 4:(sg + 1) * 4, :, :])
        vps = nps()
        for h in range(4):
            S.tt("dve", NWTM[:, :, :], NWT[:, None, h * C:(h + 1) * C].bc([128, 16, C]), SMB[:, :, :], ALU.mult)
            hv = vps[0:C, h * 128:(h + 1) * 128]
            S.mm(hv, R[:, h * C:(h + 1) * C], VB[:, h * 128:(h + 1) * 128], start=True, stop=False)
            for s_ in range(16):
                S.mm(hv, NWTM[:, s_, :], SSB[:, s_, h, :], start=False, stop=(s_ == 15))
        S.copy("dve", VNEW[:, :], vps[0:C, :])
        ops_ = nps()
        for h in range(4):
            S.tt("dve", NWTM[:, :, :], QG[:, None, h * C:(h + 1) * C].bc([128, 16, C]), SMB[:, :, :], ALU.mult)
            for s_ in range(16):
                S.mm(ops_[:, h * C:(h + 1) * C], SSB[:, s_, h, :], NWTM[:, s_, :], start=(s_ == 0), stop=False)
            S.mm(ops_[:, h * C:(h + 1) * C], VNEW[:, h * 128:(h + 1) * 128], PT[:, h * C:(h + 1) * C],
                 start=False, stop=True)
        S.copy("act", OT[:, :], ops_[:, 0:W4])
        for h in range(4):
            for sg in range(4):
                if sg % 2 == 0:
                    S.tt("dve", VNM[:, :, :], VNEW[:, None, h * 128:(h + 1) * 128].bc([C, 8, 128]),
                         con("sm")[0:C, sg * 4:sg * 4 + 8, None].bc([C, 8, 128]), ALU.mult)
                sn = nps()
                for sl in range(4):
                    s_ = sg * 4 + sl
                    S.mm(sn[:, sl * 128:(sl + 1) * 128], KDEC[:, h * 128:(h + 1) * 128], VNM[:, s_ % 8, :])
                for sl in range(4):
                    s_ = sg * 4 + sl
                    S.stt("dve", SS[:, s_, h, :], SS[:, s_, h, :], GLs[:, s_ * 4 + h:s_ * 4 + h + 1],
                          sn[:, sl * 128:(sl + 1) * 128], ALU.mult, ALU.add)
        nsd_r = nsd_s[l].rearrange("s h d v -> d s h v")
        for sg in range(4):
            S.dma("sp", nsd_r[:, sg * 4:(sg + 1) * 4, :, :], SS[:, sg * 4:(sg + 1) * 4, :, :])
        S.pop()
        SQ1, RSQ = SC1, SC2
        S.act(SQ1[:, :], OT[:, :], AF.Square)
        sps = nps()
        S.mm(sps[:, 0:W4], con("o128"), SQ1[:, :])
        rsqrt(RSQ[:, :], sps[:, 0:W4], EPS)
        S.stt("dve", OT[:, :], OT[:, :], NAW[:, l:l + 1], RSQ[:, :], ALU.mult, ALU.mult)
        S.tt("dve", MIX[:, 0:4, :], OT[:, :].re(H4, h=4), ZS[:, :].re(H4, h=4), ALU.mult)
        S.pop()

        if FLAGS.get("stop") == "out":
            S.pop(); return
        S.push()
        DDs = S.sbuf("sDD", [128, 8, C])
        SQs = S.sbuf("sSQ", [128, 8, C])
        RSs = S.sbuf("sRS", [128, C])
        for hb in range(2):
            mps = nps()
            for dcl in range(4):
                dc = hb * 4 + dcl
                for kc in range(8):
                    S.mm(mps[:, dcl * C:(dcl + 1) * C], WOUT[:, kc, dc * 128:(dc + 1) * 128], MIX[:, kc, :],
                         start=(kc == 0), stop=(kc == 7))
            xv = X[:, hb * 4:(hb + 1) * 4, T0:T0 + C]
            S.stt("dve", xv, xv, ALPHA, mps[:, 0:W4].re(H4, h=4), ALU.mult, ALU.add)
        layernorm(T0, C, 0, l, DDs, SQs, RSs)
        S.pop()
        S.pop()

    early_store = [False]

    def phase_b(l):
        S.push()
        LNW = 128
        XB = S.sbuf("XB", [128, 8, 1088], BF16)
        ACTB = S.sbuf("ACTB", [128, 11, 1088], BF16)
        WO = [S.sbuf("WO%d" % i, [128, 11, 1024], BF16) for i in range(2)]
        WG = [S.sbuf("WG%d" % i, [128, 8, 256], BF16) for i in range(2)]
        WU = [S.sbuf("WU%d" % i, [128, 8, 256], BF16) for i in range(2)]
        SIL = [S.sbuf("SIL%d" % i, [128, 512]) for i in range(2)]
        DD = [S.sbuf("DD%d" % i, [128, 8, LNW]) for i in range(2)]
        SQ = [S.sbuf("SQ%d" % i, [128, 8, LNW]) for i in range(2)]
        RS = [S.sbuf("RS%d" % i, [128, LNW]) for i in range(2)]
        LT = {"DD": DD, "SQ": SQ, "RS": RS}
        wfi_r = w_fi[l].rearrange("(c p) n -> p c n", p=128)
        wfo_r = w_fo[l].rearrange("(c p) n -> p c n", p=128)
        wcnt = [0]

        def ffn_pass(t0, cgs):
            ntok = sum(cgs)
            for c in range(8):
                S.copy("act" if c % 2 else "dve", XB[:, c, 0:ntok], X[:, c, t0:t0 + ntok])
            for half in range(2):
                wo = WO[half]
                S.dma("pool", wo[:, :, :], wfo_r[:, half * 11:(half + 1) * 11, :])
                for fp in range(6):
                    nf = 2 if fp < 5 else 1
                    fc0 = half * 11 + fp * 2
                    wg, wu = WG[wcnt[0] % 2], WU[wcnt[0] % 2]
                    wcnt[0] += 1
                    S.dma("pool", wg[:, :, 0:nf * 128], wfi_r[:, :, fc0 * 128:(fc0 + nf) * 128])
                    S.dma("pool", wu[:, :, 0:nf * 128], wfi_r[:, :, FF + fc0 * 128:FF + (fc0 + nf) * 128])
                    for f in range(nf):
                        fl = fp * 2 + f
                        cs = 0
                        for ci, cw in enumerate(cgs):
                            gp_, up_ = nps(), nps()
                            for kc in range(8):
                                S.mm(gp_[:, 0:cw], wg[:, kc, f * 128:(f + 1) * 128], XB[:, kc, cs:cs + cw],
                                     start=(kc == 0), stop=(kc == 7))
                            for kc in range(8):
                                S.mm(up_[:, 0:cw], wu[:, kc, f * 128:(f + 1) * 128], XB[:, kc, cs:cs + cw],
                                     start=(kc == 0), stop=(kc == 7))
                            sl = SIL[(fl + ci) % 2]
                            S.act(sl[:, 0:cw], gp_[:, 0:cw], AF.Silu)
                            S.tt("dve", ACTB[:, fl, cs:cs + cw], sl[:, 0:cw], up_[:, 0:cw], ALU.mult)
                            cs += cw
                        yield
                for dc in range(8):
                    cs = 0
                    for cw in cgs:
                        yp = nps()
                        for fl in range(11):
                            S.mm(yp[:, 0:cw], wo[:, fl, dc * 128:(dc + 1) * 128], ACTB[:, fl, cs:cs + cw],
                                 start=(fl == 0), stop=(fl == 10))
                        xv = X[:, dc, t0 + cs:t0 + cs + cw]
                        if half == 0:
                            S.stt("dve", xv, xv, ALPHA, yp[:, 0:cw], ALU.mult, ALU.add)
                        else:
                            S.tt("dve", xv, xv, yp[:, 0:cw], ALU.add)
                        cs += cw
                    yield

        def ln_pass(t0, ntok):
            pieces = []
            c = 0
            while c < ntok:
                n = min(LNW, ntok - c)
                pieces.append((t0 + c, n))
                c += n
            for k in range(0, len(pieces), 2):
                gens = [layernorm_g(pc[0], pc[1], 2, l, LT["DD"][j], LT["SQ"][j], LT["RS"][j],
                                    sub_eng="pool" if j else "dve", affine_eng="dve")
                        for j, pc in enumerate(pieces[k:k + 2])]
                while gens:
                    for g_ in list(gens):
                        try:
                            next(g_)
                        except StopIteration:
                            gens.remove(g_)
                    yield

        for _ in ffn_pass(0, (512, 512)):
            pass
        interleave(ffn_pass(1024, (512, 512, 64)), ln_pass(0, 1024))
        if l == n_layers - 1:
            yT_e = yT.rearrange("(c p) t -> p c t", p=128)
            for c in range(8):
                S.dma("sp", yT_e[:, c, 0:1024], X[:, c, 0:1024])
            early_store[0] = True
        S.pop()
        nxt = alloc_mix_weights(l + 1) if (l + 1 < n_layers and do_a) else None
        S.push()
        DD = [S.sbuf("DDt%d" % i, [128, 8, LNW]) for i in range(2)]
        SQ = [S.sbuf("SQt%d" % i, [128, 8, LNW]) for i in range(2)]
        RS = [S.sbuf("RSt%d" % i, [128, LNW]) for i in range(2)]
        LT.update(DD=DD, SQ=SQ, RS=RS)
        for _ in ln_pass(1024, 1088):
            pass
        S.pop()
        return nxt

    wts = alloc_mix_weights(0) if do_a else None
    for l in range(n_layers):
        if do_a:
            phase_a(l, *wts)
            wts = None
        if do_b:
            wts = phase_b(l)
        elif do_a and l + 1 < n_layers:
            wts = alloc_mix_weights(l + 1)

    yT_r = yT.rearrange("(c p) t -> p c t", p=128)
    c_lo = 1024 if early_store[0] else 0
    for c in range(8):
        S.dma("sp", yT_r[:, c, c_lo:T], X[:, c, c_lo:T])
    S.barrier()
    S.close()
    return nc


def make_in_maps(inp):
    f = np.float32
    x_prompt, x_sample = inp["x_prompt"], inp["x_sample"]
    conv_wT = np.ascontiguousarray(inp["conv_w"].reshape(L, 4, 12, 128).transpose(3, 0, 2, 1)).astype(f)
    lnp = np.stack([inp["ln1_g"], inp["ln1_b"], inp["ln2_g"], inp["ln2_b"]], axis=1)
    lnp = np.ascontiguousarray(lnp.reshape(L, 4, 8, 128).transpose(3, 0, 1, 2)).astype(f)
    gp = np.zeros((128, L, 3, 8), f)
    gp[:, :, 0, 0:4] = inp["dt_bias"][None]
    gp[:, :, 1, 0:4] = inp["a_log"][None]
    gp[:, :, 2, 0:8] = inp["sinks"][None]
    naw = np.ascontiguousarray(inp["norm_a_w"].T).astype(f)
    shared = {
        "w_in": inp["w_in"], "w_out": inp["w_out"], "w_ffn_in": inp["w_ffn_in"], "w_ffn_out": inp["w_ffn_out"],
        "conv_wT": conv_wT, "lnp": lnp, "gp": gp, "naw": naw, "consts": CONSTS, "smb": SMB_CONST,
    }
    maps = []
    for c in range(NCORES):
        sl = slice(NSQ * c, NSQ * (c + 1))
        xt = np.concatenate([x_prompt[c], x_sample[sl].reshape(NS, D)], axis=0).T
        sc = inp["state_conv"][:, sl]
        scT = sc.reshape(L, NSQ, 3, 12, 128).transpose(0, 4, 3, 1, 2)
        ck = inp["cache_swa_k"][:, sl]
        ckT = np.tile(ck.transpose(0, 4, 1, 3, 2), (1, 2, 1, 1, 1))
        cv = inp["cache_swa_v"][:, sl]
        cvD = cv.transpose(0, 2, 1, 3, 4).reshape(L, 128, NSQ, 128)
        m = dict(shared)
        m.update({
            "xT": np.ascontiguousarray(xt, f),
            "sd": np.ascontiguousarray(inp["state_delta"][:, sl], f),
            "scT": np.ascontiguousarray(scT, f),
            "ckT": np.ascontiguousarray(ckT, f),
            "cvD": np.ascontiguousarray(cvD, f),
            "ck_raw": np.ascontiguousarray(ck.reshape(L, NSQ, 128, 128), f),
            "cv_raw": np.ascontiguousarray(cv.reshape(L, NSQ, 128, 128), f),
        })
        maps.append(m)
    return maps


_NC_CACHE = {}


def kernel(**inputs):
    inp = {k: np.asarray(v) for k, v in inputs.items()}
    if "nc" not in _NC_CACHE:
        _NC_CACHE["nc"] = build()
    nc = _NC_CACHE["nc"]
    maps = make_in_maps(inp)
    res = run_bass_kernel_spmd(nc, maps, core_ids=list(range(NCORES))).results
    f = np.float32
    y_p = np.stack([res[c]["yT"][:, :NP].T for c in range(NCORES)]).astype(f)
    y_s = np.concatenate([res[c]["yT"][:, NP:].T.reshape(NSQ, 4, D) for c in range(NCORES)]).astype(f)
    nsd_p = np.stack([res[c]["nsd_p"] for c in range(NCORES)], axis=1).astype(f)
    nsc_p = np.stack([res[c]["nsc_p"] for c in range(NCORES)], axis=1).astype(f)
    nk_p = np.stack([res[c]["nk_p"].reshape(L, 128, 2, 64) for c in range(NCORES)], axis=1).astype(f)
    nv_p = np.stack([res[c]["nv_p"].reshape(L, 128, 2, 64) for c in range(NCORES)], axis=1).astype(f)
    nsd_s = np.concatenate([res[c]["nsd_s"] for c in range(NCORES)], axis=1).astype(f)
    nsc_s = np.concatenate([res[c]["nsc_s"] for c in range(NCORES)], axis=1).astype(f)
    nk_s = np.concatenate([res[c]["nk_s"].reshape(L, NSQ, 128, 2, 64) for c in range(NCORES)], axis=1).astype(f)
    nv_s = np.concatenate([res[c]["nv_s"].reshape(L, NSQ, 128, 2, 64) for c in range(NCORES)], axis=1).astype(f)
    return (y_p, y_s, nsd_p, nsc_p, nk_p, nv_p, nsd_s, nsc_s, nk_s, nv_s)
```
